# Optimizing a Trainium2 kernel written in Bass

```python
import math
import jax, jax.numpy as jnp
from jax import lax
import numpy as np

D_MODEL = 1024
BATCH = 16
SEQ = 4096
DEPTH = 4

CTX_LEN = 256
GRID_W = 64
N_MIXERS = 3
E_BRANCH = 2 * D_MODEL
MLA_HEADS = D_MODEL // 64
MLA_NOPE = 128
MLA_ROPE = 64
MLA_V = 128
MLA_WIDTH = MLA_HEADS * MLA_V
MLA_Q_RANK = D_MODEL // 4
MLA_KV_RANK = D_MODEL // 8
MLA_SPLITS = (MLA_Q_RANK, MLA_Q_RANK + MLA_KV_RANK, MLA_Q_RANK + MLA_KV_RANK + MLA_ROPE)
MLA_IN = MLA_Q_RANK + MLA_KV_RANK + MLA_ROPE + MLA_WIDTH
ROPE_BASE = 10000.0
Q_BLOCK = 128
S5_GROUP = 16
S5_GROUPS = E_BRANCH // S5_GROUP
S5_STATE = 64
S5_CHUNK = 128
DT_MIN = 0.001
DT_MAX = 0.1
CONV_WIDTH = 31
CONV_PAD = CONV_WIDTH // 2
DEEPNORM_ALPHA = (2.0 * DEPTH) ** 0.25
DEEPNORM_BETA = (8.0 * DEPTH) ** -0.25
NORM_EPS = 1e-6

kernel_name = "hybrid_mla_s5_conformer_dit_block"


def _layernorm(x, g, b):
    xf = x.astype(jnp.float32)
    mu = jnp.mean(xf, axis=-1, keepdims=True)
    var = jnp.mean(jnp.square(xf - mu), axis=-1, keepdims=True)
    y = (xf - mu) * lax.rsqrt(var + NORM_EPS) * g.astype(jnp.float32) + b.astype(jnp.float32)
    return y.astype(x.dtype)


def _rmsnorm(x, g):
    xf = x.astype(jnp.float32)
    y = xf * lax.rsqrt(jnp.mean(jnp.square(xf), axis=-1, keepdims=True) + NORM_EPS) * g.astype(jnp.float32)
    return y.astype(x.dtype)


def _axial_rope_tables(rows):
    r, col = jnp.meshgrid(jnp.arange(rows, dtype=jnp.float32), jnp.arange(GRID_W, dtype=jnp.float32), indexing='ij')
    r, col = r.reshape(-1), col.reshape(-1)
    quarter = MLA_ROPE // 4
    inv = ROPE_BASE ** (-jnp.arange(quarter, dtype=jnp.float32) / quarter)
    ang_r = r[:, None] * inv
    ang_c = col[:, None] * inv
    ang = jnp.concatenate([ang_r, ang_r, ang_c, ang_c], axis=-1)
    return jnp.cos(ang), jnp.sin(ang)


def _apply_rope(v, cos, sin):
    x1, x2, x3, x4 = jnp.split(v, 4, axis=-1)
    rot = jnp.concatenate([-x2, x1, -x4, x3], axis=-1)
    return v * cos.astype(v.dtype) + rot * sin.astype(v.dtype)


def _mla_split(h, w_in, kv_norm):
    q_dn, kv_dn, k_rope, gate = jnp.split(h @ w_in, MLA_SPLITS, axis=-1)
    return q_dn, _rmsnorm(kv_dn, kv_norm), k_rope, gate


def _mla_queries(q_dn, q_norm, w_uq, w_uk):
    b, n, _ = q_dn.shape
    q = (_rmsnorm(q_dn, q_norm) @ w_uq).reshape(b, n, MLA_HEADS, MLA_NOPE + MLA_ROPE)
    q_abs = jnp.einsum('bnhd,chd->bnhc', q[..., :MLA_NOPE], w_uk)
    return q_abs, q[..., MLA_NOPE:]


def _mla_output(o_lat, gate, w_uv, w_out):
    b, n = o_lat.shape[:2]
    v = jnp.einsum('bnhc,chv->bnhv', o_lat, w_uv).reshape(b, n, MLA_WIDTH)
    return (v * jax.nn.silu(gate)) @ w_out


def _mla_mixer(h_lat, h_ctx, cos, sin, w_in, q_norm, kv_norm, w_uq, w_uk, w_uv, w_out, with_ctx_out):
    b, s, _ = h_lat.shape
    scale = (MLA_NOPE + MLA_ROPE) ** -0.5
    q_dn_l, ckv_l, kr_l, g_l = _mla_split(h_lat, w_in, kv_norm)
    q_dn_c, ckv_c, kr_c, g_c = _mla_split(h_ctx, w_in, kv_norm)
    qa_l, qr_l = _mla_queries(q_dn_l, q_norm, w_uq, w_uk)
    keys_lat = jnp.concatenate([ckv_l, _apply_rope(kr_l, cos, sin)], axis=-1)
    keys_ctx = jnp.concatenate([ckv_c, kr_c], axis=-1)

    def block(args):
        qa, qr, cs, sn = args
        q_pos = jnp.concatenate([qa, _apply_rope(qr, cs[:, None], sn[:, None])], axis=-1)
        q_free = jnp.concatenate([qa, qr], axis=-1)
        s_all = jnp.concatenate([jnp.einsum('bthd,bkd->bhtk', q_pos, keys_lat),
                                 jnp.einsum('bthd,bkd->bhtk', q_free, keys_ctx)], axis=-1)
        p = jax.nn.softmax(s_all.astype(jnp.float32) * scale, axis=-1).astype(ckv_l.dtype)
        return (jnp.einsum('bhtk,bkc->bthc', p[..., :s], ckv_l)
                + jnp.einsum('bhtk,bkc->bthc', p[..., s:], ckv_c))

    nblk = s // Q_BLOCK
    to_blocks = lambda a: a.reshape(b, nblk, Q_BLOCK, *a.shape[2:]).swapaxes(0, 1)
    o = lax.map(block, (to_blocks(qa_l), to_blocks(qr_l),
                        cos.reshape(nblk, Q_BLOCK, MLA_ROPE), sin.reshape(nblk, Q_BLOCK, MLA_ROPE)))
    o = o.swapaxes(0, 1).reshape(b, s, MLA_HEADS, MLA_KV_RANK)
    y_lat = _mla_output(o, g_l, w_uv, w_out)
    if not with_ctx_out:
        return y_lat, None
    qa_c, qr_c = _mla_queries(q_dn_c, q_norm, w_uq, w_uk)
    q_c = jnp.concatenate([qa_c, qr_c], axis=-1)
    p_c = jax.nn.softmax(jnp.einsum('bthd,bkd->bhtk', q_c, keys_ctx).astype(jnp.float32) * scale,
                         axis=-1).astype(ckv_c.dtype)
    o_c = jnp.einsum('bhtk,bkc->bthc', p_c, ckv_c)
    return y_lat, _mla_output(o_c, g_c, w_uv, w_out)


def _ssm_combine(left, right):
    a_l, b_l = left
    a_r, b_r = right
    return a_r * a_l, a_r * b_l + b_r


def _s5_direction(u_ctx, u_lat, lam, dt, bmat, cmat):
    lam_dt = lam * dt[:, None]
    lam_bar = jnp.exp(lam_dt)
    b_bar = ((lam_bar - 1.0) / lam)[..., None] * bmat
    powers = jnp.exp(lam_dt[None] * jnp.arange(1, S5_CHUNK + 1, dtype=jnp.float32)[:, None, None])

    def step(h, u_chunk):
        bu = jnp.einsum('btgs,gps->btgp', u_chunk.astype(jnp.complex64), b_bar)
        a = jnp.broadcast_to(lam_bar, bu.shape)
        _, xs = lax.associative_scan(_ssm_combine, (a, bu), axis=1)
        xs = xs + powers[None] * h[:, None]
        y = jnp.einsum('btgp,gsp->btgs', xs, cmat).real
        return xs[:, -1], y

    def run(u, h0):
        bsz, n = u.shape[:2]
        nch = n // S5_CHUNK
        chunks = u.reshape(bsz, nch, S5_CHUNK, S5_GROUPS, S5_GROUP).swapaxes(0, 1)
        h_fin, ys = lax.scan(step, h0, chunks)
        return h_fin, ys.swapaxes(0, 1).reshape(bsz, n, S5_GROUPS * S5_GROUP)

    h0 = jnp.zeros((u_lat.shape[0], S5_GROUPS, S5_STATE), jnp.complex64)
    h_ctx, y_ctx = run(u_ctx, h0)
    _, y_lat = run(u_lat, h_ctx)
    return y_ctx, y_lat


def _s5_mixer(h_lat, h_ctx, w_in, lam_re, lam_im, log_dt, b_re, b_im, c_re, c_im, d_skip, w_glu, b_glu, w_out,
              with_ctx_out):
    u_l, g_l = jnp.split(h_lat @ w_in, 2, axis=-1)
    u_c, g_c = jnp.split(h_ctx @ w_in, 2, axis=-1)
    ul, uc = u_l.astype(jnp.float32), u_c.astype(jnp.float32)
    dsk = d_skip.astype(jnp.float32)
    y_l, y_c = dsk * ul, dsk * uc
    f32 = lambda a: a.astype(jnp.float32)
    for direction in range(2):
        lam = lax.complex(jnp.minimum(f32(lam_re[direction]), -1e-4), f32(lam_im[direction]))
        dt = jnp.exp(f32(log_dt[direction]))
        bmat = lax.complex(f32(b_re[direction]), f32(b_im[direction]))
        cmat = lax.complex(f32(c_re[direction]), f32(c_im[direction]))
        if direction == 0:
            yc, yl = _s5_direction(uc, ul, lam, dt, bmat, cmat)
        else:
            yc, yl = _s5_direction(jnp.flip(uc, 1), jnp.flip(ul, 1), lam, dt, bmat, cmat)
            yc, yl = jnp.flip(yc, 1), jnp.flip(yl, 1)
        y_l, y_c = y_l + yl, y_c + yc

    def glu_out(y, g):
        y = jax.nn.gelu(y.astype(g.dtype))
        ya, yb = jnp.split(y @ w_glu + b_glu, 2, axis=-1)
        return ((ya * jax.nn.sigmoid(yb)) * jax.nn.silu(g)) @ w_out

    y_ctx = glu_out(y_c, g_c) if with_ctx_out else None
    return glu_out(y_l, g_l), y_ctx


def _conv_mixer(h_lat, h_ctx, w_in, dw, dw_b, ln_g, ln_b, w_out, with_ctx_out):
    def branch(h):
        a, bgate, g = jnp.split(h @ w_in, 3, axis=-1)
        v = a * jax.nn.sigmoid(bgate)
        v = lax.conv_general_dilated(v, dw[:, None, :], window_strides=(1,), padding=[(CONV_PAD, CONV_PAD)],
                                     dimension_numbers=('NWC', 'WIO', 'NWC'),
                                     feature_group_count=E_BRANCH) + dw_b
        v = jax.nn.silu(_layernorm(v, ln_g, ln_b))
        return (v * jax.nn.silu(g)) @ w_out
    y_ctx = branch(h_ctx) if with_ctx_out else None
    return branch(h_lat), y_ctx


def setup_inputs(seed: int = 0) -> dict:
    key = jax.random.key(seed)
    ks = iter(jax.random.split(key, 48))
    f = jnp.float32
    nrm = lambda shape, scale: jax.random.normal(next(ks), shape, f) * scale
    n_a, n_b, n_c = [len(range(k, DEPTH, N_MIXERS)) for k in range(N_MIXERS)]
    D, E, G, P, GS = D_MODEL, E_BRANCH, S5_GROUPS, S5_STATE, S5_GROUP
    lam_im = jnp.broadcast_to(math.pi * jnp.arange(P, dtype=f), (n_b, 2, G, P))
    return {
        'x': nrm((BATCH, SEQ, D), 1.0),
        'c': nrm((BATCH, D), 1.0),
        'ctx': nrm((BATCH, CTX_LEN, D), 1.0),
        'c_ctx': nrm((D,), 1.0),
        'w_mod': nrm((DEPTH, D, 3 * D), 0.5 * D ** -0.5),
        'b_mod': nrm((DEPTH, 3 * D), 0.02),
        'ln_g': 1.0 + nrm((DEPTH, D), 0.02),
        'ln_b': nrm((DEPTH, D), 0.02),
        'mla_w_in': nrm((n_a, D, MLA_IN), D ** -0.5),
        'mla_q_norm': 1.0 + nrm((n_a, MLA_Q_RANK), 0.02),
        'mla_kv_norm': 1.0 + nrm((n_a, MLA_KV_RANK), 0.02),
        'mla_w_uq': nrm((n_a, MLA_Q_RANK, MLA_HEADS * (MLA_NOPE + MLA_ROPE)), MLA_Q_RANK ** -0.5),
        'mla_w_uk': nrm((n_a, MLA_KV_RANK, MLA_HEADS, MLA_NOPE), MLA_NOPE ** -0.5),
        'mla_w_uv': nrm((n_a, MLA_KV_RANK, MLA_HEADS, MLA_V), MLA_KV_RANK ** -0.5),
        'mla_w_out': nrm((n_a, MLA_WIDTH, D), DEEPNORM_BETA * MLA_WIDTH ** -0.5),
        's5_w_in': nrm((n_b, D, 2 * E), D ** -0.5),
        's5_lam_re': -0.5 + nrm((n_b, 2, G, P), 0.01),
        's5_lam_im': lam_im,
        's5_log_dt': jax.random.uniform(next(ks), (n_b, 2, G), f, math.log(DT_MIN), math.log(DT_MAX)),
        's5_b_re': nrm((n_b, 2, G, P, GS), (2.0 * GS) ** -0.5),
        's5_b_im': nrm((n_b, 2, G, P, GS), (2.0 * GS) ** -0.5),
        's5_c_re': nrm((n_b, 2, G, GS, P), P ** -0.5),
        's5_c_im': nrm((n_b, 2, G, GS, P), P ** -0.5),
        's5_d': nrm((n_b, E), 0.5),
        's5_w_glu': nrm((n_b, E, 2 * E), E ** -0.5),
        's5_b_glu': nrm((n_b, 2 * E), 0.02),
        's5_w_out': nrm((n_b, E, D), DEEPNORM_BETA * E ** -0.5),
        'cv_w_in': nrm((n_c, D, 3 * E), D ** -0.5),
        'cv_dw': nrm((n_c, CONV_WIDTH, E), CONV_WIDTH ** -0.5),
        'cv_dw_b': nrm((n_c, E), 0.02),
        'cv_ln_g': 1.0 + nrm((n_c, E), 0.02),
        'cv_ln_b': nrm((n_c, E), 0.02),
        'cv_w_out': nrm((n_c, E, D), DEEPNORM_BETA * E ** -0.5),
    }


def reference(x, c, ctx, c_ctx, w_mod, b_mod, ln_g, ln_b,
              mla_w_in, mla_q_norm, mla_kv_norm, mla_w_uq, mla_w_uk, mla_w_uv, mla_w_out,
              s5_w_in, s5_lam_re, s5_lam_im, s5_log_dt, s5_b_re, s5_b_im, s5_c_re, s5_c_im, s5_d,
              s5_w_glu, s5_b_glu, s5_w_out,
              cv_w_in, cv_dw, cv_dw_b, cv_ln_g, cv_ln_b, cv_w_out):
    n_tokens = x.shape[1]
    ROWS = n_tokens // GRID_W
    cos, sin = _axial_rope_tables(ROWS)
    cond = jax.nn.silu(c)
    cond_ctx = jax.nn.silu(c_ctx)
    for i in range(DEPTH):
        kind, j = i % N_MIXERS, i // N_MIXERS
        with_ctx_out = i < DEPTH - 1
        sh, sc, gt = jnp.split(cond @ w_mod[i] + b_mod[i], 3, axis=-1)
        sh_c, sc_c, gt_c = jnp.split(cond_ctx @ w_mod[i] + b_mod[i], 3, axis=-1)
        h_lat = x * (1.0 + sc[:, None]) + sh[:, None]
        h_ctx = ctx * (1.0 + sc_c) + sh_c
        if kind == 0:
            y_lat, y_ctx = _mla_mixer(h_lat, h_ctx, cos, sin, mla_w_in[j], mla_q_norm[j], mla_kv_norm[j],
                                      mla_w_uq[j], mla_w_uk[j], mla_w_uv[j], mla_w_out[j], with_ctx_out)
        elif kind == 1:
            y_lat, y_ctx = _s5_mixer(h_lat, h_ctx, s5_w_in[j], s5_lam_re[j], s5_lam_im[j], s5_log_dt[j],
                                     s5_b_re[j], s5_b_im[j], s5_c_re[j], s5_c_im[j], s5_d[j],
                                     s5_w_glu[j], s5_b_glu[j], s5_w_out[j], with_ctx_out)
        else:
            y_lat, y_ctx = _conv_mixer(h_lat, h_ctx, cv_w_in[j], cv_dw[j], cv_dw_b[j], cv_ln_g[j], cv_ln_b[j],
                                       cv_w_out[j], with_ctx_out)
        x = _layernorm(DEEPNORM_ALPHA * x + gt[:, None] * y_lat, ln_g[i], ln_b[i])
        if with_ctx_out:
            ctx = _layernorm(DEEPNORM_ALPHA * ctx + gt_c * y_ctx, ln_g[i], ln_b[i])
    return x
```

```python
import contextlib
import math
import os
KSTOP = os.environ.get('KSTOP', '')
KATT = int(os.environ.get('KATT', '9'))
import numpy as np
import concourse.bass as bass
import concourse.mybir as mybir
from concourse.bass_utils import run_bass_kernel_spmd

F32 = mybir.dt.float32
BF16 = mybir.dt.bfloat16
I32 = mybir.dt.int32
AF = mybir.ActivationFunctionType
ALU = mybir.AluOpType

D = 1024
SEQ = 4096
CTX = 256
EXT = SEQ + CTX
NT = EXT // 128
E = 2048
DEPTH = 4
ALPHA = (2.0 * DEPTH) ** 0.25
EPS = 1e-6
N_CORES = 8


def rope_table():
    t = np.arange(SEQ)
    r, col = (t // 64).astype(np.float32), (t % 64).astype(np.float32)
    inv = (10000.0 ** (-np.arange(16, dtype=np.float32) / 16)).astype(np.float32)
    ang = np.concatenate([r[None] * inv[:, None], r[None] * inv[:, None], col[None] * inv[:, None], col[None] * inv[:, None]], 0)
    return np.ascontiguousarray(np.stack([np.cos(ang), np.sin(ang)], 1).astype(np.float32))


class Buf:
    __slots__ = ('name', 'w', 'r', 'excl')

    def __init__(self, name, excl=None):
        self.name = name
        self.w = None
        self.r = {}
        self.excl = (name[0] == 'p' or name == 'lcp') if excl is None else excl


class Sched:
    ENG = ('pe', 'act', 'dve', 'pool', 'sp')
    NRING = 8

    def __init__(self, nc):
        self.nc = nc
        self.stack = contextlib.ExitStack()
        self.eng = {'pe': nc.tensor, 'act': nc.scalar, 'dve': nc.vector, 'pool': nc.gpsimd, 'sp': nc.sync}
        self.sems = []
        self.esem = {}
        for e in self.ENG:
            self.esem[e] = self._newsem('e_' + e)
        self.ring = {}
        for q in ('sp', 'act', 'pool'):
            self.ring[q] = [self._newsem('d_%s%d' % (q, i)) for i in range(self.NRING)]
        self.ringpos = {q: 0 for q in self.ring}
        self.semval = [0] * len(self.sems)
        self.obs = {e: [0] * len(self.sems) for e in self.ENG}
        self.prog = {e: [] for e in self.ENG}
        self.ninstr = 0

    def _newsem(self, name):
        s = self.stack.enter_context(self.nc.semaphore(name))
        self.sems.append(s)
        return len(self.sems) - 1

    def close(self):
        self.stack.close()

    def _collect(self, engine, reads, writes):
        need = {}

        def add(ev, raw):
            if ev is None:
                return
            si, val, eng = ev
            if eng == engine and not raw:
                return
            if need.get(si, 0) < val:
                need[si] = val
        for b in reads:
            add(b.w, True)
            if b.excl:
                for ev in b.r.values():
                    add(ev, False)
        for b in writes:
            add(b.w, False)
            for ev in b.r.values():
                add(ev, False)
        waits = []
        ob = self.obs[engine]
        for si, val in need.items():
            if ob[si] < val:
                ob[si] = val
                waits.append((si, val))
        return waits

    def _update(self, ev, reads, writes):
        si = ev[0]
        for b in reads:
            b.r[si] = ev
        for b in writes:
            b.w = ev
            b.r = {}

    def op(self, engine, fn, reads=(), writes=()):
        waits = self._collect(engine, reads, writes)
        si = self.esem[engine]
        self.semval[si] += 1
        ev = (si, self.semval[si], engine)
        self.prog[engine].append((waits, fn, si, 1))
        self._update(ev, reads, writes)
        self.ninstr += 1
        return ev

    def dma(self, queue, out, in_, reads=(), writes=(), **kw):
        ring = self.ring[queue]
        si = ring[self.ringpos[queue] % self.NRING]
        self.ringpos[queue] += 1
        waits = self._collect(queue, reads, writes)
        ob = self.obs[queue]
        if ob[si] < self.semval[si]:
            ob[si] = self.semval[si]
            waits.append((si, self.semval[si]))
        self.semval[si] += 16
        ev = (si, self.semval[si], 'dma')
        self.prog[queue].append((waits, (lambda e, out=out, in_=in_, kw=kw: e.dma_start(out=out, in_=in_, **kw)), si, 16))
        self._update(ev, reads, writes)
        self.ninstr += 1
        return ev

    def finish(self, bufs, engine='sp'):
        waits = self._collect(engine, bufs, ())
        self.prog[engine].append((waits, None, None, 0))

    def flush(self):
        sems = self.sems
        prog = self.prog
        eng = self.eng

        def replay(name):
            e = eng[name]
            for waits, fn, si, inc in prog[name]:
                for wsi, val in waits:
                    e.wait_ge(sems[wsi], val)
                if fn is not None:
                    fn(e).then_inc(sems[si], inc)

        with self.nc.Block() as block:
            @block.tensor
            def _(t):
                replay('pe')

            @block.scalar
            def _(t):
                replay('act')

            @block.vector
            def _(t):
                replay('dve')

            @block.gpsimd
            def _(t):
                replay('pool')

            @block.sync
            def _(t):
                replay('sp')
        self.prog = {e: [] for e in self.ENG}


WEIGHT_SPECS = [
    ('w_mod', (4, 1024, 3072)), ('b_mod', (4, 3072)), ('ln_g', (4, 1024)), ('ln_b', (4, 1024)),
    ('mla_w_in', (2, 1024, 2496)), ('mla_q_norm', (2, 256)), ('mla_kv_norm', (2, 128)),
    ('mla_w_uq', (2, 256, 3072)), ('mla_w_uk', (2, 128, 16, 128)), ('mla_w_uv', (2, 128, 16, 128)),
    ('mla_w_out', (2, 2048, 1024)),
    ('s5_w_in', (1, 1024, 4096)), ('s5_lam_re', (1, 2, 128, 64)), ('s5_lam_im', (1, 2, 128, 64)),
    ('s5_log_dt', (1, 2, 128)), ('s5_b_re', (1, 2, 128, 64, 16)), ('s5_b_im', (1, 2, 128, 64, 16)),
    ('s5_c_re', (1, 2, 128, 16, 64)), ('s5_c_im', (1, 2, 128, 16, 64)), ('s5_d', (1, 2048)),
    ('s5_w_glu', (1, 2048, 4096)), ('s5_b_glu', (1, 4096)), ('s5_w_out', (1, 2048, 1024)),
    ('cv_w_in', (1, 1024, 6144)), ('cv_dw', (1, 31, 2048)), ('cv_dw_b', (1, 2048)),
    ('cv_ln_g', (1, 2048)), ('cv_ln_b', (1, 2048)), ('cv_w_out', (1, 2048, 1024)),
]


class Prog:
    def __init__(self, nb, layers, debug_ctx=False):
        self.nb = nb
        self.layers = layers
        self.R = nb + 1
        nc = self.nc = bass.Bass('TRN2', target_bir_lowering=False)
        self.S = Sched(nc)
        self.uid = 0
        self.x_in = nc.dram_tensor("x", [nb, SEQ, D], F32, kind="ExternalInput").ap()
        self.c_in = nc.dram_tensor("c", [nb, D], F32, kind="ExternalInput").ap()
        self.ctx_in = nc.dram_tensor("ctx", [nb, CTX, D], F32, kind="ExternalInput").ap()
        self.cctx_in = nc.dram_tensor("c_ctx", [1, D], F32, kind="ExternalInput").ap()
        self.rope_in = nc.dram_tensor("rope", [64, 2, SEQ], F32, kind="ExternalInput").ap()
        self.w = {}
        for name, shape in WEIGHT_SPECS:
            self.w[name] = nc.dram_tensor(name, list(shape), F32, kind="ExternalInput").ap()
        self.y_out = nc.dram_tensor("y", [nb, SEQ, D], F32, kind="ExternalOutput").ap()
        self.b_y = Buf('y')
        self.debug_ctx = debug_ctx
        if debug_ctx:
            self.ctx_out = nc.dram_tensor("ctx_out", [nb, CTX, D], F32, kind="ExternalOutput").ap()
        self.xs = [nc.dram_tensor("xs%d" % k, [nb, EXT, D], F32, kind="Internal").ap() for k in range(2)]
        self.b_xs = [[Buf('xs%d_%d' % (k, b)) for b in range(nb)] for k in range(2)]
        self.b_in = Buf('inputs')

    def name(self, s):
        self.uid += 1
        return "%s_%d" % (s, self.uid)

    def sb(self, st, nm, shape, dt):
        return st.enter_context(self.nc.sbuf_tensor(self.name(nm), list(shape), dt))

    def ps(self, st, nm, shape=(128, 512), dt=F32):
        return st.enter_context(self.nc.psum_tensor(self.name(nm), list(shape), dt))

    def dram(self, nm, shape, dt):
        return self.nc.dram_tensor(self.name(nm), list(shape), dt, kind="Internal").ap()

    def xrows(self, li, b, t):
        if li == 0:
            if t < 2:
                return self.ctx_in[b, t * 128:(t + 1) * 128, :], self.b_in
            return self.x_in[b, (t - 2) * 128:(t - 1) * 128, :], self.b_in
        k = (li - 1) % 2
        return self.xs[k][b, t * 128:(t + 1) * 128, :], self.b_xs[k][b]

    def xdst(self, li, b, t, last):
        if last:
            if t < 2:
                if self.debug_ctx:
                    return self.ctx_out[b, t * 128:(t + 1) * 128, :], self.b_y
                return None, None
            return self.y_out[b, (t - 2) * 128:(t - 1) * 128, :], self.b_y
        k = li % 2
        return self.xs[k][b, t * 128:(t + 1) * 128, :], self.b_xs[k][b]

    def setup_consts(self, st):
        S = self.S
        self.ident = self.sb(st, 'ident', (128, 128), F32)
        self.identb = self.sb(st, 'identb', (128, 128), BF16)
        self.onesb = self.sb(st, 'onesb', (128, 128), BF16)
        self.b_const = Buf('const')
        ident, identb, onesb = self.ident, self.identb, self.onesb
        S.op('pool', lambda e: e.memset(ident[:], 0.0), writes=[self.b_const])
        S.op('pool', lambda e: e.affine_select(out=ident[:], in_=ident[:], pattern=[[-1, 128]], compare_op=ALU.not_equal,
                                               fill=1.0, base=0, channel_multiplier=1),
             reads=[self.b_const], writes=[self.b_const])
        S.op('pool', lambda e: e.tensor_copy(out=identb[:], in_=ident[:]), reads=[self.b_const], writes=[self.b_const])
        S.op('pool', lambda e: e.memset(onesb[:], 1.0), writes=[self.b_const])

    def cast_weight(self, name, idx, rows, cols):
        src = self.w[name][idx]
        dst = self.dram(name + 'b', (rows, cols), BF16)
        buf = Buf(name + 'b')
        n = 512 if cols % 512 == 0 else cols
        s2 = src.rearrange("k (a n) -> (k a) n", n=n)
        d2 = dst.rearrange("k (a n) -> (k a) n", n=n)
        tot = s2.shape[0]
        step = 2048
        for r0 in range(0, tot, step):
            r1 = min(tot, r0 + step)
            self.S.dma('pool', d2[r0:r1, :], s2[r0:r1, :], reads=[self.b_in], writes=[buf])
        return dst, buf

    def modulation(self, st, i):
        S, R = self.S, self.R
        nb = self.nb
        wmb, b_wmb = self.cast_weight('w_mod', i, 1024, 3072)
        m = {}
        m['modT'] = self.sb(st, 'modT', (128, 16, R), F32)
        m['gt'] = self.sb(st, 'gtbc', (128, R, 1024), F32)
        m['lng'] = self.sb(st, 'lng', (128, 1024), F32)
        m['lnb'] = self.sb(st, 'lnb', (128, 1024), F32)
        m['buf'] = Buf('mod')
        with contextlib.ExitStack() as st2:
            wm = self.sb(st2, 'wm', (128, 8, 3072), BF16)
            crow = self.sb(st2, 'crow', (R, 1024), F32)
            srow = self.sb(st2, 'srow', (R, 1024), F32)
            condT = self.sb(st2, 'condT', (128, 8, R), BF16)
            crep = self.sb(st2, 'crep', (128, 8, 128), BF16)
            bmf = self.sb(st2, 'bmf', (1, 3072), F32)
            bmb = self.sb(st2, 'bmb', (1, 3072), BF16)
            pT = self.ps(st2, 'pT')
            pA = self.ps(st2, 'pA')
            pB = self.ps(st2, 'pB')
            b_wm, b_crow, b_srow, b_condT, b_crep, b_bm, b_pT, b_pA, b_pB = [Buf(n) for n in
                                                                             'wm crow srow condT crep bm pT pA pB'.split()]
            S.dma('sp', wm[:], wmb.rearrange("(c p) n -> p c n", p=128), reads=[b_wmb], writes=[b_wm])
            S.dma('act', crow[0:nb, :], self.c_in[:, :], reads=[self.b_in], writes=[b_crow])
            S.dma('act', crow[nb:nb + 1, :], self.cctx_in[:, :], reads=[self.b_in], writes=[b_crow])
            S.dma('act', bmf[:], self.w['b_mod'][i:i + 1, :], reads=[self.b_in], writes=[b_bm])
            S.dma('act', m['lng'][:], self.w['ln_g'][i].partition_broadcast(128), reads=[self.b_in], writes=[m['buf']])
            S.dma('act', m['lnb'][:], self.w['ln_b'][i].partition_broadcast(128), reads=[self.b_in], writes=[m['buf']])
            S.op('act', lambda e: e.activation(out=srow[:], in_=crow[:], func=AF.Silu), reads=[b_crow], writes=[b_srow])
            S.op('dve', lambda e: e.tensor_copy(out=bmb[:], in_=bmf[:]), reads=[b_bm], writes=[b_bm])
            for c in range(8):
                S.op('pe', lambda e, c=c: e.transpose(pT[:, c * R:(c + 1) * R], srow[:, c * 128:(c + 1) * 128], self.ident[0:R, 0:R]),
                     reads=[b_srow, self.b_const], writes=[b_pT])
            S.op('dve', lambda e: e.tensor_copy(out=condT[:], in_=pT[:, 0:8 * R].rearrange("p (c r) -> p c r", r=R)),
                 reads=[b_pT], writes=[b_condT])
            for fc in range(16):
                for k in range(8):
                    S.op('pe', lambda e, fc=fc, k=k: e.matmul(pA[:, fc * R:(fc + 1) * R], lhsT=wm[:, k, fc * 128:(fc + 1) * 128],
                                                              rhs=condT[:, k, :], start=(k == 0), stop=False),
                         reads=[b_wm, b_condT], writes=[b_pA])
                S.op('pe', lambda e, fc=fc: e.matmul(pA[:, fc * R:(fc + 1) * R], lhsT=bmb[0:1, fc * 128:(fc + 1) * 128],
                                                     rhs=self.onesb[0:1, 0:R], start=False, stop=True),
                     reads=[b_bm, self.b_const], writes=[b_pA])
            S.op('dve', lambda e: e.tensor_copy(out=m['modT'][:, 0:8, :], in_=pA[:, 0:8 * R].rearrange("p (c r) -> p c r", r=R)),
                 reads=[b_pA], writes=[m['buf']])
            S.op('dve', lambda e: e.tensor_scalar(out=m['modT'][:, 8:16, :], in0=pA[:, 8 * R:16 * R].rearrange("p (c r) -> p c r", r=R),
                                                  scalar1=1.0, scalar2=None, op0=ALU.add),
                 reads=[b_pA], writes=[m['buf']])
            for r in range(R):
                S.op('dve', lambda e, r=r: e.tensor_copy(out=crep[:], in_=condT[:, :, r:r + 1].to_broadcast([128, 8, 128])),
                     reads=[b_condT], writes=[b_crep])
                for n in range(2):
                    c0 = 2048 + n * 512
                    for k in range(8):
                        S.op('pe', lambda e, k=k, c0=c0: e.matmul(pB[:], lhsT=crep[:, k, :], rhs=wm[:, k, c0:c0 + 512],
                                                                  start=(k == 0), stop=False),
                             reads=[b_crep, b_wm], writes=[b_pB])
                    S.op('pe', lambda e, c0=c0: e.matmul(pB[:], lhsT=self.onesb[0:1, :], rhs=bmb[0:1, c0:c0 + 512],
                                                         start=False, stop=True),
                         reads=[b_bm, self.b_const], writes=[b_pB])
                    S.op('act', lambda e, r=r, n=n: e.copy(out=m['gt'][:, r, n * 512:(n + 1) * 512], in_=pB[:]),
                         reads=[b_pB], writes=[m['buf']])
            S.flush()
        return m

    def phase_a(self, st, li, b, m, hT, b_hT):
        S = self.S
        with contextlib.ExitStack() as st2:
            xt = [self.sb(st2, 'xt', (128, 1024), F32) for _ in range(2)]
            b_xt = [Buf('xt0'), Buf('xt1')]
            pt = [self.ps(st2, 'pt') for _ in range(2)]
            b_pt = [Buf('pt0'), Buf('pt1')]
            for t in range(NT):
                src, b_src = self.xrows(li, b, t)
                r = self.nb if t < 2 else b
                xa, bxa = xt[t % 2], b_xt[t % 2]
                S.dma('sp' if t % 2 == 0 else 'act', xa[:], src, reads=[b_src], writes=[bxa])
                for half in range(2):
                    pp, bpp = pt[half], b_pt[half]
                    for c in range(4):
                        cc = half * 4 + c
                        S.op('pe', lambda e, cc=cc, c=c, pp=pp, xa=xa: e.transpose(pp[:, c * 128:(c + 1) * 128],
                                                                                    xa[:, cc * 128:(cc + 1) * 128], self.ident[:]),
                             reads=[bxa, self.b_const], writes=[bpp])
                    for c in range(4):
                        cc = half * 4 + c
                        if c % 2 == 0:
                            S.op('act', lambda e, cc=cc, c=c, pp=pp, r=r, t=t: e.activation(
                                out=hT[:, cc, t * 128:(t + 1) * 128], in_=pp[:, c * 128:(c + 1) * 128], func=AF.Identity,
                                scale=m['modT'][:, 8 + cc, r:r + 1], bias=m['modT'][:, cc, r:r + 1]),
                                 reads=[bpp, m['buf']], writes=[b_hT])
                        else:
                            S.op('dve', lambda e, cc=cc, c=c, pp=pp, r=r, t=t: e.tensor_scalar(
                                out=hT[:, cc, t * 128:(t + 1) * 128], in0=pp[:, c * 128:(c + 1) * 128],
                                scalar1=m['modT'][:, 8 + cc, r:r + 1], scalar2=m['modT'][:, cc, r:r + 1],
                                op0=ALU.mult, op1=ALU.add),
                                 reads=[bpp, m['buf']], writes=[b_hT])
            S.flush()

    def alloc_c(self, st):
        c = {}
        c['py'] = [self.ps(st, 'py') for _ in range(2)]
        c['b_py'] = Buf('py')
        c['xr'] = self.sb(st, 'xr', (128, 1024), F32)
        c['yg'] = self.sb(st, 'yg', (128, 1024), F32)
        c['rr'] = self.sb(st, 'rr', (128, 1024), F32)
        c['xn'] = self.sb(st, 'xn', (128, 1024), F32)
        c['stt'] = self.sb(st, 'stt', (128, 2, 6), F32)
        c['mv'] = self.sb(st, 'mv', (128, 4), F32)
        for n in 'xr yg rr xn stt mv'.split():
            c['b_' + n] = Buf(n)
        return c

    def phase_c_tile(self, c, li, b, t, m, lhs_fn, b_act, wout, b_wout, last):
        S = self.S
        dst, b_dst = self.xdst(li, b, t, last)
        if dst is None:
            return
        r = self.nb if t < 2 else b
        src, b_src = self.xrows(li, b, t)
        S.dma('sp', c['xr'][:], src, reads=[b_src], writes=[c['b_xr']])
        for n in range(2):
            for k in range(16):
                S.op('pe', lambda e, n=n, k=k: e.matmul(c['py'][n][:], lhsT=lhs_fn(k), rhs=wout[:, k, n * 512:(n + 1) * 512],
                                                        start=(k == 0), stop=(k == 15)),
                     reads=[b_act, b_wout], writes=[c['b_py']])
        for n in range(2):
            S.op('dve', lambda e, n=n: e.tensor_tensor(out=c['yg'][:, n * 512:(n + 1) * 512], in0=c['py'][n][:],
                                                       in1=m['gt'][:, r, n * 512:(n + 1) * 512], op=ALU.mult),
                 reads=[c['b_py'], m['buf']], writes=[c['b_yg']])
        S.op('dve', lambda e: e.scalar_tensor_tensor(out=c['rr'][:], in0=c['xr'][:], scalar=ALPHA, in1=c['yg'][:],
                                                     op0=ALU.mult, op1=ALU.add),
             reads=[c['b_xr'], c['b_yg']], writes=[c['b_rr']])
        for n in range(2):
            S.op('dve', lambda e, n=n: e.bn_stats(out=c['stt'][:, n, :], in_=c['rr'][:, n * 512:(n + 1) * 512]),
                 reads=[c['b_rr']], writes=[c['b_stt']])
        S.op('dve', lambda e: e.bn_aggr(out=c['mv'][:, 0:2], in_=c['stt'][:].rearrange("p a s -> p (a s)")),
             reads=[c['b_stt']], writes=[c['b_mv']])
        S.op('act', lambda e: e.activation(out=c['mv'][:, 2:3], in_=c['mv'][:, 1:2], func=AF.Sqrt, bias=EPS, scale=1.0),
             reads=[c['b_mv']], writes=[c['b_mv']])
        S.op('dve', lambda e: e.reciprocal(out=c['mv'][:, 2:3], in_=c['mv'][:, 2:3]), reads=[c['b_mv']], writes=[c['b_mv']])
        S.op('dve', lambda e: e.scalar_tensor_tensor(out=c['mv'][:, 3:4], in0=c['mv'][:, 0:1], scalar=-1.0, in1=c['mv'][:, 2:3],
                                                     op0=ALU.mult, op1=ALU.mult),
             reads=[c['b_mv']], writes=[c['b_mv']])
        S.op('act', lambda e: e.activation(out=c['xn'][:], in_=c['rr'][:], func=AF.Identity, scale=c['mv'][:, 2:3],
                                           bias=c['mv'][:, 3:4]),
             reads=[c['b_rr'], c['b_mv']], writes=[c['b_xn']])
        S.op('pool', lambda e: e.tensor_tensor(out=c['xn'][:], in0=c['xn'][:], in1=m['lng'][:], op=ALU.mult),
             reads=[c['b_xn'], m['buf']], writes=[c['b_xn']])
        S.op('pool', lambda e: e.tensor_tensor(out=c['xn'][:], in0=c['xn'][:], in1=m['lnb'][:], op=ALU.add),
             reads=[c['b_xn'], m['buf']], writes=[c['b_xn']])
        S.dma('act', dst, c['xn'][:], reads=[c['b_xn']], writes=[b_dst])

    def layer_conv(self, li, i, last):
        S = self.S
        nb = self.nb
        j = i // 3
        with contextlib.ExitStack() as st:
            m = self.modulation(st, i)
            winb = self.dram('cvwinb', (16, 1024, 384), BF16)
            b_winb = Buf('cvwinb')
            srcv = self.w['cv_w_in'][j].rearrange("r (s k c) -> k r s c", s=3, k=16, c=128)
            for k in range(16):
                S.dma('pool', winb[k].rearrange("r (s c) -> r s c", s=3), srcv[k], reads=[self.b_in], writes=[b_winb])
            woutb, b_woutb = self.cast_weight('cv_w_out', j, 2048, 1024)
            wout = self.sb(st, 'wout', (128, 16, 1024), BF16)
            b_wout = Buf('wout')
            S.dma('sp', wout[:], woutb.rearrange("(c p) n -> p c n", p=128), reads=[b_woutb], writes=[b_wout])
            dwT = self.sb(st, 'dwT', (128, 16, 31), F32)
            pp = self.sb(st, 'cvp', (128, 3, 16), F32)
            b_par = Buf('cvpar')
            with contextlib.ExitStack() as st2:
                dwn = self.sb(st2, 'dwn', (31, 2048), F32)
                prow = self.sb(st2, 'prow', (3, 2048), F32)
                pT = self.ps(st2, 'pT')
                b_dwn, b_pT = Buf('dwn'), Buf('pT')
                S.dma('sp', dwn[:], self.w['cv_dw'][j], reads=[self.b_in], writes=[b_dwn])
                S.dma('sp', prow[0:1, :], self.w['cv_dw_b'][j:j + 1, :], reads=[self.b_in], writes=[b_dwn])
                S.dma('sp', prow[1:2, :], self.w['cv_ln_g'][j:j + 1, :], reads=[self.b_in], writes=[b_dwn])
                S.dma('sp', prow[2:3, :], self.w['cv_ln_b'][j:j + 1, :], reads=[self.b_in], writes=[b_dwn])
                for k in range(16):
                    S.op('pe', lambda e, k=k: e.transpose(pT[:, 0:31], dwn[:, k * 128:(k + 1) * 128], self.ident[0:31, 0:31]),
                         reads=[b_dwn, self.b_const], writes=[b_pT])
                    S.op('pe', lambda e, k=k: e.transpose(pT[:, 32:35], prow[:, k * 128:(k + 1) * 128], self.ident[0:3, 0:3]),
                         reads=[b_dwn, self.b_const], writes=[b_pT])
                    S.op('dve', lambda e, k=k: e.tensor_copy(out=dwT[:, k, :], in_=pT[:, 0:31]), reads=[b_pT], writes=[b_par])
                    S.op('dve', lambda e, k=k: e.tensor_copy(out=pp[:, :, k], in_=pT[:, 32:35]), reads=[b_pT], writes=[b_par])
                S.flush()
            for b in range(nb):
                with contextlib.ExitStack() as st3:
                    hT = self.sb(st3, 'hT', (128, 8, EXT), BF16)
                    b_hT = Buf('hT')
                    self.phase_a(st3, li, b, m, hT, b_hT)
                    self.conv_body(st3, li, j, b, m, hT, b_hT, winb, b_winb, wout, b_wout, dwT, pp, b_par, last)

    def conv_body(self, st, li, j, b, m, hT, b_hT, winb, b_winb, wout, b_wout, dwT, pp, b_par, last):
        S = self.S
        TB = 256
        W = TB + 30
        wk = [self.sb(st, 'wk', (128, 8, 384), BF16) for _ in range(2)]
        b_wk = [Buf('wk0'), Buf('wk1')]
        vT = [self.sb(st, 'vT', (128, W), BF16) for _ in range(2)]
        b_vT = [Buf('vT0'), Buf('vT1')]
        sig = self.sb(st, 'sig', (128, W), F32)
        b_sig = Buf('sig')
        dg = [self.sb(st, 'dg', (128, 31, 128), BF16) for _ in range(2)]
        b_dg = [Buf('dg0'), Buf('dg1')]
        cT = self.sb(st, 'cT', (128, 16, TB), BF16)
        b_cT = Buf('cT')
        sg = self.sb(st, 'sg', (128, 16, TB), BF16)
        b_sg = Buf('sg')
        csq = self.sb(st, 'csq', (128, TB), BF16)
        b_csq = Buf('csq')
        mean = self.sb(st, 'mean', (128, TB), F32)
        rstd = self.sb(st, 'rstd', (128, TB), F32)
        b_st = Buf('stat')
        z1 = self.sb(st, 'z1', (128, TB), F32)
        z2 = self.sb(st, 'z2', (128, TB), F32)
        z3 = self.sb(st, 'z3', (128, TB), BF16)
        b_z1, b_z2, b_z3 = Buf('z1'), Buf('z2'), Buf('z3')
        pA, pB, pC, pG, pS, pQ = [self.ps(st, n) for n in 'pA pB pC pG pS pQ'.split()]
        b_pA, b_pB, b_pC, b_pG, b_pS, b_pQ = [Buf(n) for n in 'pA pB pC pG pS pQ'.split()]
        c = self.alloc_c(st)
        blocks = [(0, CTX, 0)] + [(CTX, EXT, CTX + q * TB) for q in range(SEQ // TB)]
        cnt = 0
        def do_block(s0, s1, t0, cnt0):
            vlo, vhi = max(s0, t0 - 15), min(s1, t0 + TB + 15)
            off = vlo - (t0 - 15)
            Wv = vhi - vlo
            def do_chunk(k, cnt):
                w_, bw_ = wk[cnt % 2], b_wk[cnt % 2]
                v_, bv_ = vT[cnt % 2], b_vT[cnt % 2]
                d_, bd_ = dg[cnt % 2], b_dg[cnt % 2]
                S.dma('sp' if k % 2 == 0 else 'act', w_[:], winb[k].rearrange("(c p) n -> p c n", p=128),
                      reads=[b_winb], writes=[bw_])
                for kk in range(8):
                    S.op('pe', lambda e, kk=kk, w_=w_: e.matmul(pB[:, 0:Wv], lhsT=w_[:, kk, 128:256], rhs=hT[:, kk, vlo:vhi],
                                                              start=(kk == 0), stop=(kk == 7)),
                         reads=[bw_, b_hT], writes=[b_pB])
                for kk in range(8):
                    S.op('pe', lambda e, kk=kk, w_=w_: e.matmul(pA[:, 0:Wv], lhsT=w_[:, kk, 0:128], rhs=hT[:, kk, vlo:vhi],
                                                              start=(kk == 0), stop=(kk == 7)),
                         reads=[bw_, b_hT], writes=[b_pA])
                for kk in range(8):
                    S.op('pe', lambda e, kk=kk, w_=w_: e.matmul(pG[:, 0:TB], lhsT=w_[:, kk, 256:384], rhs=hT[:, kk, t0:t0 + TB],
                                                              start=(kk == 0), stop=(kk == 7)),
                         reads=[bw_, b_hT], writes=[b_pG])
                S.op('act', lambda e: e.activation(out=sig[:, 0:Wv], in_=pB[:, 0:Wv], func=AF.Sigmoid),
                     reads=[b_pB], writes=[b_sig])
                if Wv < W:
                    S.op('pool', lambda e, v_=v_: e.memset(v_[:], 0.0), writes=[bv_])
                S.op('dve', lambda e, v_=v_: e.tensor_tensor(out=v_[:, off:off + Wv], in0=pA[:, 0:Wv], in1=sig[:, 0:Wv], op=ALU.mult),
                     reads=[b_pA, b_sig], writes=[bv_])
                S.op('act', lambda e, k=k: e.activation(out=sg[:, k, :], in_=pG[:, 0:TB], func=AF.Silu),
                     reads=[b_pG], writes=[b_sg])
                for tap in range(31):
                    if tap % 2 == 0:
                        S.op('act', lambda e, tap=tap, k=k, d_=d_: e.activation(out=d_[:, tap, :], in_=self.identb[:], func=AF.Copy,
                                                                               scale=dwT[:, k, tap:tap + 1]),
                             reads=[self.b_const, b_par], writes=[bd_])
                    else:
                        S.op('pool', lambda e, tap=tap, k=k, d_=d_: e.tensor_scalar(out=d_[:, tap, :], in0=self.identb[:],
                                                                                   scalar1=dwT[:, k, tap:tap + 1], scalar2=None,
                                                                                   op0=ALU.mult),
                             reads=[self.b_const, b_par], writes=[bd_])
                for tap in range(31):
                    S.op('pe', lambda e, tap=tap, d_=d_, v_=v_: e.matmul(pC[:, 0:TB], lhsT=d_[:, tap, :], rhs=v_[:, tap:tap + TB],
                                                                       start=(tap == 0), stop=(tap == 30)),
                         reads=[bd_, bv_], writes=[b_pC])
                S.op('act', lambda e, k=k: e.activation(out=cT[:, k, :], in_=pC[:, 0:TB], func=AF.Identity, bias=pp[:, 0, k:k + 1], scale=1.0),
                     reads=[b_pC, b_par], writes=[b_cT])
                S.op('act', lambda e, k=k: e.activation(out=csq[:], in_=pC[:, 0:TB], func=AF.Square, bias=pp[:, 0, k:k + 1], scale=1.0),
                     reads=[b_pC, b_par], writes=[b_csq])
                S.op('pe', lambda e, k=k: e.matmul(pS[:, 0:TB], lhsT=self.onesb[:], rhs=cT[:, k, :], start=(k == 0), stop=(k == 15)),
                     reads=[b_cT, self.b_const], writes=[b_pS])
                S.op('pe', lambda e, k=k: e.matmul(pQ[:, 0:TB], lhsT=self.onesb[:], rhs=csq[:], start=(k == 0), stop=(k == 15)),
                     reads=[b_csq, self.b_const], writes=[b_pQ])
            for k in range(16):
                do_chunk(k, cnt0 + k)
            S.op('act', lambda e: e.activation(out=mean[:], in_=pS[:, 0:TB], func=AF.Copy, scale=1.0 / E), reads=[b_pS], writes=[b_st])
            S.op('dve', lambda e: e.tensor_tensor(out=z1[:], in0=mean[:], in1=mean[:], op=ALU.mult), reads=[b_st], writes=[b_z1])
            S.op('dve', lambda e: e.scalar_tensor_tensor(out=z2[:], in0=pQ[:, 0:TB], scalar=1.0 / E, in1=z1[:], op0=ALU.mult, op1=ALU.subtract),
                 reads=[b_pQ, b_z1], writes=[b_z2])
            S.op('act', lambda e: e.activation(out=z1[:], in_=z2[:], func=AF.Sqrt, bias=EPS, scale=1.0), reads=[b_z2], writes=[b_z1])
            S.op('dve', lambda e: e.reciprocal(out=rstd[:], in_=z1[:]), reads=[b_z1], writes=[b_st])
            for k in range(16):
                S.op('dve', lambda e, k=k: e.tensor_tensor(out=z1[:], in0=cT[:, k, :], in1=mean[:], op=ALU.subtract),
                     reads=[b_cT, b_st], writes=[b_z1])
                S.op('pool', lambda e: e.tensor_tensor(out=z2[:], in0=z1[:], in1=rstd[:], op=ALU.mult),
                     reads=[b_z1, b_st], writes=[b_z2])
                S.op('act', lambda e, k=k: e.activation(out=z3[:], in_=z2[:], func=AF.Silu, scale=pp[:, 1, k:k + 1], bias=pp[:, 2, k:k + 1]),
                     reads=[b_z2, b_par], writes=[b_z3])
                S.op('dve', lambda e, k=k: e.tensor_tensor(out=cT[:, k, :], in0=z3[:], in1=sg[:, k, :], op=ALU.mult),
                     reads=[b_z3, b_sg], writes=[b_cT])
            for tt in range(TB // 128):
                t = t0 // 128 + tt
                self.phase_c_tile(c, li, b, t, m, (lambda k, tt=tt: cT[:, k, tt * 128:(tt + 1) * 128]), b_cT, wout, b_wout, last)
        for bi, (s0, s1, t0) in enumerate(blocks):
            do_block(s0, s1, t0, bi * 16)
        S.flush()

    def load_cols(self, dst, srcs, b_dst):
        S = self.S
        n = len(srcs)
        with contextlib.ExitStack() as st2:
            rows = self.sb(st2, 'lcrow', (n, 128), F32)
            pT = self.ps(st2, 'lcp')
            b_rows, b_pT = Buf('lcrow'), Buf('lcp')
            for q, src in enumerate(srcs):
                S.dma('sp', rows[q:q + 1, :], src, reads=[self.b_in], writes=[b_rows])
            S.op('pe', lambda e: e.transpose(pT[:, 0:n], rows[:, :], self.ident[0:n, 0:n]), reads=[b_rows, self.b_const], writes=[b_pT])
            S.op('dve', lambda e: e.tensor_copy(out=dst, in_=pT[:, 0:n]), reads=[b_pT], writes=[b_dst])
            S.flush()

    def layer_mla(self, li, i, last):
        S = self.S
        nb = self.nb
        j = i // 3
        with_ctx_out = not last
        SCALE = 192.0 ** -0.5
        with contextlib.ExitStack() as st:
            m = self.modulation(st, i)
            winb, b_winb = self.cast_weight('mla_w_in', j, 1024, 2496)
            wuqb, b_wuqb = self.cast_weight('mla_w_uq', j, 256, 3072)
            woutb, b_woutb = self.cast_weight('mla_w_out', j, 2048, 1024)
            b_w = Buf('mlaw')
            wuq = self.sb(st, 'wuq', (128, 2, 3072), BF16)
            wuqr = self.sb(st, 'wuqr', (128, 2, 16, 64), BF16)
            wukT = self.sb(st, 'wukT', (128, 16, 128), BF16)
            wuv = self.sb(st, 'wuv', (128, 16, 128), BF16)
            nrm = self.sb(st, 'nrm', (128, 3), F32)
            rope = self.sb(st, 'rope', (64, 2, SEQ), BF16)
            onesf = self.sb(st, 'onesf', (128, 128), F32)
            S.op('pool', lambda e: e.memset(onesf[:], 1.0), writes=[b_w])
            S.dma('act', wuq[:], wuqb.rearrange("(c p) n -> p c n", p=128), reads=[b_wuqb], writes=[b_w])
            S.dma('pool', rope[:], self.rope_in[:, :, :], reads=[self.b_in], writes=[b_w])
            self.load_cols(nrm[:, :], [self.w['mla_q_norm'][j:j + 1, 0:128], self.w['mla_q_norm'][j:j + 1, 128:256],
                                       self.w['mla_kv_norm'][j:j + 1, :]], b_w)
            for g4, (srcg, sign) in enumerate([(1, -1.0), (0, 1.0), (3, -1.0), (2, 1.0)]):
                wq4 = wuq[:].rearrange("p c (h f) -> p c h f", f=192)
                S.op('pool', lambda e, g4=g4, srcg=srcg, sign=sign, wq4=wq4: e.tensor_scalar(
                    out=wuqr[:, :, :, g4 * 16:(g4 + 1) * 16], in0=wq4[:, :, :, 128 + srcg * 16:128 + (srcg + 1) * 16],
                    scalar1=sign, scalar2=None, op0=ALU.mult), reads=[b_w], writes=[b_w])
            with contextlib.ExitStack() as st2:
                wf = self.sb(st2, 'wukf', (128, 16, 128), F32)
                wf2 = self.sb(st2, 'wuvf', (128, 16, 128), F32)
                pT = self.ps(st2, 'pT')
                b_wf, b_pT = Buf('wf'), Buf('pT')
                S.dma('sp', wf[:], self.w['mla_w_uk'][j], reads=[self.b_in], writes=[b_wf])
                S.dma('act', wf2[:], self.w['mla_w_uv'][j], reads=[self.b_in], writes=[b_wf])
                S.op('dve', lambda e: e.tensor_copy(out=wuv[:], in_=wf2[:]), reads=[b_wf], writes=[b_w])
                for h in range(16):
                    S.op('pe', lambda e, h=h: e.transpose(pT[:, (h % 4) * 128:(h % 4 + 1) * 128], wf[:, h, :], self.ident[:]),
                         reads=[b_wf, self.b_const], writes=[b_pT])
                    if h % 4 == 3:
                        S.op('dve', lambda e, h=h: e.tensor_copy(out=wukT[:, h - 3:h + 1, :], in_=pT[:].rearrange("p (a c) -> p a c", a=4)),
                             reads=[b_pT], writes=[b_w])
                S.flush()
            W = dict(woutb=woutb, b_woutb=b_woutb, wuq=wuq, wuqr=wuqr, wukT=wukT, wuv=wuv, nrm=nrm, rope=rope, onesf=onesf, b_w=b_w,
                     winb=winb, b_winb=b_winb)
            for b in range(nb):
                if KSTOP == 'prep':
                    break
                with contextlib.ExitStack() as st3:
                    qnT = self.sb(st3, 'qnT', (128, 2, EXT), BF16)
                    KT = self.sb(st3, 'KT', (128, EXT), BF16)
                    KTr = self.sb(st3, 'KTr', (64, EXT), BF16)
                    V = self.sb(st3, 'V', (128, NT, 128), BF16)
                    gscr = self.dram('gscr', (128, 16, EXT), BF16)
                    A = dict(qnT=qnT, KT=KT, KTr=KTr, V=V, gscr=gscr, b_qk=Buf('qk'), b_gscr=Buf('gscr'))
                    with contextlib.ExitStack() as st4:
                        hT = self.sb(st4, 'hT', (128, 8, EXT), BF16)
                        b_hT = Buf('hT')
                        self.phase_a(st4, li, b, m, hT, b_hT)
                        self.mla_b0(st4, hT, b_hT, W, A, with_ctx_out)
                    if KSTOP != 'b0':
                        self.mla_att(st3, li, b, m, W, A, with_ctx_out, last, SCALE)

    def mla_b0(self, st, hT, b_hT, W, A, with_ctx_out):
        S = self.S
        nrm, rope, b_w0 = W['nrm'], W['rope'], W['b_w']
        qnT, KT, KTr, V, b_qk = A['qnT'], A['KT'], A['KTr'], A['V'], A['b_qk']
        wA = self.sb(st, 'wA', (128, 8, 512), BF16)
        b_wA = Buf('wA')
        S.dma('sp', wA[:, :, 0:448], W['winb'][:, 0:448].rearrange("(c p) n -> p c n", p=128), reads=[W['b_winb']], writes=[b_wA])
        for g4, (srcg, sign) in enumerate([(1, -1.0), (0, 1.0), (3, -1.0), (2, 1.0)]):
            S.op('dve', lambda e, g4=g4, srcg=srcg, sign=sign: e.tensor_scalar(
                out=wA[:, :, 448 + g4 * 16:448 + (g4 + 1) * 16], in0=wA[:, :, 384 + srcg * 16:384 + (srcg + 1) * 16],
                scalar1=sign, scalar2=None, op0=ALU.mult), reads=[b_wA], writes=[b_wA])
        b_w = b_wA
        pq = [self.ps(st, 'pq') for _ in range(2)]
        pkv, pkr, pkrr, pss, pssk = [self.ps(st, n) for n in 'pkv pkr pkrr pss pssk'.split()]
        pvt = self.ps(st, 'pvt', (128, 512), BF16)
        b_pq, b_pkv, b_pkr, b_pkrr, b_pss, b_pssk, b_pvt = [Buf(n) for n in 'pq pkv pkr pkrr pss pssk pvt'.split()]
        sq = self.sb(st, 'sq', (128, 3, 512), BF16)
        rs = self.sb(st, 'rs', (128, 2, 512), F32)
        t1 = self.sb(st, 't1', (64, 512), F32)
        t2 = self.sb(st, 't2', (64, 512), F32)
        b_sq, b_rs, b_t1, b_t2 = Buf('sq'), Buf('rs'), Buf('t1'), Buf('t2')

        def block(t0, n):
            for c in range(2):
                for kk in range(8):
                    S.op('pe', lambda e, c=c, kk=kk: e.matmul(pq[c][:, 0:n], lhsT=wA[:, kk, c * 128:(c + 1) * 128], rhs=hT[:, kk, t0:t0 + n],
                                                              start=(kk == 0), stop=(kk == 7)), reads=[b_w, b_hT], writes=[b_pq])
            for kk in range(8):
                S.op('pe', lambda e, kk=kk: e.matmul(pkv[:, 0:n], lhsT=wA[:, kk, 256:384], rhs=hT[:, kk, t0:t0 + n],
                                                     start=(kk == 0), stop=(kk == 7)), reads=[b_w, b_hT], writes=[b_pkv])
            for kk in range(8):
                S.op('pe', lambda e, kk=kk: e.matmul(pkr[0:64, 0:n], lhsT=wA[:, kk, 384:448], rhs=hT[:, kk, t0:t0 + n],
                                                     start=(kk == 0), stop=(kk == 7)), reads=[b_w, b_hT], writes=[b_pkr])
            lat = t0 >= CTX
            if lat:
                for kk in range(8):
                    S.op('pe', lambda e, kk=kk: e.matmul(pkrr[0:64, 0:n], lhsT=wA[:, kk, 448:512], rhs=hT[:, kk, t0:t0 + n],
                                                         start=(kk == 0), stop=(kk == 7)), reads=[b_w, b_hT], writes=[b_pkrr])
            for c in range(2):
                S.op('act', lambda e, c=c: e.activation(out=sq[:, c, 0:n], in_=pq[c][:, 0:n], func=AF.Square), reads=[b_pq], writes=[b_sq])
            S.op('act', lambda e: e.activation(out=sq[:, 2, 0:n], in_=pkv[:, 0:n], func=AF.Square), reads=[b_pkv], writes=[b_sq])
            for c in range(2):
                S.op('pe', lambda e, c=c: e.matmul(pss[:, 0:n], lhsT=self.onesb[:], rhs=sq[:, c, 0:n], start=(c == 0), stop=(c == 1)),
                     reads=[b_sq, self.b_const], writes=[b_pss])
            S.op('pe', lambda e: e.matmul(pssk[:, 0:n], lhsT=self.onesb[:], rhs=sq[:, 2, 0:n], start=True, stop=True),
                 reads=[b_sq, self.b_const], writes=[b_pssk])
            S.op('act', lambda e: e.activation(out=rs[:, 0, 0:n], in_=pss[:, 0:n], func=AF.Sqrt, scale=1.0 / 256, bias=EPS), reads=[b_pss], writes=[b_rs])
            S.op('act', lambda e: e.activation(out=rs[:, 1, 0:n], in_=pssk[:, 0:n], func=AF.Sqrt, scale=1.0 / 128, bias=EPS), reads=[b_pssk], writes=[b_rs])
            S.op('dve', lambda e: e.reciprocal(out=rs[:, :, 0:n], in_=rs[:, :, 0:n]), reads=[b_rs], writes=[b_rs])
            for c in range(2):
                S.op('dve', lambda e, c=c: e.scalar_tensor_tensor(out=qnT[:, c, t0:t0 + n], in0=pq[c][:, 0:n], scalar=nrm[:, c:c + 1], in1=rs[:, 0, 0:n],
                                                                  op0=ALU.mult, op1=ALU.mult), reads=[b_pq, b_rs, b_w0], writes=[b_qk])
            S.op('dve', lambda e: e.scalar_tensor_tensor(out=KT[:, t0:t0 + n], in0=pkv[:, 0:n], scalar=nrm[:, 2:3], in1=rs[:, 1, 0:n],
                                                         op0=ALU.mult, op1=ALU.mult), reads=[b_pkv, b_rs, b_w0], writes=[b_qk])
            if lat:
                l0 = t0 - CTX
                S.op('dve', lambda e: e.tensor_tensor(out=t1[:, 0:n], in0=pkr[0:64, 0:n], in1=rope[:, 0, l0:l0 + n], op=ALU.mult),
                     reads=[b_pkr, b_w0], writes=[b_t1])
                S.op('dve', lambda e: e.tensor_tensor(out=t2[:, 0:n], in0=pkrr[0:64, 0:n], in1=rope[:, 1, l0:l0 + n], op=ALU.mult),
                     reads=[b_pkrr, b_w0], writes=[b_t2])
                S.op('pool', lambda e: e.tensor_tensor(out=KTr[:, t0:t0 + n], in0=t1[:, 0:n], in1=t2[:, 0:n], op=ALU.add),
                     reads=[b_t1, b_t2], writes=[b_qk])
            else:
                S.op('act', lambda e: e.activation(out=KTr[:, t0:t0 + n], in_=pkr[0:64, 0:n], func=AF.Copy), reads=[b_pkr], writes=[b_qk])
            for q in range(n // 128):
                t = t0 // 128 + q
                S.op('pe', lambda e, t=t, q=q: e.transpose(pvt[:, q * 128:(q + 1) * 128], KT[:, t * 128:(t + 1) * 128], self.identb[:]),
                     reads=[b_qk, self.b_const], writes=[b_pvt])
            S.op('act', lambda e: e.activation(out=V[:, t0 // 128:t0 // 128 + n // 128, :],
                                               in_=pvt[:, 0:n].rearrange("p (a c) -> p a c", c=128), func=AF.Copy),
                 reads=[b_pvt], writes=[b_qk])

        block(0, CTX)
        for q in range(SEQ // 512):
            block(CTX + q * 512, 512)
        wg = [self.sb(st, 'wg', (128, 8, 128), BF16) for _ in range(2)]
        b_wg = [Buf('wg0'), Buf('wg1')]
        gst = [self.sb(st, 'gst', (128, 512), BF16) for _ in range(2)]
        b_gst = [Buf('gst0'), Buf('gst1')]
        pg = [pq[0], pq[1], pkv]
        b_pg = [Buf('pg0'), Buf('pg1'), Buf('pg2')]
        winb, b_winb = W['winb'], W['b_winb']
        tstart = 0 if with_ctx_out else CTX
        cnt = [0]

        def gate_chunk(k):
            w_, bw_ = wg[k % 2], b_wg[k % 2]
            S.dma('sp', w_[:], winb[:, 448 + k * 128:448 + (k + 1) * 128].rearrange("(c p) n -> p c n", p=128), reads=[b_winb], writes=[bw_])
            t0 = tstart
            while t0 < EXT:
                n = min(512, EXT - t0) if t0 >= CTX else CTX
                p_, bp_ = pg[cnt[0] % 3], b_pg[cnt[0] % 3]
                g_, bg_ = gst[cnt[0] % 2], b_gst[cnt[0] % 2]
                cnt[0] += 1
                for kk in range(8):
                    S.op('pe', lambda e, kk=kk, p_=p_, t0=t0, n=n: e.matmul(p_[:, 0:n], lhsT=w_[:, kk, :], rhs=hT[:, kk, t0:t0 + n],
                                                                         start=(kk == 0), stop=(kk == 7)), reads=[bw_, b_hT], writes=[bp_])
                S.op('act', lambda e, p_=p_, g_=g_, n=n: e.activation(out=g_[:, 0:n], in_=p_[:, 0:n], func=AF.Silu), reads=[bp_], writes=[bg_])
                S.dma('act', A['gscr'][:, k, t0:t0 + n], g_[:, 0:n], reads=[bg_], writes=[A['b_gscr']])
                t0 += n
        for k in range(16):
            gate_chunk(k)
        S.flush()

    def mla_att(self, st, li, b, m, W, A, with_ctx_out, last, SCALE):
        S = self.S
        wuq, wuqr, wukT, wuv, rope, onesf, b_w = [W[n] for n in 'wuq wuqr wukT wuv rope onesf b_w'.split()]
        wout = self.sb(st, 'wout', (128, 16, 1024), BF16)
        b_wout = Buf('wout')
        S.dma('sp', wout[:], W['woutb'].rearrange("(c p) n -> p c n", p=128), reads=[W['b_woutb']], writes=[b_wout])
        qnT, KT, KTr, V, gscr, b_qk, b_gscr = [A[n] for n in 'qnT KT KTr V gscr b_qk b_gscr'.split()]
        c = self.alloc_c(st)
        pS = [self.ps(st, 'pS') for _ in range(2)]
        b_pS = [Buf('pS%d' % q) for q in range(2)]
        pR = self.ps(st, 'pR')
        b_pR = Buf('pR')
        pO = [self.ps(st, 'pO') for _ in range(2)]
        b_pO = [Buf('pO0'), Buf('pO1')]
        pM = self.ps(st, 'pM')
        b_pM = Buf('pM')
        gh = [self.sb(st, 'gh', (128, 512), BF16) for _ in range(2)]
        b_gh = [Buf('gh0'), Buf('gh1')]
        actT = self.sb(st, 'actT', (128, 16, 512), BF16)
        b_actT = Buf('actT')
        qh = self.sb(st, 'qh', (128, 512), BF16)
        Qa = [self.sb(st, 'Qa', (128, 512), BF16) for _ in range(2)]
        Qf = [self.sb(st, 'Qf', (64, 512), BF16) for _ in range(2)]
        Qp = [self.sb(st, 'Qp', (64, 512), BF16) for _ in range(2)]
        b_qh = Buf('qh')
        b_Q = [Buf('Q0'), Buf('Q1')]
        t1 = self.sb(st, 'qt1', (64, 512), F32)
        t2 = self.sb(st, 'qt2', (64, 512), F32)
        b_t1, b_t2 = Buf('qt1'), Buf('qt2')
        PT = [self.sb(st, 'PT', (128, 512), BF16) for _ in range(4)]
        b_PT = [Buf('PT%d' % q) for q in range(4)]
        acc = [self.sb(st, 'acc', (128, 512), F32) for _ in range(2)]
        b_acc = [Buf('acc0'), Buf('acc1')]
        rcp = self.sb(st, 'rcp', (128, 512), F32)
        b_rcp = Buf('rcp')
        oT = self.sb(st, 'oT', (128, 512), BF16)
        b_oT = Buf('oT')
        vt = self.sb(st, 'vt', (128, 512), F32)
        b_vt = Buf('vt')
        cnt = [0]

        def head(h, t0, n, kchunks, hi):
            lat = t0 >= CTX
            Qa_, Qf_, Qp_, bQ_ = Qa[hi % 2], Qf[hi % 2], Qp[hi % 2], b_Q[hi % 2]
            pO_, bpO_ = pO[hi % 2], b_pO[hi % 2]
            gh_, bgh_ = gh[hi % 2], b_gh[hi % 2]
            S.dma('sp', gh_[:, 0:n], gscr[:, h, t0:t0 + n], reads=[b_gscr], writes=[bgh_])
            if KATT < 1:
                return
            for kc in range(2):
                S.op('pe', lambda e, kc=kc: e.matmul(pM[:, 0:n], lhsT=wuq[:, kc, h * 192:h * 192 + 128], rhs=qnT[:, kc, t0:t0 + n],
                                                     start=(kc == 0), stop=(kc == 1)), reads=[b_w, b_qk], writes=[b_pM])
            S.op('act', lambda e: e.activation(out=qh[:, 0:n], in_=pM[:, 0:n], func=AF.Copy), reads=[b_pM], writes=[b_qh])
            S.op('pe', lambda e: e.matmul(pM[:, 0:n], lhsT=wukT[:, h, :], rhs=qh[:, 0:n], start=True, stop=True),
                 reads=[b_w, b_qh], writes=[b_pM])
            S.op('dve', lambda e: e.tensor_copy(out=Qa_[:, 0:n], in_=pM[:, 0:n]), reads=[b_pM], writes=[bQ_])
            if KATT == 1 and os.environ.get('KSUB', '') == 'a':
                return
            for kc in range(2):
                S.op('pe', lambda e, kc=kc: e.matmul(pR[0:64, 0:n], lhsT=wuq[:, kc, h * 192 + 128:h * 192 + 192], rhs=qnT[:, kc, t0:t0 + n],
                                                     start=(kc == 0), stop=(kc == 1)), reads=[b_w, b_qk], writes=[b_pR])
            S.op('act', lambda e: e.activation(out=Qf_[:, 0:n], in_=pR[0:64, 0:n], func=AF.Copy), reads=[b_pR], writes=[bQ_])
            KSUB = os.environ.get('KSUB', '')
            if KATT == 1 and KSUB == 'b':
                return
            if lat:
                l0 = t0 - CTX
                S.op('dve', lambda e: e.tensor_tensor(out=t1[:, 0:n], in0=pR[0:64, 0:n], in1=rope[:, 0, l0:l0 + n], op=ALU.mult),
                     reads=[b_pR, b_w], writes=[b_t1])
                if KATT == 1 and KSUB == 'c':
                    return
                for kc in range(2):
                    S.op('pe', lambda e, kc=kc: e.matmul(pR[0:64, 0:n], lhsT=wuqr[:, kc, h, :], rhs=qnT[:, kc, t0:t0 + n],
                                                         start=(kc == 0), stop=(kc == 1)), reads=[b_w, b_qk], writes=[b_pR])
                if KATT == 1 and KSUB == 'd':
                    return
                S.op('dve', lambda e: e.tensor_tensor(out=t2[:, 0:n], in0=pR[0:64, 0:n], in1=rope[:, 1, l0:l0 + n], op=ALU.mult),
                     reads=[b_pR, b_w], writes=[b_t2])
                S.op('pool', lambda e: e.tensor_tensor(out=Qp_[:, 0:n], in0=t1[:, 0:n], in1=t2[:, 0:n], op=ALU.add),
                     reads=[b_t1, b_t2], writes=[bQ_])
            nk = len(kchunks)
            if KATT < 2:
                return
            for qi, kt in enumerate(kchunks):
                ci = cnt[0]
                cnt[0] += 1
                pS_, bpS_ = pS[ci % 2], b_pS[ci % 2]
                PT_, bPT_ = PT[ci % 4], b_PT[ci % 4]
                Qr_ = Qf_ if (kt < 2 or not lat) else Qp_
                S.op('pe', lambda e, kt=kt, pS_=pS_: e.matmul(pS_[:, 0:n], lhsT=KT[:, kt * 128:(kt + 1) * 128], rhs=Qa_[:, 0:n], start=True, stop=False),
                     reads=[b_qk, bQ_], writes=[bpS_])
                S.op('pe', lambda e, kt=kt, pS_=pS_, Qr_=Qr_: e.matmul(pS_[:, 0:n], lhsT=KTr[:, kt * 128:(kt + 1) * 128], rhs=Qr_[:, 0:n], start=False, stop=True),
                     reads=[b_qk, bQ_], writes=[bpS_])
                S.op('act', lambda e, pS_=pS_, PT_=PT_: e.activation(out=PT_[:, 0:n], in_=pS_[:, 0:n], func=AF.Exp, scale=SCALE),
                     reads=[bpS_], writes=[bPT_])
                S.op('pe', lambda e, kt=kt, PT_=PT_, qi=qi: e.matmul(pO_[:, 0:n], lhsT=V[:, kt, :], rhs=PT_[:, 0:n], start=(qi == 0), stop=(qi == nk - 1)),
                     reads=[b_qk, bPT_], writes=[bpO_])
                a_, ba_ = acc[qi % 2], b_acc[qi % 2]
                eng = 'dve' if qi % 2 == 0 else 'pool'
                if qi < 2:
                    S.op(eng, lambda e, a_=a_, PT_=PT_: e.tensor_copy(out=a_[:, 0:n], in_=PT_[:, 0:n]), reads=[bPT_], writes=[ba_])
                else:
                    S.op(eng, lambda e, a_=a_, PT_=PT_: e.tensor_tensor(out=a_[:, 0:n], in0=a_[:, 0:n], in1=PT_[:, 0:n], op=ALU.add),
                         reads=[bPT_, ba_], writes=[ba_])
            if KATT < 3:
                return
            S.op('pe', lambda e: e.matmul(pM[:, 0:n], lhsT=onesf[:], rhs=acc[0][:, 0:n], start=True, stop=False),
                 reads=[b_w, b_acc[0]], writes=[b_pM])
            S.op('pe', lambda e: e.matmul(pM[:, 0:n], lhsT=onesf[:], rhs=acc[1][:, 0:n], start=False, stop=True),
                 reads=[b_w, b_acc[1]], writes=[b_pM])
            S.op('dve', lambda e: e.reciprocal(out=rcp[:, 0:n], in_=pM[:, 0:n]), reads=[b_pM], writes=[b_rcp])
            if KATT < 4:
                return
            S.op('act', lambda e: e.activation(out=oT[:, 0:n], in_=pO_[:, 0:n], func=AF.Copy), reads=[bpO_], writes=[b_oT])
            S.op('pe', lambda e: e.matmul(pM[:, 0:n], lhsT=wuv[:, h, :], rhs=oT[:, 0:n], start=True, stop=True),
                 reads=[b_w, b_oT], writes=[b_pM])
            S.op('dve', lambda e: e.tensor_tensor(out=vt[:, 0:n], in0=pM[:, 0:n], in1=rcp[:, 0:n], op=ALU.mult),
                 reads=[b_pM, b_rcp], writes=[b_vt])
            S.op('pool', lambda e: e.tensor_tensor(out=actT[:, h, 0:n], in0=vt[:, 0:n], in1=gh_[:, 0:n], op=ALU.mult),
                 reads=[b_vt, bgh_], writes=[b_actT])

        def qblock(t0, n, kchunks):
            for h in range(16):
                head(h, t0, n, kchunks, h)
            for tt in range(n // 128):
                t = t0 // 128 + tt
                self.phase_c_tile(c, li, b, t, m, (lambda k, tt=tt: actT[:, k, tt * 128:(tt + 1) * 128]), b_actT, wout, b_wout, last)

        if with_ctx_out:
            qblock(0, CTX, [0, 1])
        for q in range(SEQ // 512):
            qblock(CTX + q * 512, 512, list(range(NT)))
        S.flush()

    def layer_s5(self, li, i, last):
        S = self.S
        nb = self.nb
        j = i // 3
        G, NM = 128, EXT + 1
        TWO_PI = 2.0 * math.pi * (1.0 - 2e-7)
        HALF_PI = 0.5 * math.pi * (1.0 - 2e-7)
        with_ctx_out = not last

        def revap(t, start, n, rowlen, np_=128):
            return bass.AP(t, start, [[rowlen, np_], [-1, n]])

        with contextlib.ExitStack() as st:
            m = self.modulation(st, i)
            winb, b_winb = self.cast_weight('s5_w_in', j, 1024, 4096)
            woutb, b_woutb = self.cast_weight('s5_w_out', j, 2048, 1024)
            wglub = self.dram('wglub', (16, 2048, 256), BF16)
            b_wglub = Buf('wglub')
            srcg = self.w['s5_w_glu'][j].rearrange("r (s k c) -> k r s c", s=2, k=16, c=128)
            for k in range(16):
                S.dma('pool', wglub[k].rearrange("r (s c) -> r s c", s=2), srcg[k], reads=[self.b_in], writes=[b_wglub])
            uscr = [self.dram('uscr', (128, 16, EXT), BF16) for _ in range(nb)]
            gscr = [self.dram('gscr', (128, 16, EXT), BF16) for _ in range(nb)]
            yscr = [self.dram('yscr', (128, 16, EXT), BF16) for _ in range(nb)]
            b_uscr = [Buf('uscr%d' % b) for b in range(nb)]
            b_gscr = [Buf('gscr%d' % b) for b in range(nb)]
            b_yscr = [Buf('yscr%d' % b) for b in range(nb)]
            for b in range(nb):
                with contextlib.ExitStack() as st3:
                    hT = self.sb(st3, 'hT', (128, 8, EXT), BF16)
                    b_hT = Buf('hT')
                    self.phase_a(st3, li, b, m, hT, b_hT)
                    wg = [self.sb(st3, 'wg', (128, 8, 128), BF16) for _ in range(2)]
                    b_wg = [Buf('wg0'), Buf('wg1')]
                    gst = [self.sb(st3, 'gst', (128, 512), BF16) for _ in range(2)]
                    b_gst = [Buf('gst0'), Buf('gst1')]
                    pg = [self.ps(st3, 'pg') for _ in range(3)]
                    b_pg = [Buf('pg0'), Buf('pg1'), Buf('pg2')]
                    cnt = [0]

                    def u_chunk(k, b=b, hT=hT, b_hT=b_hT, wg=wg, b_wg=b_wg, gst=gst, b_gst=b_gst, pg=pg, b_pg=b_pg, cnt=cnt):
                        w_, bw_ = wg[k % 2], b_wg[k % 2]
                        S.dma('sp', w_[:], winb[:, k * 128:(k + 1) * 128].rearrange("(c p) n -> p c n", p=128), reads=[b_winb], writes=[bw_])
                        isg = k >= 16
                        dst, bdst = (gscr[b], b_gscr[b]) if isg else (uscr[b], b_uscr[b])
                        t0 = 0
                        while t0 < EXT:
                            n = min(512, EXT - t0) if t0 >= CTX else CTX
                            p_, bp_ = pg[cnt[0] % 3], b_pg[cnt[0] % 3]
                            g_, bg_ = gst[cnt[0] % 2], b_gst[cnt[0] % 2]
                            cnt[0] += 1
                            for kk in range(8):
                                S.op('pe', lambda e, kk=kk, p_=p_, t0=t0, n=n: e.matmul(p_[:, 0:n], lhsT=w_[:, kk, :], rhs=hT[:, kk, t0:t0 + n],
                                                                                     start=(kk == 0), stop=(kk == 7)), reads=[bw_, b_hT], writes=[bp_])
                            S.op('act', lambda e, p_=p_, g_=g_, n=n: e.activation(out=g_[:, 0:n], in_=p_[:, 0:n], func=(AF.Silu if isg else AF.Copy)),
                                 reads=[bp_], writes=[bg_])
                            S.dma('act', dst[:, k % 16, t0:t0 + n], g_[:, 0:n], reads=[bg_], writes=[bdst])
                            t0 += n
                    for k in range(32):
                        u_chunk(k)
                    S.flush()
            if KSTOP == 'u':
                return
            with contextlib.ExitStack() as st3:
                if KSTOP != 'nossm':
                    self.s5_ssm(st3, j, uscr, b_uscr, yscr, b_yscr, revap, TWO_PI, HALF_PI)
            if KSTOP in ('sprep', 'ssm1', 'ssm'):
                return
            for b in range(nb):
                with contextlib.ExitStack() as st3:
                    self.s5_glu(st3, li, j, b, m, yscr[b], b_yscr[b], gscr[b], b_gscr[b], wglub, b_wglub, woutb, b_woutb, with_ctx_out, last)

    def s5_glu(self, st, li, j, b, m, yscr, b_yscr, gscr, b_gscr, wglub, b_wglub, woutb, b_woutb, with_ctx_out, last):
        S = self.S
        wout = self.sb(st, 'wout', (128, 16, 1024), BF16)
        b_wout = Buf('wout')
        S.dma('sp', wout[:], woutb.rearrange("(c p) n -> p c n", p=128), reads=[b_woutb], writes=[b_wout])
        bg = self.sb(st, 'bglu', (128, 32), F32)
        b_bg = Buf('bglu')
        self.load_cols(bg[:, :], [self.w['s5_b_glu'][j:j + 1, q * 128:(q + 1) * 128] for q in range(32)], b_bg)
        yb = self.sb(st, 'yblk', (128, 16, 512), BF16)
        gb = self.sb(st, 'gblk', (128, 16, 512), BF16)
        actT = self.sb(st, 'actT', (128, 16, 512), BF16)
        b_yb, b_gb, b_actT = Buf('yblk'), Buf('gblk'), Buf('actT')
        wk = [self.sb(st, 'wk', (128, 16, 256), BF16) for _ in range(2)]
        b_wk = [Buf('wk0'), Buf('wk1')]
        sg = self.sb(st, 'sgl', (128, 512), F32)
        za = self.sb(st, 'zal', (128, 512), F32)
        b_sg, b_za = Buf('sgl'), Buf('zal')
        pa = [self.ps(st, 'pa') for _ in range(2)]
        pb = [self.ps(st, 'pb') for _ in range(2)]
        b_pa = [Buf('pa0'), Buf('pa1')]
        b_pb = [Buf('pb0'), Buf('pb1')]
        c = self.alloc_c(st)
        cnt = [0]

        def block(t0, n):
            S.dma('sp', yb[:, :, 0:n], yscr[:, :, t0:t0 + n], reads=[b_yscr], writes=[b_yb])
            S.dma('act', gb[:, :, 0:n], gscr[:, :, t0:t0 + n], reads=[b_gscr], writes=[b_gb])
            for k in range(16):
                ci = cnt[0]
                cnt[0] += 1
                w_, bw_ = wk[ci % 2], b_wk[ci % 2]
                pa_, bpa_, pb_, bpb_ = pa[ci % 2], b_pa[ci % 2], pb[ci % 2], b_pb[ci % 2]
                S.dma('sp' if k % 2 == 0 else 'act', w_[:], wglub[k].rearrange("(c p) n -> p c n", p=128), reads=[b_wglub], writes=[bw_])
                for kk in range(16):
                    S.op('pe', lambda e, kk=kk, w_=w_, pa_=pa_: e.matmul(pa_[:, 0:n], lhsT=w_[:, kk, 0:128], rhs=yb[:, kk, 0:n],
                                                                        start=(kk == 0), stop=(kk == 15)), reads=[bw_, b_yb], writes=[bpa_])
                for kk in range(16):
                    S.op('pe', lambda e, kk=kk, w_=w_, pb_=pb_: e.matmul(pb_[:, 0:n], lhsT=w_[:, kk, 128:256], rhs=yb[:, kk, 0:n],
                                                                        start=(kk == 0), stop=(kk == 15)), reads=[bw_, b_yb], writes=[bpb_])
                S.op('act', lambda e, k=k, pb_=pb_: e.activation(out=sg[:, 0:n], in_=pb_[:, 0:n], func=AF.Sigmoid, bias=bg[:, 16 + k:17 + k], scale=1.0),
                     reads=[bpb_, b_bg], writes=[b_sg])
                S.op('dve', lambda e, k=k, pa_=pa_: e.scalar_tensor_tensor(out=za[:, 0:n], in0=pa_[:, 0:n], scalar=bg[:, k:k + 1], in1=sg[:, 0:n],
                                                                         op0=ALU.add, op1=ALU.mult), reads=[bpa_, b_sg, b_bg], writes=[b_za])
                S.op('pool', lambda e, k=k: e.tensor_tensor(out=actT[:, k, 0:n], in0=za[:, 0:n], in1=gb[:, k, 0:n], op=ALU.mult),
                     reads=[b_za, b_gb], writes=[b_actT])
            for tt in range(n // 128):
                t = t0 // 128 + tt
                self.phase_c_tile(c, li, b, t, m, (lambda k, tt=tt: actT[:, k, tt * 128:(tt + 1) * 128]), b_actT, wout, b_wout, last)
        if with_ctx_out:
            block(0, CTX)
        for q in range(SEQ // 512):
            block(CTX + q * 512, 512)
        S.flush()

    def s5_ssm(self, st, j, uscr, b_uscr, yscr, b_yscr, revap, TWO_PI, HALF_PI):
        S = self.S
        nb = self.nb
        G, NM = 128, EXT + 1
        iot = self.sb(st, 'iot', (128, NM), I32)
        sgnX = self.sb(st, 'sgnX', (128, 1), F32)
        hpi = self.sb(st, 'hpi', (128, 1), F32)
        rmask = self.sb(st, 'rmask', (128, 8), BF16)
        dcol = self.sb(st, 'dcol', (128, 16), F32)
        b_k = Buf('s5const')
        S.op('pool', lambda e: e.iota(iot[:], pattern=[[1, NM]], base=0, channel_multiplier=0), writes=[b_k])
        S.op('pool', lambda e: e.memset(sgnX[0:64, :], 1.0), writes=[b_k])
        S.op('pool', lambda e: e.memset(sgnX[64:128, :], -1.0), writes=[b_k])
        S.op('pool', lambda e: e.memset(hpi[:], HALF_PI), writes=[b_k])
        S.op('pool', lambda e: e.memset(rmask[:], 1.0), writes=[b_k])
        S.op('pool', lambda e: e.affine_select(out=rmask[:], in_=rmask[:], pattern=[[-16, 8]], compare_op=ALU.is_ge, fill=0.0,
                                               base=0, channel_multiplier=1), reads=[b_k], writes=[b_k])
        S.op('pool', lambda e: e.affine_select(out=rmask[:], in_=rmask[:], pattern=[[16, 8]], compare_op=ALU.is_ge, fill=0.0,
                                               base=15, channel_multiplier=-1), reads=[b_k], writes=[b_k])
        self.load_cols(dcol[:, :], [self.w['s5_d'][j:j + 1, q * 128:(q + 1) * 128] for q in range(16)], b_k)
        TB1, TB2, TC1, TC2, fT, rhoT = [], [], [], [], [], []
        b_tab = Buf('s5tab')
        for d in range(2):
            TB1.append(self.sb(st, 'TB1', (128, G, 16), BF16))
            TB2.append(self.sb(st, 'TB2', (128, G, 16), BF16))
            TC1.append(self.sb(st, 'TC1', (128, G, 16), BF16))
            TC2.append(self.sb(st, 'TC2', (128, G, 16), BF16))
            fT.append(self.sb(st, 'fT', (128, G), F32))
            rhoT.append(self.sb(st, 'rhoT', (128, G), F32))
        for d in range(2):
            with contextlib.ExitStack() as st2:
                N1 = self.sb(st2, 'N1', (128, 128), F32)
                N2 = self.sb(st2, 'N2', (128, 128), F32)
                LR = self.sb(st2, 'LR', (128, G), F32)
                LI = self.sb(st2, 'LI', (128, G), F32)
                DT = self.sb(st2, 'DT', (128, G), F32)
                BA = self.sb(st2, 'BA', (128, G, 16), F32)
                BB = self.sb(st2, 'BB', (128, G, 16), F32)
                CAr = self.sb(st2, 'CAr', (128, G, 16), F32)
                CBr = self.sb(st2, 'CBr', (128, G, 16), F32)
                T = [self.sb(st2, 'T%d' % q, (128, G), F32) for q in range(8)]
                TI = self.sb(st2, 'TI', (128, G), I32)
                W1 = self.sb(st2, 'W1', (128, G, 16), F32)
                W2 = self.sb(st2, 'W2', (128, G, 16), F32)
                pT = self.ps(st2, 'pT')
                bb = Buf('s5prep')
                b_pT = Buf('pT')
                S.dma('sp', N1[:, 0:64], self.w['s5_lam_re'][j, d], reads=[self.b_in], writes=[bb])
                S.dma('sp', N1[:, 64:128], self.w['s5_lam_re'][j, d], reads=[self.b_in], writes=[bb])
                S.dma('sp', N2[:, 0:64], self.w['s5_lam_im'][j, d], reads=[self.b_in], writes=[bb])
                S.dma('sp', N2[:, 64:128], self.w['s5_lam_im'][j, d], reads=[self.b_in], writes=[bb])
                S.dma('act', DT[:], self.w['s5_log_dt'][j, d].partition_broadcast(128), reads=[self.b_in], writes=[bb])
                bsrc_re = self.w['s5_b_re'][j, d].rearrange("g p j -> p g j")
                bsrc_im = self.w['s5_b_im'][j, d].rearrange("g p j -> p g j")
                for q in range(4):
                    gs = slice(q * 32, (q + 1) * 32)
                    S.dma('sp', BA[0:64, gs, :], bsrc_re[:, gs, :], reads=[self.b_in], writes=[bb])
                    S.dma('act', BA[64:128, gs, :], bsrc_im[:, gs, :], reads=[self.b_in], writes=[bb])
                    S.dma('sp', BB[0:64, gs, :], bsrc_im[:, gs, :], reads=[self.b_in], writes=[bb])
                    S.dma('act', BB[64:128, gs, :], bsrc_re[:, gs, :], reads=[self.b_in], writes=[bb])
                for (N_, L_) in ((N1, LR), (N2, LI)):
                    S.op('pe', lambda e, N_=N_: e.transpose(pT[:, 0:128], N_[:], self.ident[:]), reads=[bb, self.b_const], writes=[b_pT])
                    S.op('dve', lambda e, L_=L_: e.tensor_copy(out=L_[:], in_=pT[:, 0:128]), reads=[b_pT], writes=[bb])
                cre = self.w['s5_c_re'][j, d].rearrange("g i p -> (g i) p")
                cim = self.w['s5_c_im'][j, d].rearrange("g i p -> (g i) p")
                for fb in range(16):
                    for (dstT, first, second) in ((CAr, cre, cim), (CBr, cim, cre)):
                        S.dma('sp', N1[:, 0:64], first[fb * 128:(fb + 1) * 128, :], reads=[self.b_in], writes=[bb])
                        S.dma('act', N1[:, 64:128], second[fb * 128:(fb + 1) * 128, :], reads=[self.b_in], writes=[bb])
                        S.op('pe', lambda e: e.transpose(pT[:, 0:128], N1[:], self.ident[:]), reads=[bb, self.b_const], writes=[b_pT])
                        S.op('dve', lambda e, dstT=dstT, fb=fb: e.tensor_copy(out=dstT[:, fb * 8:(fb + 1) * 8, :],
                                                                             in_=pT[:, 0:128].rearrange("p (g i) -> p g i", i=16)),
                             reads=[b_pT], writes=[bb])
                lre, rd, th, tr, mag, sn, cs, tmp = T

                def dv(fn):
                    S.op('dve', fn, reads=[bb, b_k], writes=[bb])

                def ac(fn):
                    S.op('act', fn, reads=[bb, b_k], writes=[bb])
                ac(lambda e: e.activation(out=DT[:], in_=DT[:], func=AF.Exp))
                dv(lambda e: e.tensor_scalar(out=lre[:], in0=LR[:], scalar1=-1e-4, scalar2=None, op0=ALU.min))
                dv(lambda e: e.tensor_tensor(out=rd[:], in0=lre[:], in1=DT[:], op=ALU.mult))
                dv(lambda e: e.tensor_tensor(out=th[:], in0=LI[:], in1=DT[:], op=ALU.mult))
                dv(lambda e: e.tensor_scalar(out=tr[:], in0=th[:], scalar1=1.0 / (2.0 * math.pi), scalar2=None, op0=ALU.mult))
                dv(lambda e: e.tensor_copy(out=TI[:], in_=tr[:]))
                dv(lambda e: e.tensor_tensor(out=fT[d][:], in0=tr[:], in1=TI[:], op=ALU.subtract))
                ac(lambda e: e.activation(out=sn[:], in_=fT[d][:], func=AF.Sin, scale=TWO_PI))
                dv(lambda e: e.tensor_scalar(out=TI[:], in0=tr[:], scalar1=0.25, scalar2=None, op0=ALU.add))
                dv(lambda e: e.tensor_tensor(out=tmp[:], in0=tr[:], in1=TI[:], op=ALU.subtract))
                ac(lambda e: e.activation(out=cs[:], in_=tmp[:], func=AF.Sin, scale=TWO_PI, bias=hpi[:, 0:1]))
                ac(lambda e: e.activation(out=rhoT[d][:], in_=rd[:], func=AF.Exp))
                mag = rhoT[d]
                dv(lambda e: e.tensor_tensor(out=cs[:], in0=cs[:], in1=mag[:], op=ALU.mult))
                dv(lambda e: e.tensor_tensor(out=sn[:], in0=sn[:], in1=mag[:], op=ALU.mult))
                dv(lambda e: e.tensor_scalar(out=cs[:], in0=cs[:], scalar1=-1.0, scalar2=None, op0=ALU.add))
                dv(lambda e: e.tensor_tensor(out=tmp[:], in0=lre[:], in1=lre[:], op=ALU.mult))
                dv(lambda e: e.tensor_tensor(out=tr[:], in0=LI[:], in1=LI[:], op=ALU.mult))
                dv(lambda e: e.tensor_tensor(out=tmp[:], in0=tmp[:], in1=tr[:], op=ALU.add))
                dv(lambda e: e.reciprocal(out=tmp[:], in_=tmp[:]))
                dv(lambda e: e.tensor_tensor(out=tr[:], in0=cs[:], in1=lre[:], op=ALU.mult))
                dv(lambda e: e.tensor_tensor(out=th[:], in0=sn[:], in1=LI[:], op=ALU.mult))
                dv(lambda e: e.tensor_tensor(out=tr[:], in0=tr[:], in1=th[:], op=ALU.add))
                dv(lambda e: e.tensor_tensor(out=tr[:], in0=tr[:], in1=tmp[:], op=ALU.mult))
                dv(lambda e: e.tensor_tensor(out=th[:], in0=sn[:], in1=lre[:], op=ALU.mult))
                dv(lambda e: e.tensor_tensor(out=rd[:], in0=cs[:], in1=LI[:], op=ALU.mult))
                dv(lambda e: e.tensor_tensor(out=th[:], in0=th[:], in1=rd[:], op=ALU.subtract))
                dv(lambda e: e.tensor_tensor(out=th[:], in0=th[:], in1=tmp[:], op=ALU.mult))
                dv(lambda e: e.tensor_scalar(out=th[:], in0=th[:], scalar1=sgnX[:, 0:1], scalar2=None, op0=ALU.mult))
                crb = tr[:].unsqueeze(2).to_broadcast([128, G, 16])
                cib = th[:].unsqueeze(2).to_broadcast([128, G, 16])
                dv(lambda e: e.tensor_tensor(out=W1[:], in0=BA[:], in1=crb, op=ALU.mult))
                dv(lambda e: e.tensor_tensor(out=W2[:], in0=BB[:], in1=cib, op=ALU.mult))
                dv(lambda e: e.tensor_tensor(out=TB1[d][:], in0=W1[:], in1=W2[:], op=ALU.subtract))
                dv(lambda e: e.tensor_tensor(out=W1[:], in0=BB[:], in1=crb, op=ALU.mult))
                dv(lambda e: e.tensor_tensor(out=W2[:], in0=BA[:], in1=cib, op=ALU.mult))
                dv(lambda e: e.tensor_tensor(out=W1[:], in0=W1[:], in1=W2[:], op=ALU.add))
                dv(lambda e: e.tensor_scalar(out=TB2[d][:], in0=W1[:], scalar1=sgnX[:, 0:1], scalar2=None, op0=ALU.mult))
                dv(lambda e: e.tensor_scalar(out=TC1[d][:], in0=CAr[:], scalar1=sgnX[:, 0:1], scalar2=None, op0=ALU.mult))
                dv(lambda e: e.tensor_scalar(out=TC2[d][:], in0=CBr[:], scalar1=-1.0, scalar2=None, op0=ALU.mult))
                S.op('dve', lambda e: e.tensor_copy(out=tmp[:, 0:1], in_=tmp[:, 0:1]), reads=[bb], writes=[b_tab])
                S.flush()
        if KSTOP == 'sprep':
            return
        D1m = self.sb(st, 'D1m', (128, 8, 128), BF16)
        D2m = self.sb(st, 'D2m', (128, 8, 128), BF16)
        CAp = self.sb(st, 'CAp', (128, 8, 128), BF16)
        CA2p = self.sb(st, 'CA2p', (128, 8, 128), BF16)
        Dt = self.sb(st, 'Dt', (128, 2, 128), BF16)
        b_D, b_Dt = Buf('Dm'), Buf('Dt')
        cosT = self.sb(st, 'cosT', (128, NM), F32)
        sinT = self.sb(st, 'sinT', (128, NM), F32)
        b_rot = Buf('rot')
        SEGW = 1089
        ki = self.sb(st, 'ki', (128, SEGW), I32)
        rr = self.sb(st, 'rr', (128, SEGW), F32)
        b_ki, b_rr = Buf('ki'), Buf('rr')
        uT = [self.sb(st, 'uTfb', (128, EXT), BF16) for _ in range(nb)]
        Yacc = [self.sb(st, 'Yacc', (128, EXT), F32) for _ in range(nb)]
        b_uT = [Buf('uT%d' % b) for b in range(nb)]
        b_Y = [Buf('Yacc%d' % b) for b in range(nb)]
        ygl = self.sb(st, 'ygl', (128, EXT), BF16)
        b_ygl = Buf('ygl')
        ta = self.sb(st, 'ta', (128, 512), F32)
        tb = self.sb(st, 'tb', (128, 512), F32)
        Wp = self.sb(st, 'Wp', (128, 512), F32)
        Wr = [self.sb(st, 'Wr', (128, 512), F32) for _ in range(2)]
        cW = self.sb(st, 'cW', (128, 512), BF16)
        sW = self.sb(st, 'sW', (128, 512), BF16)
        b_ta, b_tb, b_Wp, b_cW, b_sW = [Buf(n) for n in 'ta tb Wp cW sW'.split()]
        b_Wr = [Buf('Wr0'), Buf('Wr1')]
        pX1, pX2, pY = self.ps(st, 'pX1'), self.ps(st, 'pX2'), self.ps(st, 'pY')
        pD = self.ps(st, 'pD', (128, 512), BF16)
        b_pX1, b_pX2, b_pY, b_pD = Buf('pX1'), Buf('pX2'), Buf('pY'), Buf('pD')
        blocks = [(0, CTX)] + [(CTX + q * 512, 512) for q in range(SEQ // 512)]
        order = {0: blocks, 1: [blocks[0]] + blocks[:0:-1]}
        wcnt = [0]

        def rot_tables(d, g):
            fcol = fT[d][:, g:g + 1]
            for s0 in range(0, NM, SEGW):
                n = min(SEGW, NM - s0)
                S.op('dve', lambda e, s0=s0, n=n: e.tensor_scalar(out=ki[:, 0:n], in0=iot[:, s0:s0 + n], scalar1=fcol, scalar2=None, op0=ALU.mult),
                     reads=[b_k, b_tab], writes=[b_ki])
                S.op('dve', lambda e, s0=s0, n=n: e.scalar_tensor_tensor(out=rr[:, 0:n], in0=iot[:, s0:s0 + n], scalar=fcol, in1=ki[:, 0:n],
                                                                         op0=ALU.mult, op1=ALU.subtract), reads=[b_k, b_tab, b_ki], writes=[b_rr])
                S.op('act', lambda e, s0=s0, n=n: e.activation(out=sinT[:, s0:s0 + n], in_=rr[:, 0:n], func=AF.Sin, scale=TWO_PI),
                     reads=[b_rr], writes=[b_rot])
                S.op('dve', lambda e, s0=s0, n=n: e.tensor_scalar(out=ki[:, 0:n], in0=iot[:, s0:s0 + n], scalar1=fcol, scalar2=0.25, op0=ALU.mult, op1=ALU.add),
                     reads=[b_k, b_tab], writes=[b_ki])
                S.op('dve', lambda e, s0=s0, n=n: e.scalar_tensor_tensor(out=rr[:, 0:n], in0=iot[:, s0:s0 + n], scalar=fcol, in1=ki[:, 0:n],
                                                                         op0=ALU.mult, op1=ALU.subtract), reads=[b_k, b_tab, b_ki], writes=[b_rr])
                S.op('act', lambda e, s0=s0, n=n: e.activation(out=cosT[:, s0:s0 + n], in_=rr[:, 0:n], func=AF.Sin, scale=TWO_PI, bias=hpi[:, 0:1]),
                     reads=[b_rr, b_k], writes=[b_rot])

        def scan_dir(d, g, g8, b):
            rho = rhoT[d][:, g:g + 1]
            first = True
            for (t0, n) in order[d]:
                if d == 0:
                    m_lo = t0 + 1
                else:
                    base = CTX if t0 < CTX else (EXT + CTX)
                    m_lo = base - (t0 + n - 1)
                wi = wcnt[0] % 2
                wcnt[0] += 1
                Wr_, bWr_ = Wr[wi], b_Wr[wi]
                Wprev, bWprev = Wr[1 - wi], b_Wr[1 - wi]
                S.op('pe', lambda e, t0=t0, n=n: e.matmul(pX1[:, 0:n], lhsT=D1m[:, g8, :], rhs=uT[b][:, t0:t0 + n], start=True, stop=True),
                     reads=[b_D, b_uT[b]], writes=[b_pX1])
                S.op('pe', lambda e, t0=t0, n=n: e.matmul(pX2[:, 0:n], lhsT=D2m[:, g8, :], rhs=uT[b][:, t0:t0 + n], start=True, stop=True),
                     reads=[b_D, b_uT[b]], writes=[b_pX2])
                if d == 0:
                    c_in = cosT[:, m_lo:m_lo + n]
                    s_in = sinT[:, m_lo:m_lo + n]
                    o_a, o_b = ta[:, 0:n], tb[:, 0:n]
                    w_in_ = Wr_[:, 0:n]
                    c_in2, s_in2 = c_in, s_in
                else:
                    c_in = revap(cosT, m_lo + n - 1, n, NM)
                    s_in = revap(sinT, m_lo + n - 1, n, NM)
                    o_a, o_b = revap(ta, n - 1, n, 512), revap(tb, n - 1, n, 512)
                    w_in_ = revap(Wr_, n - 1, n, 512)
                    c_in2, s_in2 = c_in, s_in
                S.op('dve', lambda e, n=n, c_in=c_in, o_a=o_a: e.tensor_tensor(out=o_a, in0=pX1[:, 0:n], in1=c_in, op=ALU.mult),
                     reads=[b_pX1, b_rot], writes=[b_ta])
                S.op('dve', lambda e, n=n, s_in=s_in, o_b=o_b: e.tensor_tensor(out=o_b, in0=pX2[:, 0:n], in1=s_in, op=ALU.mult),
                     reads=[b_pX2, b_rot], writes=[b_tb])
                S.op('pool', lambda e, n=n: e.tensor_tensor(out=Wp[:, 0:n], in0=ta[:, 0:n], in1=tb[:, 0:n], op=ALU.add),
                     reads=[b_ta, b_tb], writes=[b_Wp])
                if first:
                    S.op('dve', lambda e, n=n, Wr_=Wr_: e.tensor_tensor_scan(out=Wr_[:, 0:n], data0=rho.to_broadcast([128, n]), data1=Wp[:, 0:n],
                                                                             initial=0.0, op0=ALU.mult, op1=ALU.add),
                         reads=[b_Wp, b_tab], writes=[bWr_])
                else:
                    pn = prev_n[0]
                    S.op('dve', lambda e, n=n, Wr_=Wr_, Wprev=Wprev, pn=pn: e.tensor_tensor_scan(
                        out=Wr_[:, 0:n], data0=rho.to_broadcast([128, n]), data1=Wp[:, 0:n], initial=Wprev[:, pn - 1:pn], op0=ALU.mult, op1=ALU.add),
                         reads=[b_Wp, b_tab, bWprev], writes=[bWr_])
                first = False
                prev_n[0] = n
                S.op('dve', lambda e, n=n, w_in_=w_in_, c_in2=c_in2: e.tensor_tensor(out=cW[:, 0:n], in0=w_in_, in1=c_in2, op=ALU.mult),
                     reads=[bWr_, b_rot], writes=[b_cW])
                S.op('pool', lambda e, n=n, w_in_=w_in_, s_in2=s_in2: e.tensor_tensor(out=sW[:, 0:n], in0=w_in_, in1=s_in2, op=ALU.mult),
                     reads=[bWr_, b_rot], writes=[b_sW])
                S.op('pe', lambda e, n=n: e.matmul(pY[:, 0:n], lhsT=CAp[:, g8, :], rhs=cW[:, 0:n], start=True, stop=False),
                     reads=[b_D, b_cW], writes=[b_pY])
                S.op('pe', lambda e, n=n: e.matmul(pY[:, 0:n], lhsT=CA2p[:, g8, :], rhs=sW[:, 0:n], start=False, stop=True),
                     reads=[b_D, b_sW], writes=[b_pY])
                S.op('dve', lambda e, t0=t0, n=n: e.tensor_tensor(out=Yacc[b][:, t0:t0 + n], in0=pY[:, 0:n], in1=Yacc[b][:, t0:t0 + n], op=ALU.add),
                     reads=[b_pY, b_Y[b]], writes=[b_Y[b]])
        prev_n = [0]

        for fb in range(16 if KSTOP != 'ssm1' else 1):
            for b in range(nb):
                S.dma('sp', uT[b][:], uscr[b][:, fb, :], reads=[b_uscr[b]], writes=[b_uT[b]])
                S.op('act', lambda e, b=b, fb=fb: e.activation(out=Yacc[b][:], in_=uT[b][:], func=AF.Copy, scale=dcol[:, fb:fb + 1]),
                     reads=[b_uT[b], b_k], writes=[b_Y[b]])
            for d in range(2):
                for q, TBx in enumerate((TB1[d], TB2[d])):
                    S.op('pe', lambda e, q=q, TBx=TBx, fb=fb: e.transpose(pD[:, q * 128:(q + 1) * 128],
                                                                          TBx[:, fb * 8:(fb + 1) * 8, :].rearrange("p g j -> p (g j)"), self.identb[:]),
                         reads=[b_tab, self.b_const], writes=[b_pD])
                S.op('act', lambda e: e.activation(out=Dt[:], in_=pD[:, 0:256].rearrange("p (q c) -> p q c", q=2), func=AF.Copy),
                     reads=[b_pD], writes=[b_Dt])
                for q, Dm in enumerate((D1m, D2m)):
                    S.op('pool', lambda e, q=q, Dm=Dm: e.tensor_tensor(out=Dm[:], in0=Dt[:, q:q + 1, :].to_broadcast([128, 8, 128]),
                                                                      in1=rmask[:].unsqueeze(2).to_broadcast([128, 8, 128]), op=ALU.mult),
                         reads=[b_Dt, b_k], writes=[b_D])
                for Cp, TCx in ((CAp, TC1[d]), (CA2p, TC2[d])):
                    S.op('pool', lambda e, Cp=Cp: e.memset(Cp[:], 0.0), writes=[b_D])
                    S.op('pool', lambda e, Cp=Cp, TCx=TCx, fb=fb: e.tensor_copy(out=bass.AP(Cp, 0, [[8 * 128, 128], [144, 8], [1, 16]]),
                                                                             in_=TCx[:, fb * 8:(fb + 1) * 8, :]),
                         reads=[b_tab], writes=[b_D])
                for g8 in range(8):
                    g = fb * 8 + g8
                    rot_tables(d, g)
                    for b in range(nb):
                        scan_dir(d, g, g8, b)
            for b in range(nb):
                S.op('act', lambda e, b=b: e.activation(out=ygl[:], in_=Yacc[b][:], func=AF.Gelu), reads=[b_Y[b]], writes=[b_ygl])
                S.dma('act', yscr[b][:, fb, :], ygl[:], reads=[b_ygl], writes=[b_yscr[b]])
        S.flush()

    def build(self):
        S = self.S
        with contextlib.ExitStack() as st:
            self.setup_consts(st)
            S.flush()
            nl = len(self.layers)
            for li, i in enumerate(self.layers):
                last = (li == nl - 1)
                kind = i % 3
                if kind == 0:
                    self.layer_mla(li, i, last)
                elif kind == 1:
                    self.layer_s5(li, i, last)
                else:
                    self.layer_conv(li, i, last)
            S.finish([self.b_y])
            S.flush()
        S.close()
        return self.nc


_PROG_CACHE = {}


def _get_prog(nb, layers):
    key = (nb, tuple(layers))
    if key not in _PROG_CACHE:
        p = Prog(nb, list(layers))
        p.build()
        _PROG_CACHE[key] = p
    return _PROG_CACHE[key]


def kernel(**inputs):
    nb = 16 // N_CORES
    prog = _get_prog(nb, range(DEPTH))
    rope = rope_table()
    shared = {name: np.ascontiguousarray(np.asarray(inputs[name], dtype=np.float32)) for name, _ in WEIGHT_SPECS}
    shared['c_ctx'] = np.ascontiguousarray(np.asarray(inputs['c_ctx'], dtype=np.float32)[None, :])
    shared['rope'] = rope
    x = np.asarray(inputs['x'], dtype=np.float32)
    c = np.asarray(inputs['c'], dtype=np.float32)
    ctx = np.asarray(inputs['ctx'], dtype=np.float32)
    in_maps = []
    for i in range(N_CORES):
        d = dict(shared)
        d['x'] = np.ascontiguousarray(x[i * nb:(i + 1) * nb])
        d['c'] = np.ascontiguousarray(c[i * nb:(i + 1) * nb])
        d['ctx'] = np.ascontiguousarray(ctx[i * nb:(i + 1) * nb])
        in_maps.append(d)
    res = run_bass_kernel_spmd(prog.nc, in_maps, core_ids=list(range(N_CORES)))
    return np.concatenate([np.asarray(r['y'], dtype=np.float32) for r in res.results], axis=0)
```

```python
import contextlib
import math
import os
KSTOP = os.environ.get('KSTOP', '')
KATT = int(os.environ.get('KATT', '9'))
import numpy as np
import concourse.bass as bass
import concourse.mybir as mybir
from concourse.bass_utils import run_bass_kernel_spmd

F32 = mybir.dt.float32
BF16 = mybir.dt.bfloat16
I32 = mybir.dt.int32
AF = mybir.ActivationFunctionType
ALU = mybir.AluOpType

D = 1024
SEQ = 4096
CTX = 256
EXT = SEQ + CTX
NT = EXT // 128
E = 2048
DEPTH = 4
ALPHA = (2.0 * DEPTH) ** 0.25
EPS = 1e-6
N_CORES = 8


def rope_table():
    t = np.arange(SEQ)
    r, col = (t // 64).astype(np.float32), (t % 64).astype(np.float32)
    inv = (10000.0 ** (-np.arange(16, dtype=np.float32) / 16)).astype(np.float32)
    ang = np.concatenate([r[None] * inv[:, None], r[None] * inv[:, None], col[None] * inv[:, None], col[None] * inv[:, None]], 0)
    return np.ascontiguousarray(np.stack([np.cos(ang), np.sin(ang)], 1).astype(np.float32))


class Buf:
    __slots__ = ('name', 'w', 'r', 'excl')

    def __init__(self, name, excl=None):
        self.name = name
        self.w = None
        self.r = {}
        self.excl = (name[0] == 'p' or name == 'lcp') if excl is None else excl


class Sched:
    ENG = ('pe', 'act', 'dve', 'pool', 'sp')
    NRING = 8

    def __init__(self, nc):
        self.nc = nc
        self.stack = contextlib.ExitStack()
        self.eng = {'pe': nc.tensor, 'act': nc.scalar, 'dve': nc.vector, 'pool': nc.gpsimd, 'sp': nc.sync}
        self.sems = []
        self.esem = {}
        for e in self.ENG:
            self.esem[e] = self._newsem('e_' + e)
        self.ring = {}
        for q in ('sp', 'act', 'pool'):
            self.ring[q] = [self._newsem('d_%s%d' % (q, i)) for i in range(self.NRING)]
        self.ringpos = {q: 0 for q in self.ring}
        self.semval = [0] * len(self.sems)
        self.obs = {e: [0] * len(self.sems) for e in self.ENG}
        self.prog = {e: [] for e in self.ENG}
        self.ninstr = 0

    def _newsem(self, name):
        s = self.stack.enter_context(self.nc.semaphore(name))
        self.sems.append(s)
        return len(self.sems) - 1

    def close(self):
        self.stack.close()

    def _collect(self, engine, reads, writes):
        need = {}

        def add(ev, raw):
            if ev is None:
                return
            si, val, eng = ev
            if eng == engine and not raw:
                return
            if need.get(si, 0) < val:
                need[si] = val
        for b in reads:
            add(b.w, True)
            if b.excl:
                for ev in b.r.values():
                    add(ev, False)
        for b in writes:
            add(b.w, False)
            for ev in b.r.values():
                add(ev, False)
        waits = []
        ob = self.obs[engine]
        for si, val in need.items():
            if ob[si] < val:
                ob[si] = val
                waits.append((si, val))
        return waits

    def _update(self, ev, reads, writes):
        si = ev[0]
        for b in reads:
            b.r[si] = ev
        for b in writes:
            b.w = ev
            b.r = {}

    def op(self, engine, fn, reads=(), writes=()):
        waits = self._collect(engine, reads, writes)
        si = self.esem[engine]
        self.semval[si] += 1
        ev = (si, self.semval[si], engine)
        self.prog[engine].append((waits, fn, si, 1))
        self._update(ev, reads, writes)
        self.ninstr += 1
        return ev

    def dma(self, queue, out, in_, reads=(), writes=(), **kw):
        ring = self.ring[queue]
        si = ring[self.ringpos[queue] % self.NRING]
        self.ringpos[queue] += 1
        waits = self._collect(queue, reads, writes)
        ob = self.obs[queue]
        if ob[si] < self.semval[si]:
            ob[si] = self.semval[si]
            waits.append((si, self.semval[si]))
        self.semval[si] += 16
        ev = (si, self.semval[si], 'dma')
        self.prog[queue].append((waits, (lambda e, out=out, in_=in_, kw=kw: e.dma_start(out=out, in_=in_, **kw)), si, 16))
        self._update(ev, reads, writes)
        self.ninstr += 1
        return ev

    def finish(self, bufs, engine='sp'):
        waits = self._collect(engine, bufs, ())
        self.prog[engine].append((waits, None, None, 0))

    def flush(self):
        sems = self.sems
        prog = self.prog
        eng = self.eng

        def replay(name):
            e = eng[name]
            for waits, fn, si, inc in prog[name]:
                for wsi, val in waits:
                    e.wait_ge(sems[wsi], val)
                if fn is not None:
                    fn(e).then_inc(sems[si], inc)

        with self.nc.Block() as block:
            @block.tensor
            def _(t):
                replay('pe')

            @block.scalar
            def _(t):
                replay('act')

            @block.vector
            def _(t):
                replay('dve')

            @block.gpsimd
            def _(t):
                replay('pool')

            @block.sync
            def _(t):
                replay('sp')
        self.prog = {e: [] for e in self.ENG}


WEIGHT_SPECS = [
    ('w_mod', (4, 1024, 3072)), ('b_mod', (4, 3072)), ('ln_g', (4, 1024)), ('ln_b', (4, 1024)),
    ('mla_w_in', (2, 1024, 2496)), ('mla_q_norm', (2, 256)), ('mla_kv_norm', (2, 128)),
    ('mla_w_uq', (2, 256, 3072)), ('mla_w_uk', (2, 128, 16, 128)), ('mla_w_uv', (2, 128, 16, 128)),
    ('mla_w_out', (2, 2048, 1024)),
    ('s5_w_in', (1, 1024, 4096)), ('s5_lam_re', (1, 2, 128, 64)), ('s5_lam_im', (1, 2, 128, 64)),
    ('s5_log_dt', (1, 2, 128)), ('s5_b_re', (1, 2, 128, 64, 16)), ('s5_b_im', (1, 2, 128, 64, 16)),
    ('s5_c_re', (1, 2, 128, 16, 64)), ('s5_c_im', (1, 2, 128, 16, 64)), ('s5_d', (1, 2048)),
    ('s5_w_glu', (1, 2048, 4096)), ('s5_b_glu', (1, 4096)), ('s5_w_out', (1, 2048, 1024)),
    ('cv_w_in', (1, 1024, 6144)), ('cv_dw', (1, 31, 2048)), ('cv_dw_b', (1, 2048)),
    ('cv_ln_g', (1, 2048)), ('cv_ln_b', (1, 2048)), ('cv_w_out', (1, 2048, 1024)),
]


class Prog:
    def __init__(self, nb, layers, debug_ctx=False):
        self.nb = nb
        self.layers = layers
        self.R = nb + 1
        nc = self.nc = bass.Bass('TRN2', target_bir_lowering=False)
        self.S = Sched(nc)
        self.uid = 0
        self.x_in = nc.dram_tensor("x", [nb, SEQ, D], F32, kind="ExternalInput").ap()
        self.c_in = nc.dram_tensor("c", [nb, D], F32, kind="ExternalInput").ap()
        self.ctx_in = nc.dram_tensor("ctx", [nb, CTX, D], F32, kind="ExternalInput").ap()
        self.cctx_in = nc.dram_tensor("c_ctx", [1, D], F32, kind="ExternalInput").ap()
        self.rope_in = nc.dram_tensor("rope", [64, 2, SEQ], F32, kind="ExternalInput").ap()
        self.w = {}
        for name, shape in WEIGHT_SPECS:
            self.w[name] = nc.dram_tensor(name, list(shape), F32, kind="ExternalInput").ap()
        self.y_out = nc.dram_tensor("y", [nb, SEQ, D], F32, kind="ExternalOutput").ap()
        self.b_y = Buf('y')
        self.debug_ctx = debug_ctx
        if debug_ctx:
            self.ctx_out = nc.dram_tensor("ctx_out", [nb, CTX, D], F32, kind="ExternalOutput").ap()
        self.xs = [nc.dram_tensor("xs%d" % k, [nb, EXT, D], F32, kind="Internal").ap() for k in range(2)]
        self.b_xs = [[Buf('xs%d_%d' % (k, b)) for b in range(nb)] for k in range(2)]
        self.b_in = Buf('inputs')

    def name(self, s):
        self.uid += 1
        return "%s_%d" % (s, self.uid)

    def sb(self, st, nm, shape, dt):
        return st.enter_context(self.nc.sbuf_tensor(self.name(nm), list(shape), dt))

    def ps(self, st, nm, shape=(128, 512), dt=F32):
        return st.enter_context(self.nc.psum_tensor(self.name(nm), list(shape), dt))

    def dram(self, nm, shape, dt):
        return self.nc.dram_tensor(self.name(nm), list(shape), dt, kind="Internal").ap()

    def xrows(self, li, b, t):
        if li == 0:
            if t < 2:
                return self.ctx_in[b, t * 128:(t + 1) * 128, :], self.b_in
            return self.x_in[b, (t - 2) * 128:(t - 1) * 128, :], self.b_in
        k = (li - 1) % 2
        return self.xs[k][b, t * 128:(t + 1) * 128, :], self.b_xs[k][b]

    def xdst(self, li, b, t, last):
        if last:
            if t < 2:
                if self.debug_ctx:
                    return self.ctx_out[b, t * 128:(t + 1) * 128, :], self.b_y
                return None, None
            return self.y_out[b, (t - 2) * 128:(t - 1) * 128, :], self.b_y
        k = li % 2
        return self.xs[k][b, t * 128:(t + 1) * 128, :], self.b_xs[k][b]

    def setup_consts(self, st):
        S = self.S
        self.ident = self.sb(st, 'ident', (128, 128), F32)
        self.identb = self.sb(st, 'identb', (128, 128), BF16)
        self.onesb = self.sb(st, 'onesb', (128, 128), BF16)
        self.b_const = Buf('const')
        ident, identb, onesb = self.ident, self.identb, self.onesb
        S.op('pool', lambda e: e.memset(ident[:], 0.0), writes=[self.b_const])
        S.op('pool', lambda e: e.affine_select(out=ident[:], in_=ident[:], pattern=[[-1, 128]], compare_op=ALU.not_equal,
                                               fill=1.0, base=0, channel_multiplier=1),
             reads=[self.b_const], writes=[self.b_const])
        S.op('pool', lambda e: e.tensor_copy(out=identb[:], in_=ident[:]), reads=[self.b_const], writes=[self.b_const])
        S.op('pool', lambda e: e.memset(onesb[:], 1.0), writes=[self.b_const])

    def cast_weight(self, name, idx, rows, cols):
        src = self.w[name][idx]
        dst = self.dram(name + 'b', (rows, cols), BF16)
        buf = Buf(name + 'b')
        n = 512 if cols % 512 == 0 else cols
        s2 = src.rearrange("k (a n) -> (k a) n", n=n)
        d2 = dst.rearrange("k (a n) -> (k a) n", n=n)
        tot = s2.shape[0]
        step = 2048
        for r0 in range(0, tot, step):
            r1 = min(tot, r0 + step)
            self.S.dma('pool', d2[r0:r1, :], s2[r0:r1, :], reads=[self.b_in], writes=[buf])
        return dst, buf

    def modulation(self, st, i):
        S, R = self.S, self.R
        nb = self.nb
        wmb, b_wmb = self.cast_weight('w_mod', i, 1024, 3072)
        m = {}
        m['modT'] = self.sb(st, 'modT', (128, 16, R), F32)
        m['gt'] = self.sb(st, 'gtbc', (128, R, 1024), F32)
        m['lng'] = self.sb(st, 'lng', (128, 1024), F32)
        m['lnb'] = self.sb(st, 'lnb', (128, 1024), F32)
        m['buf'] = Buf('mod')
        with contextlib.ExitStack() as st2:
            wm = self.sb(st2, 'wm', (128, 8, 3072), BF16)
            crow = self.sb(st2, 'crow', (R, 1024), F32)
            srow = self.sb(st2, 'srow', (R, 1024), F32)
            condT = self.sb(st2, 'condT', (128, 8, R), BF16)
            crep = self.sb(st2, 'crep', (128, 8, 128), BF16)
            bmf = self.sb(st2, 'bmf', (1, 3072), F32)
            bmb = self.sb(st2, 'bmb', (1, 3072), BF16)
            pT = self.ps(st2, 'pT')
            pA = self.ps(st2, 'pA')
            pB = self.ps(st2, 'pB')
            b_wm, b_crow, b_srow, b_condT, b_crep, b_bm, b_pT, b_pA, b_pB = [Buf(n) for n in
                                                                             'wm crow srow condT crep bm pT pA pB'.split()]
            S.dma('sp', wm[:], wmb.rearrange("(c p) n -> p c n", p=128), reads=[b_wmb], writes=[b_wm])
            S.dma('act', crow[0:nb, :], self.c_in[:, :], reads=[self.b_in], writes=[b_crow])
            S.dma('act', crow[nb:nb + 1, :], self.cctx_in[:, :], reads=[self.b_in], writes=[b_crow])
            S.dma('act', bmf[:], self.w['b_mod'][i:i + 1, :], reads=[self.b_in], writes=[b_bm])
            S.dma('act', m['lng'][:], self.w['ln_g'][i].partition_broadcast(128), reads=[self.b_in], writes=[m['buf']])
            S.dma('act', m['lnb'][:], self.w['ln_b'][i].partition_broadcast(128), reads=[self.b_in], writes=[m['buf']])
            S.op('act', lambda e: e.activation(out=srow[:], in_=crow[:], func=AF.Silu), reads=[b_crow], writes=[b_srow])
            S.op('dve', lambda e: e.tensor_copy(out=bmb[:], in_=bmf[:]), reads=[b_bm], writes=[b_bm])
            for c in range(8):
                S.op('pe', lambda e, c=c: e.transpose(pT[:, c * R:(c + 1) * R], srow[:, c * 128:(c + 1) * 128], self.ident[0:R, 0:R]),
                     reads=[b_srow, self.b_const], writes=[b_pT])
            S.op('dve', lambda e: e.tensor_copy(out=condT[:], in_=pT[:, 0:8 * R].rearrange("p (c r) -> p c r", r=R)),
                 reads=[b_pT], writes=[b_condT])
            for fc in range(16):
                for k in range(8):
                    S.op('pe', lambda e, fc=fc, k=k: e.matmul(pA[:, fc * R:(fc + 1) * R], lhsT=wm[:, k, fc * 128:(fc + 1) * 128],
                                                              rhs=condT[:, k, :], start=(k == 0), stop=False),
                         reads=[b_wm, b_condT], writes=[b_pA])
                S.op('pe', lambda e, fc=fc: e.matmul(pA[:, fc * R:(fc + 1) * R], lhsT=bmb[0:1, fc * 128:(fc + 1) * 128],
                                                     rhs=self.onesb[0:1, 0:R], start=False, stop=True),
                     reads=[b_bm, self.b_const], writes=[b_pA])
            S.op('dve', lambda e: e.tensor_copy(out=m['modT'][:, 0:8, :], in_=pA[:, 0:8 * R].rearrange("p (c r) -> p c r", r=R)),
                 reads=[b_pA], writes=[m['buf']])
            S.op('dve', lambda e: e.tensor_scalar(out=m['modT'][:, 8:16, :], in0=pA[:, 8 * R:16 * R].rearrange("p (c r) -> p c r", r=R),
                                                  scalar1=1.0, scalar2=None, op0=ALU.add),
                 reads=[b_pA], writes=[m['buf']])
            for r in range(R):
                S.op('dve', lambda e, r=r: e.tensor_copy(out=crep[:], in_=condT[:, :, r:r + 1].to_broadcast([128, 8, 128])),
                     reads=[b_condT], writes=[b_crep])
                for n in range(2):
                    c0 = 2048 + n * 512
                    for k in range(8):
                        S.op('pe', lambda e, k=k, c0=c0: e.matmul(pB[:], lhsT=crep[:, k, :], rhs=wm[:, k, c0:c0 + 512],
                                                                  start=(k == 0), stop=False),
                             reads=[b_crep, b_wm], writes=[b_pB])
                    S.op('pe', lambda e, c0=c0: e.matmul(pB[:], lhsT=self.onesb[0:1, :], rhs=bmb[0:1, c0:c0 + 512],
                                                         start=False, stop=True),
                         reads=[b_bm, self.b_const], writes=[b_pB])
                    S.op('act', lambda e, r=r, n=n: e.copy(out=m['gt'][:, r, n * 512:(n + 1) * 512], in_=pB[:]),
                         reads=[b_pB], writes=[m['buf']])
            S.flush()
        return m

    def phase_a(self, st, li, b, m, hT, b_hT):
        S = self.S
        with contextlib.ExitStack() as st2:
            xt = [self.sb(st2, 'xt', (128, 1024), F32) for _ in range(2)]
            b_xt = [Buf('xt0'), Buf('xt1')]
            pt = [self.ps(st2, 'pt') for _ in range(2)]
            b_pt = [Buf('pt0'), Buf('pt1')]
            for t in range(NT):
                src, b_src = self.xrows(li, b, t)
                r = self.nb if t < 2 else b
                xa, bxa = xt[t % 2], b_xt[t % 2]
                S.dma('sp' if t % 2 == 0 else 'act', xa[:], src, reads=[b_src], writes=[bxa])
                for half in range(2):
                    pp, bpp = pt[half], b_pt[half]
                    for c in range(4):
                        cc = half * 4 + c
                        S.op('pe', lambda e, cc=cc, c=c, pp=pp, xa=xa: e.transpose(pp[:, c * 128:(c + 1) * 128],
                                                                                    xa[:, cc * 128:(cc + 1) * 128], self.ident[:]),
                             reads=[bxa, self.b_const], writes=[bpp])
                    for c in range(4):
                        cc = half * 4 + c
                        if c % 2 == 0:
                            S.op('act', lambda e, cc=cc, c=c, pp=pp, r=r, t=t: e.activation(
                                out=hT[:, cc, t * 128:(t + 1) * 128], in_=pp[:, c * 128:(c + 1) * 128], func=AF.Identity,
                                scale=m['modT'][:, 8 + cc, r:r + 1], bias=m['modT'][:, cc, r:r + 1]),
                                 reads=[bpp, m['buf']], writes=[b_hT])
                        else:
                            S.op('dve', lambda e, cc=cc, c=c, pp=pp, r=r, t=t: e.tensor_scalar(
                                out=hT[:, cc, t * 128:(t + 1) * 128], in0=pp[:, c * 128:(c + 1) * 128],
                                scalar1=m['modT'][:, 8 + cc, r:r + 1], scalar2=m['modT'][:, cc, r:r + 1],
                                op0=ALU.mult, op1=ALU.add),
                                 reads=[bpp, m['buf']], writes=[b_hT])
            S.flush()

    def alloc_c(self, st):
        c = {}
        c['py'] = [self.ps(st, 'py') for _ in range(2)]
        c['b_py'] = Buf('py')
        c['xr'] = self.sb(st, 'xr', (128, 1024), F32)
        c['yg'] = self.sb(st, 'yg', (128, 1024), F32)
        c['rr'] = self.sb(st, 'rr', (128, 1024), F32)
        c['xn'] = self.sb(st, 'xn', (128, 1024), F32)
        c['stt'] = self.sb(st, 'stt', (128, 2, 6), F32)
        c['mv'] = self.sb(st, 'mv', (128, 4), F32)
        for n in 'xr yg rr xn stt mv'.split():
            c['b_' + n] = Buf(n)
        return c

    def phase_c_tile(self, c, li, b, t, m, lhs_fn, b_act, wout, b_wout, last):
        S = self.S
        dst, b_dst = self.xdst(li, b, t, last)
        if dst is None:
            return
        r = self.nb if t < 2 else b
        src, b_src = self.xrows(li, b, t)
        S.dma('sp', c['xr'][:], src, reads=[b_src], writes=[c['b_xr']])
        for n in range(2):
            for k in range(16):
                S.op('pe', lambda e, n=n, k=k: e.matmul(c['py'][n][:], lhsT=lhs_fn(k), rhs=wout[:, k, n * 512:(n + 1) * 512],
                                                        start=(k == 0), stop=(k == 15)),
                     reads=[b_act, b_wout], writes=[c['b_py']])
        for n in range(2):
            S.op('dve', lambda e, n=n: e.tensor_tensor(out=c['yg'][:, n * 512:(n + 1) * 512], in0=c['py'][n][:],
                                                       in1=m['gt'][:, r, n * 512:(n + 1) * 512], op=ALU.mult),
                 reads=[c['b_py'], m['buf']], writes=[c['b_yg']])
        S.op('dve', lambda e: e.scalar_tensor_tensor(out=c['rr'][:], in0=c['xr'][:], scalar=ALPHA, in1=c['yg'][:],
                                                     op0=ALU.mult, op1=ALU.add),
             reads=[c['b_xr'], c['b_yg']], writes=[c['b_rr']])
        for n in range(2):
            S.op('dve', lambda e, n=n: e.bn_stats(out=c['stt'][:, n, :], in_=c['rr'][:, n * 512:(n + 1) * 512]),
                 reads=[c['b_rr']], writes=[c['b_stt']])
        S.op('dve', lambda e: e.bn_aggr(out=c['mv'][:, 0:2], in_=c['stt'][:].rearrange("p a s -> p (a s)")),
             reads=[c['b_stt']], writes=[c['b_mv']])
        S.op('act', lambda e: e.activation(out=c['mv'][:, 2:3], in_=c['mv'][:, 1:2], func=AF.Sqrt, bias=EPS, scale=1.0),
             reads=[c['b_mv']], writes=[c['b_mv']])
        S.op('dve', lambda e: e.reciprocal(out=c['mv'][:, 2:3], in_=c['mv'][:, 2:3]), reads=[c['b_mv']], writes=[c['b_mv']])
        S.op('dve', lambda e: e.scalar_tensor_tensor(out=c['mv'][:, 3:4], in0=c['mv'][:, 0:1], scalar=-1.0, in1=c['mv'][:, 2:3],
                                                     op0=ALU.mult, op1=ALU.mult),
             reads=[c['b_mv']], writes=[c['b_mv']])
        S.op('act', lambda e: e.activation(out=c['xn'][:], in_=c['rr'][:], func=AF.Identity, scale=c['mv'][:, 2:3],
                                           bias=c['mv'][:, 3:4]),
             reads=[c['b_rr'], c['b_mv']], writes=[c['b_xn']])
        S.op('pool', lambda e: e.tensor_tensor(out=c['xn'][:], in0=c['xn'][:], in1=m['lng'][:], op=ALU.mult),
             reads=[c['b_xn'], m['buf']], writes=[c['b_xn']])
        S.op('pool', lambda e: e.tensor_tensor(out=c['xn'][:], in0=c['xn'][:], in1=m['lnb'][:], op=ALU.add),
             reads=[c['b_xn'], m['buf']], writes=[c['b_xn']])
        S.dma('act', dst, c['xn'][:], reads=[c['b_xn']], writes=[b_dst])

    def layer_conv(self, li, i, last):
        S = self.S
        nb = self.nb
        j = i // 3
        with contextlib.ExitStack() as st:
            m = self.modulation(st, i)
            winb = self.dram('cvwinb', (16, 1024, 384), BF16)
            b_winb = Buf('cvwinb')
            srcv = self.w['cv_w_in'][j].rearrange("r (s k c) -> k r s c", s=3, k=16, c=128)
            for k in range(16):
                S.dma('pool', winb[k].rearrange("r (s c) -> r s c", s=3), srcv[k], reads=[self.b_in], writes=[b_winb])
            woutb, b_woutb = self.cast_weight('cv_w_out', j, 2048, 1024)
            wout = self.sb(st, 'wout', (128, 16, 1024), BF16)
            b_wout = Buf('wout')
            S.dma('sp', wout[:], woutb.rearrange("(c p) n -> p c n", p=128), reads=[b_woutb], writes=[b_wout])
            dwT = self.sb(st, 'dwT', (128, 16, 31), F32)
            pp = self.sb(st, 'cvp', (128, 3, 16), F32)
            b_par = Buf('cvpar')
            with contextlib.ExitStack() as st2:
                dwn = self.sb(st2, 'dwn', (31, 2048), F32)
                prow = self.sb(st2, 'prow', (3, 2048), F32)
                pT = self.ps(st2, 'pT')
                b_dwn, b_pT = Buf('dwn'), Buf('pT')
                S.dma('sp', dwn[:], self.w['cv_dw'][j], reads=[self.b_in], writes=[b_dwn])
                S.dma('sp', prow[0:1, :], self.w['cv_dw_b'][j:j + 1, :], reads=[self.b_in], writes=[b_dwn])
                S.dma('sp', prow[1:2, :], self.w['cv_ln_g'][j:j + 1, :], reads=[self.b_in], writes=[b_dwn])
                S.dma('sp', prow[2:3, :], self.w['cv_ln_b'][j:j + 1, :], reads=[self.b_in], writes=[b_dwn])
                for k in range(16):
                    S.op('pe', lambda e, k=k: e.transpose(pT[:, 0:31], dwn[:, k * 128:(k + 1) * 128], self.ident[0:31, 0:31]),
                         reads=[b_dwn, self.b_const], writes=[b_pT])
                    S.op('pe', lambda e, k=k: e.transpose(pT[:, 32:35], prow[:, k * 128:(k + 1) * 128], self.ident[0:3, 0:3]),
                         reads=[b_dwn, self.b_const], writes=[b_pT])
                    S.op('dve', lambda e, k=k: e.tensor_copy(out=dwT[:, k, :], in_=pT[:, 0:31]), reads=[b_pT], writes=[b_par])
                    S.op('dve', lambda e, k=k: e.tensor_copy(out=pp[:, :, k], in_=pT[:, 32:35]), reads=[b_pT], writes=[b_par])
                S.flush()
            dgscr = self.dram('dgscr', (16, 128, 31 * 128), BF16)
            b_dgscr = Buf('dgscr')
            with contextlib.ExitStack() as st2:
                dgt = [self.sb(st2, 'dgt', (128, 31, 128), BF16) for _ in range(2)]
                b_dgt = [Buf('dgt0'), Buf('dgt1')]
                for k in range(16):
                    d_, bd_ = dgt[k % 2], b_dgt[k % 2]
                    for tap in range(31):
                        if tap % 2 == 0:
                            S.op('act', lambda e, tap=tap, k=k, d_=d_: e.activation(out=d_[:, tap, :], in_=self.identb[:], func=AF.Copy,
                                                                                   scale=dwT[:, k, tap:tap + 1]),
                                 reads=[self.b_const, b_par], writes=[bd_])
                        else:
                            S.op('dve', lambda e, tap=tap, k=k, d_=d_: e.tensor_scalar(out=d_[:, tap, :], in0=self.identb[:],
                                                                                      scalar1=dwT[:, k, tap:tap + 1], scalar2=None,
                                                                                      op0=ALU.mult),
                                 reads=[self.b_const, b_par], writes=[bd_])
                    S.dma('sp', dgscr[k], d_[:].rearrange("p t c -> p (t c)"), reads=[bd_], writes=[b_dgscr])
                S.flush()
            self.dgscr, self.b_dgscr = dgscr, b_dgscr
            for b in range(nb):
                with contextlib.ExitStack() as st3:
                    hT = self.sb(st3, 'hT', (128, 8, EXT), BF16)
                    b_hT = Buf('hT')
                    self.phase_a(st3, li, b, m, hT, b_hT)
                    self.conv_body(st3, li, j, b, m, hT, b_hT, winb, b_winb, wout, b_wout, dwT, pp, b_par, last)

    def conv_body(self, st, li, j, b, m, hT, b_hT, winb, b_winb, wout, b_wout, dwT, pp, b_par, last):
        S = self.S
        TB = 256
        W = TB + 30
        wk = [self.sb(st, 'wk', (128, 8, 384), BF16) for _ in range(2)]
        b_wk = [Buf('wk0'), Buf('wk1')]
        vT = [self.sb(st, 'vT', (128, W), BF16) for _ in range(2)]
        b_vT = [Buf('vT0'), Buf('vT1')]
        sig = self.sb(st, 'sig', (128, W), F32)
        b_sig = Buf('sig')
        dg = [self.sb(st, 'dg', (128, 31, 128), BF16) for _ in range(2)]
        b_dg = [Buf('dg0'), Buf('dg1')]
        cT = self.sb(st, 'cT', (128, 16, TB), BF16)
        b_cT = Buf('cT')
        sg = self.sb(st, 'sg', (128, 16, TB), BF16)
        b_sg = Buf('sg')
        csq = self.sb(st, 'csq', (128, TB), BF16)
        b_csq = Buf('csq')
        mean = self.sb(st, 'mean', (128, TB), F32)
        rstd = self.sb(st, 'rstd', (128, TB), F32)
        b_st = Buf('stat')
        z1 = self.sb(st, 'z1', (128, TB), F32)
        z2 = self.sb(st, 'z2', (128, TB), F32)
        z3 = self.sb(st, 'z3', (128, TB), BF16)
        b_z1, b_z2, b_z3 = Buf('z1'), Buf('z2'), Buf('z3')
        pA, pB, pC, pG, pS, pQ = [self.ps(st, n) for n in 'pA pB pC pG pS pQ'.split()]
        b_pA, b_pB, b_pC, b_pG, b_pS, b_pQ = [Buf(n) for n in 'pA pB pC pG pS pQ'.split()]
        c = self.alloc_c(st)
        blocks = [(0, CTX, 0)] + [(CTX, EXT, CTX + q * TB) for q in range(SEQ // TB)]
        cnt = 0
        def do_block(s0, s1, t0, cnt0):
            vlo, vhi = max(s0, t0 - 15), min(s1, t0 + TB + 15)
            off = vlo - (t0 - 15)
            Wv = vhi - vlo
            def do_chunk(k, cnt):
                w_, bw_ = wk[cnt % 2], b_wk[cnt % 2]
                v_, bv_ = vT[cnt % 2], b_vT[cnt % 2]
                d_, bd_ = dg[cnt % 2], b_dg[cnt % 2]
                S.dma('sp' if k % 2 == 0 else 'act', w_[:], winb[k].rearrange("(c p) n -> p c n", p=128),
                      reads=[b_winb], writes=[bw_])
                for kk in range(8):
                    S.op('pe', lambda e, kk=kk, w_=w_: e.matmul(pB[:, 0:Wv], lhsT=w_[:, kk, 128:256], rhs=hT[:, kk, vlo:vhi],
                                                              start=(kk == 0), stop=(kk == 7)),
                         reads=[bw_, b_hT], writes=[b_pB])
                for kk in range(8):
                    S.op('pe', lambda e, kk=kk, w_=w_: e.matmul(pA[:, 0:Wv], lhsT=w_[:, kk, 0:128], rhs=hT[:, kk, vlo:vhi],
                                                              start=(kk == 0), stop=(kk == 7)),
                         reads=[bw_, b_hT], writes=[b_pA])
                for kk in range(8):
                    S.op('pe', lambda e, kk=kk, w_=w_: e.matmul(pG[:, 0:TB], lhsT=w_[:, kk, 256:384], rhs=hT[:, kk, t0:t0 + TB],
                                                              start=(kk == 0), stop=(kk == 7)),
                         reads=[bw_, b_hT], writes=[b_pG])
                S.op('act', lambda e: e.activation(out=sig[:, 0:Wv], in_=pB[:, 0:Wv], func=AF.Sigmoid),
                     reads=[b_pB], writes=[b_sig])
                if Wv < W:
                    S.op('pool', lambda e, v_=v_: e.memset(v_[:], 0.0), writes=[bv_])
                S.op('dve', lambda e, v_=v_: e.tensor_tensor(out=v_[:, off:off + Wv], in0=pA[:, 0:Wv], in1=sig[:, 0:Wv], op=ALU.mult),
                     reads=[b_pA, b_sig], writes=[bv_])
                S.op('act', lambda e, k=k: e.activation(out=sg[:, k, :], in_=pG[:, 0:TB], func=AF.Silu),
                     reads=[b_pG], writes=[b_sg])
                S.dma('sp', d_[:].rearrange("p t c -> p (t c)"), self.dgscr[k], reads=[self.b_dgscr], writes=[bd_])
                for tap in range(31):
                    S.op('pe', lambda e, tap=tap, d_=d_, v_=v_: e.matmul(pC[:, 0:TB], lhsT=d_[:, tap, :], rhs=v_[:, tap:tap + TB],
                                                                       start=(tap == 0), stop=(tap == 30)),
                         reads=[bd_, bv_], writes=[b_pC])
                S.op('act', lambda e, k=k: e.activation(out=cT[:, k, :], in_=pC[:, 0:TB], func=AF.Identity, bias=pp[:, 0, k:k + 1], scale=1.0),
                     reads=[b_pC, b_par], writes=[b_cT])
                S.op('act', lambda e, k=k: e.activation(out=csq[:], in_=pC[:, 0:TB], func=AF.Square, bias=pp[:, 0, k:k + 1], scale=1.0),
                     reads=[b_pC, b_par], writes=[b_csq])
                S.op('pe', lambda e, k=k: e.matmul(pS[:, 0:TB], lhsT=self.onesb[:], rhs=cT[:, k, :], start=(k == 0), stop=(k == 15)),
                     reads=[b_cT, self.b_const], writes=[b_pS])
                S.op('pe', lambda e, k=k: e.matmul(pQ[:, 0:TB], lhsT=self.onesb[:], rhs=csq[:], start=(k == 0), stop=(k == 15)),
                     reads=[b_csq, self.b_const], writes=[b_pQ])
            for k in range(16):
                do_chunk(k, cnt0 + k)
            S.op('act', lambda e: e.activation(out=mean[:], in_=pS[:, 0:TB], func=AF.Copy, scale=1.0 / E), reads=[b_pS], writes=[b_st])
            S.op('dve', lambda e: e.tensor_tensor(out=z1[:], in0=mean[:], in1=mean[:], op=ALU.mult), reads=[b_st], writes=[b_z1])
            S.op('dve', lambda e: e.scalar_tensor_tensor(out=z2[:], in0=pQ[:, 0:TB], scalar=1.0 / E, in1=z1[:], op0=ALU.mult, op1=ALU.subtract),
                 reads=[b_pQ, b_z1], writes=[b_z2])
            S.op('act', lambda e: e.activation(out=z1[:], in_=z2[:], func=AF.Sqrt, bias=EPS, scale=1.0), reads=[b_z2], writes=[b_z1])
            S.op('dve', lambda e: e.reciprocal(out=rstd[:], in_=z1[:]), reads=[b_z1], writes=[b_st])
            for k in range(16):
                S.op('dve', lambda e, k=k: e.tensor_tensor(out=z1[:], in0=cT[:, k, :], in1=mean[:], op=ALU.subtract),
                     reads=[b_cT, b_st], writes=[b_z1])
                S.op('pool', lambda e: e.tensor_tensor(out=z2[:], in0=z1[:], in1=rstd[:], op=ALU.mult),
                     reads=[b_z1, b_st], writes=[b_z2])
                S.op('act', lambda e, k=k: e.activation(out=z3[:], in_=z2[:], func=AF.Silu, scale=pp[:, 1, k:k + 1], bias=pp[:, 2, k:k + 1]),
                     reads=[b_z2, b_par], writes=[b_z3])
                S.op('dve', lambda e, k=k: e.tensor_tensor(out=cT[:, k, :], in0=z3[:], in1=sg[:, k, :], op=ALU.mult),
                     reads=[b_z3, b_sg], writes=[b_cT])
            for tt in range(TB // 128):
                t = t0 // 128 + tt
                self.phase_c_tile(c, li, b, t, m, (lambda k, tt=tt: cT[:, k, tt * 128:(tt + 1) * 128]), b_cT, wout, b_wout, last)
        for bi, (s0, s1, t0) in enumerate(blocks):
            do_block(s0, s1, t0, bi * 16)
        S.flush()

    def load_cols(self, dst, srcs, b_dst):
        S = self.S
        n = len(srcs)
        with contextlib.ExitStack() as st2:
            rows = self.sb(st2, 'lcrow', (n, 128), F32)
            pT = self.ps(st2, 'lcp')
            b_rows, b_pT = Buf('lcrow'), Buf('lcp')
            for q, src in enumerate(srcs):
                S.dma('sp', rows[q:q + 1, :], src, reads=[self.b_in], writes=[b_rows])
            S.op('pe', lambda e: e.transpose(pT[:, 0:n], rows[:, :], self.ident[0:n, 0:n]), reads=[b_rows, self.b_const], writes=[b_pT])
            S.op('dve', lambda e: e.tensor_copy(out=dst, in_=pT[:, 0:n]), reads=[b_pT], writes=[b_dst])
            S.flush()

    def layer_mla(self, li, i, last):
        S = self.S
        nb = self.nb
        j = i // 3
        with_ctx_out = not last
        SCALE = 192.0 ** -0.5
        with contextlib.ExitStack() as st:
            m = self.modulation(st, i)
            winb, b_winb = self.cast_weight('mla_w_in', j, 1024, 2496)
            wuqb, b_wuqb = self.cast_weight('mla_w_uq', j, 256, 3072)
            woutb, b_woutb = self.cast_weight('mla_w_out', j, 2048, 1024)
            b_w = Buf('mlaw')
            wuq = self.sb(st, 'wuq', (128, 2, 3072), BF16)
            wuqr = self.sb(st, 'wuqr', (128, 2, 16, 64), BF16)
            wukT = self.sb(st, 'wukT', (128, 16, 128), BF16)
            wuv = self.sb(st, 'wuv', (128, 16, 128), BF16)
            nrm = self.sb(st, 'nrm', (128, 3), F32)
            rope = self.sb(st, 'rope', (64, 2, SEQ), BF16)
            onesf = self.sb(st, 'onesf', (128, 128), F32)
            S.op('pool', lambda e: e.memset(onesf[:], 1.0), writes=[b_w])
            S.dma('act', wuq[:], wuqb.rearrange("(c p) n -> p c n", p=128), reads=[b_wuqb], writes=[b_w])
            S.dma('pool', rope[:], self.rope_in[:, :, :], reads=[self.b_in], writes=[b_w])
            self.load_cols(nrm[:, :], [self.w['mla_q_norm'][j:j + 1, 0:128], self.w['mla_q_norm'][j:j + 1, 128:256],
                                       self.w['mla_kv_norm'][j:j + 1, :]], b_w)
            for g4, (srcg, sign) in enumerate([(1, -1.0), (0, 1.0), (3, -1.0), (2, 1.0)]):
                wq4 = wuq[:].rearrange("p c (h f) -> p c h f", f=192)
                S.op('pool', lambda e, g4=g4, srcg=srcg, sign=sign, wq4=wq4: e.tensor_scalar(
                    out=wuqr[:, :, :, g4 * 16:(g4 + 1) * 16], in0=wq4[:, :, :, 128 + srcg * 16:128 + (srcg + 1) * 16],
                    scalar1=sign, scalar2=None, op0=ALU.mult), reads=[b_w], writes=[b_w])
            with contextlib.ExitStack() as st2:
                wf = self.sb(st2, 'wukf', (128, 16, 128), F32)
                wf2 = self.sb(st2, 'wuvf', (128, 16, 128), F32)
                pT = self.ps(st2, 'pT')
                b_wf, b_pT = Buf('wf'), Buf('pT')
                S.dma('sp', wf[:], self.w['mla_w_uk'][j], reads=[self.b_in], writes=[b_wf])
                S.dma('act', wf2[:], self.w['mla_w_uv'][j], reads=[self.b_in], writes=[b_wf])
                S.op('dve', lambda e: e.tensor_copy(out=wuv[:], in_=wf2[:]), reads=[b_wf], writes=[b_w])
                for h in range(16):
                    S.op('pe', lambda e, h=h: e.transpose(pT[:, (h % 4) * 128:(h % 4 + 1) * 128], wf[:, h, :], self.ident[:]),
                         reads=[b_wf, self.b_const], writes=[b_pT])
                    if h % 4 == 3:
                        S.op('dve', lambda e, h=h: e.tensor_copy(out=wukT[:, h - 3:h + 1, :], in_=pT[:].rearrange("p (a c) -> p a c", a=4)),
                             reads=[b_pT], writes=[b_w])
                S.flush()
            W = dict(woutb=woutb, b_woutb=b_woutb, wuq=wuq, wuqr=wuqr, wukT=wukT, wuv=wuv, nrm=nrm, rope=rope, onesf=onesf, b_w=b_w,
                     winb=winb, b_winb=b_winb)
            for b in range(nb):
                if KSTOP == 'prep':
                    break
                with contextlib.ExitStack() as st3:
                    qnT = self.sb(st3, 'qnT', (128, 2, EXT), BF16)
                    KT = self.sb(st3, 'KT', (128, EXT), BF16)
                    KTr = self.sb(st3, 'KTr', (64, EXT), BF16)
                    V = self.sb(st3, 'V', (128, NT, 128), BF16)
                    gscr = self.dram('gscr', (128, 16, EXT), BF16)
                    A = dict(qnT=qnT, KT=KT, KTr=KTr, V=V, gscr=gscr, b_qk=Buf('qk'), b_gscr=Buf('gscr'))
                    with contextlib.ExitStack() as st4:
                        hT = self.sb(st4, 'hT', (128, 8, EXT), BF16)
                        b_hT = Buf('hT')
                        self.phase_a(st4, li, b, m, hT, b_hT)
                        self.mla_b0(st4, hT, b_hT, W, A, with_ctx_out)
                    if KSTOP != 'b0':
                        self.mla_att(st3, li, b, m, W, A, with_ctx_out, last, SCALE)

    def mla_b0(self, st, hT, b_hT, W, A, with_ctx_out):
        S = self.S
        nrm, rope, b_w0 = W['nrm'], W['rope'], W['b_w']
        qnT, KT, KTr, V, b_qk = A['qnT'], A['KT'], A['KTr'], A['V'], A['b_qk']
        wA = self.sb(st, 'wA', (128, 8, 512), BF16)
        b_wA = Buf('wA')
        S.dma('sp', wA[:, :, 0:448], W['winb'][:, 0:448].rearrange("(c p) n -> p c n", p=128), reads=[W['b_winb']], writes=[b_wA])
        for g4, (srcg, sign) in enumerate([(1, -1.0), (0, 1.0), (3, -1.0), (2, 1.0)]):
            S.op('dve', lambda e, g4=g4, srcg=srcg, sign=sign: e.tensor_scalar(
                out=wA[:, :, 448 + g4 * 16:448 + (g4 + 1) * 16], in0=wA[:, :, 384 + srcg * 16:384 + (srcg + 1) * 16],
                scalar1=sign, scalar2=None, op0=ALU.mult), reads=[b_wA], writes=[b_wA])
        b_w = b_wA
        pq = [self.ps(st, 'pq') for _ in range(2)]
        pkv, pkr, pkrr, pss, pssk = [self.ps(st, n) for n in 'pkv pkr pkrr pss pssk'.split()]
        pvt = self.ps(st, 'pvt', (128, 512), BF16)
        b_pq, b_pkv, b_pkr, b_pkrr, b_pss, b_pssk, b_pvt = [Buf(n) for n in 'pq pkv pkr pkrr pss pssk pvt'.split()]
        sq = self.sb(st, 'sq', (128, 3, 512), BF16)
        rs = self.sb(st, 'rs', (128, 2, 512), F32)
        t1 = self.sb(st, 't1', (64, 512), F32)
        t2 = self.sb(st, 't2', (64, 512), F32)
        b_sq, b_rs, b_t1, b_t2 = Buf('sq'), Buf('rs'), Buf('t1'), Buf('t2')

        def block(t0, n):
            for c in range(2):
                for kk in range(8):
                    S.op('pe', lambda e, c=c, kk=kk: e.matmul(pq[c][:, 0:n], lhsT=wA[:, kk, c * 128:(c + 1) * 128], rhs=hT[:, kk, t0:t0 + n],
                                                              start=(kk == 0), stop=(kk == 7)), reads=[b_w, b_hT], writes=[b_pq])
            for kk in range(8):
                S.op('pe', lambda e, kk=kk: e.matmul(pkv[:, 0:n], lhsT=wA[:, kk, 256:384], rhs=hT[:, kk, t0:t0 + n],
                                                     start=(kk == 0), stop=(kk == 7)), reads=[b_w, b_hT], writes=[b_pkv])
            for kk in range(8):
                S.op('pe', lambda e, kk=kk: e.matmul(pkr[0:64, 0:n], lhsT=wA[:, kk, 384:448], rhs=hT[:, kk, t0:t0 + n],
                                                     start=(kk == 0), stop=(kk == 7)), reads=[b_w, b_hT], writes=[b_pkr])
            lat = t0 >= CTX
            if lat:
                for kk in range(8):
                    S.op('pe', lambda e, kk=kk: e.matmul(pkrr[0:64, 0:n], lhsT=wA[:, kk, 448:512], rhs=hT[:, kk, t0:t0 + n],
                                                         start=(kk == 0), stop=(kk == 7)), reads=[b_w, b_hT], writes=[b_pkrr])
            for c in range(2):
                S.op('act', lambda e, c=c: e.activation(out=sq[:, c, 0:n], in_=pq[c][:, 0:n], func=AF.Square), reads=[b_pq], writes=[b_sq])
            S.op('act', lambda e: e.activation(out=sq[:, 2, 0:n], in_=pkv[:, 0:n], func=AF.Square), reads=[b_pkv], writes=[b_sq])
            for c in range(2):
                S.op('pe', lambda e, c=c: e.matmul(pss[:, 0:n], lhsT=self.onesb[:], rhs=sq[:, c, 0:n], start=(c == 0), stop=(c == 1)),
                     reads=[b_sq, self.b_const], writes=[b_pss])
            S.op('pe', lambda e: e.matmul(pssk[:, 0:n], lhsT=self.onesb[:], rhs=sq[:, 2, 0:n], start=True, stop=True),
                 reads=[b_sq, self.b_const], writes=[b_pssk])
            S.op('act', lambda e: e.activation(out=rs[:, 0, 0:n], in_=pss[:, 0:n], func=AF.Sqrt, scale=1.0 / 256, bias=EPS), reads=[b_pss], writes=[b_rs])
            S.op('act', lambda e: e.activation(out=rs[:, 1, 0:n], in_=pssk[:, 0:n], func=AF.Sqrt, scale=1.0 / 128, bias=EPS), reads=[b_pssk], writes=[b_rs])
            S.op('dve', lambda e: e.reciprocal(out=rs[:, :, 0:n], in_=rs[:, :, 0:n]), reads=[b_rs], writes=[b_rs])
            for c in range(2):
                S.op('dve', lambda e, c=c: e.scalar_tensor_tensor(out=qnT[:, c, t0:t0 + n], in0=pq[c][:, 0:n], scalar=nrm[:, c:c + 1], in1=rs[:, 0, 0:n],
                                                                  op0=ALU.mult, op1=ALU.mult), reads=[b_pq, b_rs, b_w0], writes=[b_qk])
            S.op('dve', lambda e: e.scalar_tensor_tensor(out=KT[:, t0:t0 + n], in0=pkv[:, 0:n], scalar=nrm[:, 2:3], in1=rs[:, 1, 0:n],
                                                         op0=ALU.mult, op1=ALU.mult), reads=[b_pkv, b_rs, b_w0], writes=[b_qk])
            if lat:
                l0 = t0 - CTX
                S.op('dve', lambda e: e.tensor_tensor(out=t1[:, 0:n], in0=pkr[0:64, 0:n], in1=rope[:, 0, l0:l0 + n], op=ALU.mult),
                     reads=[b_pkr, b_w0], writes=[b_t1])
                S.op('dve', lambda e: e.tensor_tensor(out=t2[:, 0:n], in0=pkrr[0:64, 0:n], in1=rope[:, 1, l0:l0 + n], op=ALU.mult),
                     reads=[b_pkrr, b_w0], writes=[b_t2])
                S.op('pool', lambda e: e.tensor_tensor(out=KTr[:, t0:t0 + n], in0=t1[:, 0:n], in1=t2[:, 0:n], op=ALU.add),
                     reads=[b_t1, b_t2], writes=[b_qk])
            else:
                S.op('act', lambda e: e.activation(out=KTr[:, t0:t0 + n], in_=pkr[0:64, 0:n], func=AF.Copy), reads=[b_pkr], writes=[b_qk])
            for q in range(n // 128):
                t = t0 // 128 + q
                S.op('pe', lambda e, t=t, q=q: e.transpose(pvt[:, q * 128:(q + 1) * 128], KT[:, t * 128:(t + 1) * 128], self.identb[:]),
                     reads=[b_qk, self.b_const], writes=[b_pvt])
            S.op('act', lambda e: e.activation(out=V[:, t0 // 128:t0 // 128 + n // 128, :],
                                               in_=pvt[:, 0:n].rearrange("p (a c) -> p a c", c=128), func=AF.Copy),
                 reads=[b_pvt], writes=[b_qk])

        block(0, CTX)
        for q in range(SEQ // 512):
            block(CTX + q * 512, 512)
        wg = [self.sb(st, 'wg', (128, 8, 128), BF16) for _ in range(2)]
        b_wg = [Buf('wg0'), Buf('wg1')]
        gst = [self.sb(st, 'gst', (128, 512), BF16) for _ in range(2)]
        b_gst = [Buf('gst0'), Buf('gst1')]
        pg = [pq[0], pq[1], pkv]
        b_pg = [Buf('pg0'), Buf('pg1'), Buf('pg2')]
        winb, b_winb = W['winb'], W['b_winb']
        tstart = 0 if with_ctx_out else CTX
        cnt = [0]

        def gate_chunk(k):
            w_, bw_ = wg[k % 2], b_wg[k % 2]
            S.dma('sp', w_[:], winb[:, 448 + k * 128:448 + (k + 1) * 128].rearrange("(c p) n -> p c n", p=128), reads=[b_winb], writes=[bw_])
            t0 = tstart
            while t0 < EXT:
                n = min(512, EXT - t0) if t0 >= CTX else CTX
                p_, bp_ = pg[cnt[0] % 3], b_pg[cnt[0] % 3]
                g_, bg_ = gst[cnt[0] % 2], b_gst[cnt[0] % 2]
                cnt[0] += 1
                for kk in range(8):
                    S.op('pe', lambda e, kk=kk, p_=p_, t0=t0, n=n: e.matmul(p_[:, 0:n], lhsT=w_[:, kk, :], rhs=hT[:, kk, t0:t0 + n],
                                                                         start=(kk == 0), stop=(kk == 7)), reads=[bw_, b_hT], writes=[bp_])
                S.op('act', lambda e, p_=p_, g_=g_, n=n: e.activation(out=g_[:, 0:n], in_=p_[:, 0:n], func=AF.Silu), reads=[bp_], writes=[bg_])
                S.dma('act', A['gscr'][:, k, t0:t0 + n], g_[:, 0:n], reads=[bg_], writes=[A['b_gscr']])
                t0 += n
        for k in range(16):
            gate_chunk(k)
        S.flush()

    def mla_att(self, st, li, b, m, W, A, with_ctx_out, last, SCALE):
        S = self.S
        wuq, wuqr, wukT, wuv, rope, onesf, b_w = [W[n] for n in 'wuq wuqr wukT wuv rope onesf b_w'.split()]
        wout = self.sb(st, 'wout', (128, 16, 1024), BF16)
        b_wout = Buf('wout')
        S.dma('sp', wout[:], W['woutb'].rearrange("(c p) n -> p c n", p=128), reads=[W['b_woutb']], writes=[b_wout])
        qnT, KT, KTr, V, gscr, b_qk, b_gscr = [A[n] for n in 'qnT KT KTr V gscr b_qk b_gscr'.split()]
        c = self.alloc_c(st)
        pS = [self.ps(st, 'pS') for _ in range(2)]
        b_pS = [Buf('pS%d' % q) for q in range(2)]
        pR = self.ps(st, 'pR')
        b_pR = Buf('pR')
        pO = [self.ps(st, 'pO') for _ in range(2)]
        b_pO = [Buf('pO0'), Buf('pO1')]
        pM = self.ps(st, 'pM')
        b_pM = Buf('pM')
        gh = [self.sb(st, 'gh', (128, 512), BF16) for _ in range(2)]
        b_gh = [Buf('gh0'), Buf('gh1')]
        actT = self.sb(st, 'actT', (128, 16, 512), BF16)
        b_actT = Buf('actT')
        qh = self.sb(st, 'qh', (128, 512), BF16)
        Qa = [self.sb(st, 'Qa', (128, 512), BF16) for _ in range(2)]
        Qf = [self.sb(st, 'Qf', (64, 512), BF16) for _ in range(2)]
        Qp = [self.sb(st, 'Qp', (64, 512), BF16) for _ in range(2)]
        b_qh = Buf('qh')
        b_Q = [Buf('Q0'), Buf('Q1')]
        t1 = self.sb(st, 'qt1', (64, 512), F32)
        t2 = self.sb(st, 'qt2', (64, 512), F32)
        b_t1, b_t2 = Buf('qt1'), Buf('qt2')
        PT = [self.sb(st, 'PT', (128, 512), BF16) for _ in range(4)]
        b_PT = [Buf('PT%d' % q) for q in range(4)]
        acc = [self.sb(st, 'acc', (128, 512), F32) for _ in range(2)]
        b_acc = [Buf('acc0'), Buf('acc1')]
        rcp = self.sb(st, 'rcp', (128, 512), F32)
        b_rcp = Buf('rcp')
        oT = self.sb(st, 'oT', (128, 512), BF16)
        b_oT = Buf('oT')
        vt = self.sb(st, 'vt', (128, 512), F32)
        b_vt = Buf('vt')
        cnt = [0]

        def head(h, t0, n, kchunks, hi):
            lat = t0 >= CTX
            Qa_, Qf_, Qp_, bQ_ = Qa[hi % 2], Qf[hi % 2], Qp[hi % 2], b_Q[hi % 2]
            pO_, bpO_ = pO[hi % 2], b_pO[hi % 2]
            gh_, bgh_ = gh[hi % 2], b_gh[hi % 2]
            S.dma('sp', gh_[:, 0:n], gscr[:, h, t0:t0 + n], reads=[b_gscr], writes=[bgh_])
            if KATT < 1:
                return
            for kc in range(2):
                S.op('pe', lambda e, kc=kc: e.matmul(pM[:, 0:n], lhsT=wuq[:, kc, h * 192:h * 192 + 128], rhs=qnT[:, kc, t0:t0 + n],
                                                     start=(kc == 0), stop=(kc == 1)), reads=[b_w, b_qk], writes=[b_pM])
            S.op('act', lambda e: e.activation(out=qh[:, 0:n], in_=pM[:, 0:n], func=AF.Copy), reads=[b_pM], writes=[b_qh])
            S.op('pe', lambda e: e.matmul(pM[:, 0:n], lhsT=wukT[:, h, :], rhs=qh[:, 0:n], start=True, stop=True),
                 reads=[b_w, b_qh], writes=[b_pM])
            S.op('dve', lambda e: e.tensor_copy(out=Qa_[:, 0:n], in_=pM[:, 0:n]), reads=[b_pM], writes=[bQ_])
            if KATT == 1 and os.environ.get('KSUB', '') == 'a':
                return
            for kc in range(2):
                S.op('pe', lambda e, kc=kc: e.matmul(pR[0:64, 0:n], lhsT=wuq[:, kc, h * 192 + 128:h * 192 + 192], rhs=qnT[:, kc, t0:t0 + n],
                                                     start=(kc == 0), stop=(kc == 1)), reads=[b_w, b_qk], writes=[b_pR])
            S.op('act', lambda e: e.activation(out=Qf_[:, 0:n], in_=pR[0:64, 0:n], func=AF.Copy), reads=[b_pR], writes=[bQ_])
            KSUB = os.environ.get('KSUB', '')
            if KATT == 1 and KSUB == 'b':
                return
            if lat:
                l0 = t0 - CTX
                S.op('dve', lambda e: e.tensor_tensor(out=t1[:, 0:n], in0=pR[0:64, 0:n], in1=rope[:, 0, l0:l0 + n], op=ALU.mult),
                     reads=[b_pR, b_w], writes=[b_t1])
                if KATT == 1 and KSUB == 'c':
                    return
                for kc in range(2):
                    S.op('pe', lambda e, kc=kc: e.matmul(pR[0:64, 0:n], lhsT=wuqr[:, kc, h, :], rhs=qnT[:, kc, t0:t0 + n],
                                                         start=(kc == 0), stop=(kc == 1)), reads=[b_w, b_qk], writes=[b_pR])
                if KATT == 1 and KSUB == 'd':
                    return
                S.op('dve', lambda e: e.tensor_tensor(out=t2[:, 0:n], in0=pR[0:64, 0:n], in1=rope[:, 1, l0:l0 + n], op=ALU.mult),
                     reads=[b_pR, b_w], writes=[b_t2])
                S.op('pool', lambda e: e.tensor_tensor(out=Qp_[:, 0:n], in0=t1[:, 0:n], in1=t2[:, 0:n], op=ALU.add),
                     reads=[b_t1, b_t2], writes=[bQ_])
            nk = len(kchunks)
            if KATT < 2:
                return
            for qi, kt in enumerate(kchunks):
                ci = cnt[0]
                cnt[0] += 1
                pS_, bpS_ = pS[ci % 2], b_pS[ci % 2]
                PT_, bPT_ = PT[ci % 4], b_PT[ci % 4]
                Qr_ = Qf_ if (kt < 2 or not lat) else Qp_
                S.op('pe', lambda e, kt=kt, pS_=pS_: e.matmul(pS_[:, 0:n], lhsT=KT[:, kt * 128:(kt + 1) * 128], rhs=Qa_[:, 0:n], start=True, stop=False),
                     reads=[b_qk, bQ_], writes=[bpS_])
                S.op('pe', lambda e, kt=kt, pS_=pS_, Qr_=Qr_: e.matmul(pS_[:, 0:n], lhsT=KTr[:, kt * 128:(kt + 1) * 128], rhs=Qr_[:, 0:n], start=False, stop=True),
                     reads=[b_qk, bQ_], writes=[bpS_])
                S.op('act', lambda e, pS_=pS_, PT_=PT_: e.activation(out=PT_[:, 0:n], in_=pS_[:, 0:n], func=AF.Exp, scale=SCALE),
                     reads=[bpS_], writes=[bPT_])
                S.op('pe', lambda e, kt=kt, PT_=PT_, qi=qi: e.matmul(pO_[:, 0:n], lhsT=V[:, kt, :], rhs=PT_[:, 0:n], start=(qi == 0), stop=(qi == nk - 1)),
                     reads=[b_qk, bPT_], writes=[bpO_])
                S.op('pe', lambda e, PT_=PT_, qi=qi: e.matmul(pM[:, 0:n], lhsT=self.onesb[:], rhs=PT_[:, 0:n], start=(qi == 0), stop=(qi == nk - 1)),
                     reads=[self.b_const, bPT_], writes=[b_pM])
            if KATT < 3:
                return
            S.op('dve', lambda e: e.reciprocal(out=rcp[:, 0:n], in_=pM[:, 0:n]), reads=[b_pM], writes=[b_rcp])
            if KATT < 4:
                return
            S.op('act', lambda e: e.activation(out=oT[:, 0:n], in_=pO_[:, 0:n], func=AF.Copy), reads=[bpO_], writes=[b_oT])
            S.op('pe', lambda e: e.matmul(pM[:, 0:n], lhsT=wuv[:, h, :], rhs=oT[:, 0:n], start=True, stop=True),
                 reads=[b_w, b_oT], writes=[b_pM])
            S.op('dve', lambda e: e.tensor_tensor(out=vt[:, 0:n], in0=pM[:, 0:n], in1=rcp[:, 0:n], op=ALU.mult),
                 reads=[b_pM, b_rcp], writes=[b_vt])
            S.op('pool', lambda e: e.tensor_tensor(out=actT[:, h, 0:n], in0=vt[:, 0:n], in1=gh_[:, 0:n], op=ALU.mult),
                 reads=[b_vt, bgh_], writes=[b_actT])

        def qblock(t0, n, kchunks):
            for h in range(16):
                head(h, t0, n, kchunks, h)
            for tt in range(n // 128):
                t = t0 // 128 + tt
                self.phase_c_tile(c, li, b, t, m, (lambda k, tt=tt: actT[:, k, tt * 128:(tt + 1) * 128]), b_actT, wout, b_wout, last)

        if with_ctx_out:
            qblock(0, CTX, [0, 1])
        for q in range(SEQ // 512):
            qblock(CTX + q * 512, 512, list(range(NT)))
        S.flush()

    def layer_s5(self, li, i, last):
        S = self.S
        nb = self.nb
        j = i // 3
        G, NM = 128, EXT + 1
        TWO_PI = 2.0 * math.pi * (1.0 - 2e-7)
        HALF_PI = 0.5 * math.pi * (1.0 - 2e-7)
        with_ctx_out = not last

        def revap(t, start, n, rowlen, np_=128):
            return bass.AP(t, start, [[rowlen, np_], [-1, n]])

        with contextlib.ExitStack() as st:
            m = self.modulation(st, i)
            winb, b_winb = self.cast_weight('s5_w_in', j, 1024, 4096)
            woutb, b_woutb = self.cast_weight('s5_w_out', j, 2048, 1024)
            wglub = self.dram('wglub', (16, 2048, 256), BF16)
            b_wglub = Buf('wglub')
            srcg = self.w['s5_w_glu'][j].rearrange("r (s k c) -> k r s c", s=2, k=16, c=128)
            for k in range(16):
                S.dma('pool', wglub[k].rearrange("r (s c) -> r s c", s=2), srcg[k], reads=[self.b_in], writes=[b_wglub])
            uscr = [self.dram('uscr', (128, 16, EXT), BF16) for _ in range(nb)]
            gscr = [self.dram('gscr', (128, 16, EXT), BF16) for _ in range(nb)]
            yscr = [self.dram('yscr', (128, 16, EXT), BF16) for _ in range(nb)]
            b_uscr = [Buf('uscr%d' % b) for b in range(nb)]
            b_gscr = [Buf('gscr%d' % b) for b in range(nb)]
            b_yscr = [Buf('yscr%d' % b) for b in range(nb)]
            for b in range(nb):
                with contextlib.ExitStack() as st3:
                    hT = self.sb(st3, 'hT', (128, 8, EXT), BF16)
                    b_hT = Buf('hT')
                    self.phase_a(st3, li, b, m, hT, b_hT)
                    wg = [self.sb(st3, 'wg', (128, 8, 128), BF16) for _ in range(2)]
                    b_wg = [Buf('wg0'), Buf('wg1')]
                    gst = [self.sb(st3, 'gst', (128, 512), BF16) for _ in range(2)]
                    b_gst = [Buf('gst0'), Buf('gst1')]
                    pg = [self.ps(st3, 'pg') for _ in range(3)]
                    b_pg = [Buf('pg0'), Buf('pg1'), Buf('pg2')]
                    cnt = [0]

                    def u_chunk(k, b=b, hT=hT, b_hT=b_hT, wg=wg, b_wg=b_wg, gst=gst, b_gst=b_gst, pg=pg, b_pg=b_pg, cnt=cnt):
                        w_, bw_ = wg[k % 2], b_wg[k % 2]
                        S.dma('sp', w_[:], winb[:, k * 128:(k + 1) * 128].rearrange("(c p) n -> p c n", p=128), reads=[b_winb], writes=[bw_])
                        isg = k >= 16
                        dst, bdst = (gscr[b], b_gscr[b]) if isg else (uscr[b], b_uscr[b])
                        t0 = 0
                        while t0 < EXT:
                            n = min(512, EXT - t0) if t0 >= CTX else CTX
                            p_, bp_ = pg[cnt[0] % 3], b_pg[cnt[0] % 3]
                            g_, bg_ = gst[cnt[0] % 2], b_gst[cnt[0] % 2]
                            cnt[0] += 1
                            for kk in range(8):
                                S.op('pe', lambda e, kk=kk, p_=p_, t0=t0, n=n: e.matmul(p_[:, 0:n], lhsT=w_[:, kk, :], rhs=hT[:, kk, t0:t0 + n],
                                                                                     start=(kk == 0), stop=(kk == 7)), reads=[bw_, b_hT], writes=[bp_])
                            S.op('act', lambda e, p_=p_, g_=g_, n=n: e.activation(out=g_[:, 0:n], in_=p_[:, 0:n], func=(AF.Silu if isg else AF.Copy)),
                                 reads=[bp_], writes=[bg_])
                            S.dma('act', dst[:, k % 16, t0:t0 + n], g_[:, 0:n], reads=[bg_], writes=[bdst])
                            t0 += n
                    for k in range(32):
                        u_chunk(k)
                    S.flush()
            if KSTOP == 'u':
                return
            with contextlib.ExitStack() as st3:
                if KSTOP != 'nossm':
                    self.s5_ssm(st3, j, uscr, b_uscr, yscr, b_yscr, revap, TWO_PI, HALF_PI)
            if KSTOP in ('sprep', 'ssm1', 'ssm'):
                return
            for b in range(nb):
                with contextlib.ExitStack() as st3:
                    self.s5_glu(st3, li, j, b, m, yscr[b], b_yscr[b], gscr[b], b_gscr[b], wglub, b_wglub, woutb, b_woutb, with_ctx_out, last)

    def s5_glu(self, st, li, j, b, m, yscr, b_yscr, gscr, b_gscr, wglub, b_wglub, woutb, b_woutb, with_ctx_out, last):
        S = self.S
        wout = self.sb(st, 'wout', (128, 16, 1024), BF16)
        b_wout = Buf('wout')
        S.dma('sp', wout[:], woutb.rearrange("(c p) n -> p c n", p=128), reads=[b_woutb], writes=[b_wout])
        bg = self.sb(st, 'bglu', (128, 32), F32)
        b_bg = Buf('bglu')
        self.load_cols(bg[:, :], [self.w['s5_b_glu'][j:j + 1, q * 128:(q + 1) * 128] for q in range(32)], b_bg)
        yb = self.sb(st, 'yblk', (128, 16, 512), BF16)
        gb = self.sb(st, 'gblk', (128, 16, 512), BF16)
        actT = self.sb(st, 'actT', (128, 16, 512), BF16)
        b_yb, b_gb, b_actT = Buf('yblk'), Buf('gblk'), Buf('actT')
        wk = [self.sb(st, 'wk', (128, 16, 256), BF16) for _ in range(2)]
        b_wk = [Buf('wk0'), Buf('wk1')]
        sg = self.sb(st, 'sgl', (128, 512), F32)
        za = self.sb(st, 'zal', (128, 512), F32)
        b_sg, b_za = Buf('sgl'), Buf('zal')
        pa = [self.ps(st, 'pa') for _ in range(2)]
        pb = [self.ps(st, 'pb') for _ in range(2)]
        b_pa = [Buf('pa0'), Buf('pa1')]
        b_pb = [Buf('pb0'), Buf('pb1')]
        c = self.alloc_c(st)
        cnt = [0]

        def block(t0, n):
            S.dma('sp', yb[:, :, 0:n], yscr[:, :, t0:t0 + n], reads=[b_yscr], writes=[b_yb])
            S.dma('act', gb[:, :, 0:n], gscr[:, :, t0:t0 + n], reads=[b_gscr], writes=[b_gb])
            for k in range(16):
                ci = cnt[0]
                cnt[0] += 1
                w_, bw_ = wk[ci % 2], b_wk[ci % 2]
                pa_, bpa_, pb_, bpb_ = pa[ci % 2], b_pa[ci % 2], pb[ci % 2], b_pb[ci % 2]
                S.dma('sp' if k % 2 == 0 else 'act', w_[:], wglub[k].rearrange("(c p) n -> p c n", p=128), reads=[b_wglub], writes=[bw_])
                for kk in range(16):
                    S.op('pe', lambda e, kk=kk, w_=w_, pa_=pa_: e.matmul(pa_[:, 0:n], lhsT=w_[:, kk, 0:128], rhs=yb[:, kk, 0:n],
                                                                        start=(kk == 0), stop=(kk == 15)), reads=[bw_, b_yb], writes=[bpa_])
                for kk in range(16):
                    S.op('pe', lambda e, kk=kk, w_=w_, pb_=pb_: e.matmul(pb_[:, 0:n], lhsT=w_[:, kk, 128:256], rhs=yb[:, kk, 0:n],
                                                                        start=(kk == 0), stop=(kk == 15)), reads=[bw_, b_yb], writes=[bpb_])
                S.op('act', lambda e, k=k, pb_=pb_: e.activation(out=sg[:, 0:n], in_=pb_[:, 0:n], func=AF.Sigmoid, bias=bg[:, 16 + k:17 + k], scale=1.0),
                     reads=[bpb_, b_bg], writes=[b_sg])
                S.op('dve', lambda e, k=k, pa_=pa_: e.scalar_tensor_tensor(out=za[:, 0:n], in0=pa_[:, 0:n], scalar=bg[:, k:k + 1], in1=sg[:, 0:n],
                                                                         op0=ALU.add, op1=ALU.mult), reads=[bpa_, b_sg, b_bg], writes=[b_za])
                S.op('pool', lambda e, k=k: e.tensor_tensor(out=actT[:, k, 0:n], in0=za[:, 0:n], in1=gb[:, k, 0:n], op=ALU.mult),
                     reads=[b_za, b_gb], writes=[b_actT])
            for tt in range(n // 128):
                t = t0 // 128 + tt
                self.phase_c_tile(c, li, b, t, m, (lambda k, tt=tt: actT[:, k, tt * 128:(tt + 1) * 128]), b_actT, wout, b_wout, last)
        if with_ctx_out:
            block(0, CTX)
        for q in range(SEQ // 512):
            block(CTX + q * 512, 512)
        S.flush()

    def s5_ssm(self, st, j, uscr, b_uscr, yscr, b_yscr, revap, TWO_PI, HALF_PI):
        S = self.S
        nb = self.nb
        G, NM = 128, EXT + 1
        iot = self.sb(st, 'iot', (128, NM), I32)
        sgnX = self.sb(st, 'sgnX', (128, 1), F32)
        hpi = self.sb(st, 'hpi', (128, 1), F32)
        rmask = self.sb(st, 'rmask', (128, 8), BF16)
        dcol = self.sb(st, 'dcol', (128, 16), F32)
        b_k = Buf('s5const')
        S.op('pool', lambda e: e.iota(iot[:], pattern=[[1, NM]], base=0, channel_multiplier=0), writes=[b_k])
        S.op('pool', lambda e: e.memset(sgnX[0:64, :], 1.0), writes=[b_k])
        S.op('pool', lambda e: e.memset(sgnX[64:128, :], -1.0), writes=[b_k])
        S.op('pool', lambda e: e.memset(hpi[:], HALF_PI), writes=[b_k])
        S.op('pool', lambda e: e.memset(rmask[:], 1.0), writes=[b_k])
        S.op('pool', lambda e: e.affine_select(out=rmask[:], in_=rmask[:], pattern=[[-16, 8]], compare_op=ALU.is_ge, fill=0.0,
                                               base=0, channel_multiplier=1), reads=[b_k], writes=[b_k])
        S.op('pool', lambda e: e.affine_select(out=rmask[:], in_=rmask[:], pattern=[[16, 8]], compare_op=ALU.is_ge, fill=0.0,
                                               base=15, channel_multiplier=-1), reads=[b_k], writes=[b_k])
        self.load_cols(dcol[:, :], [self.w['s5_d'][j:j + 1, q * 128:(q + 1) * 128] for q in range(16)], b_k)
        TB1, TB2, TC1, TC2, fT, rhoT = [], [], [], [], [], []
        b_tab = Buf('s5tab')
        for d in range(2):
            TB1.append(self.sb(st, 'TB1', (128, G, 16), BF16))
            TB2.append(self.sb(st, 'TB2', (128, G, 16), BF16))
            TC1.append(self.sb(st, 'TC1', (128, G, 16), BF16))
            TC2.append(self.sb(st, 'TC2', (128, G, 16), BF16))
            fT.append(self.sb(st, 'fT', (128, G), F32))
            rhoT.append(self.sb(st, 'rhoT', (128, G), F32))
        for d in range(2):
            with contextlib.ExitStack() as st2:
                N1 = self.sb(st2, 'N1', (128, 128), F32)
                N2 = self.sb(st2, 'N2', (128, 128), F32)
                LR = self.sb(st2, 'LR', (128, G), F32)
                LI = self.sb(st2, 'LI', (128, G), F32)
                DT = self.sb(st2, 'DT', (128, G), F32)
                BA = self.sb(st2, 'BA', (128, G, 16), F32)
                BB = self.sb(st2, 'BB', (128, G, 16), F32)
                CAr = self.sb(st2, 'CAr', (128, G, 16), F32)
                CBr = self.sb(st2, 'CBr', (128, G, 16), F32)
                T = [self.sb(st2, 'T%d' % q, (128, G), F32) for q in range(8)]
                TI = self.sb(st2, 'TI', (128, G), I32)
                W1 = self.sb(st2, 'W1', (128, G, 16), F32)
                W2 = self.sb(st2, 'W2', (128, G, 16), F32)
                pT = self.ps(st2, 'pT')
                bb = Buf('s5prep')
                b_pT = Buf('pT')
                S.dma('sp', N1[:, 0:64], self.w['s5_lam_re'][j, d], reads=[self.b_in], writes=[bb])
                S.dma('sp', N1[:, 64:128], self.w['s5_lam_re'][j, d], reads=[self.b_in], writes=[bb])
                S.dma('sp', N2[:, 0:64], self.w['s5_lam_im'][j, d], reads=[self.b_in], writes=[bb])
                S.dma('sp', N2[:, 64:128], self.w['s5_lam_im'][j, d], reads=[self.b_in], writes=[bb])
                S.dma('act', DT[:], self.w['s5_log_dt'][j, d].partition_broadcast(128), reads=[self.b_in], writes=[bb])
                bsrc_re = self.w['s5_b_re'][j, d].rearrange("g p j -> p g j")
                bsrc_im = self.w['s5_b_im'][j, d].rearrange("g p j -> p g j")
                for q in range(4):
                    gs = slice(q * 32, (q + 1) * 32)
                    S.dma('sp', BA[0:64, gs, :], bsrc_re[:, gs, :], reads=[self.b_in], writes=[bb])
                    S.dma('act', BA[64:128, gs, :], bsrc_im[:, gs, :], reads=[self.b_in], writes=[bb])
                    S.dma('sp', BB[0:64, gs, :], bsrc_im[:, gs, :], reads=[self.b_in], writes=[bb])
                    S.dma('act', BB[64:128, gs, :], bsrc_re[:, gs, :], reads=[self.b_in], writes=[bb])
                for (N_, L_) in ((N1, LR), (N2, LI)):
                    S.op('pe', lambda e, N_=N_: e.transpose(pT[:, 0:128], N_[:], self.ident[:]), reads=[bb, self.b_const], writes=[b_pT])
                    S.op('dve', lambda e, L_=L_: e.tensor_copy(out=L_[:], in_=pT[:, 0:128]), reads=[b_pT], writes=[bb])
                cre = self.w['s5_c_re'][j, d].rearrange("g i p -> (g i) p")
                cim = self.w['s5_c_im'][j, d].rearrange("g i p -> (g i) p")
                for fb in range(16):
                    for (dstT, first, second) in ((CAr, cre, cim), (CBr, cim, cre)):
                        S.dma('sp', N1[:, 0:64], first[fb * 128:(fb + 1) * 128, :], reads=[self.b_in], writes=[bb])
                        S.dma('act', N1[:, 64:128], second[fb * 128:(fb + 1) * 128, :], reads=[self.b_in], writes=[bb])
                        S.op('pe', lambda e: e.transpose(pT[:, 0:128], N1[:], self.ident[:]), reads=[bb, self.b_const], writes=[b_pT])
                        S.op('dve', lambda e, dstT=dstT, fb=fb: e.tensor_copy(out=dstT[:, fb * 8:(fb + 1) * 8, :],
                                                                             in_=pT[:, 0:128].rearrange("p (g i) -> p g i", i=16)),
                             reads=[b_pT], writes=[bb])
                lre, rd, th, tr, mag, sn, cs, tmp = T

                def dv(fn):
                    S.op('dve', fn, reads=[bb, b_k], writes=[bb])

                def ac(fn):
                    S.op('act', fn, reads=[bb, b_k], writes=[bb])
                ac(lambda e: e.activation(out=DT[:], in_=DT[:], func=AF.Exp))
                dv(lambda e: e.tensor_scalar(out=lre[:], in0=LR[:], scalar1=-1e-4, scalar2=None, op0=ALU.min))
                dv(lambda e: e.tensor_tensor(out=rd[:], in0=lre[:], in1=DT[:], op=ALU.mult))
                dv(lambda e: e.tensor_tensor(out=th[:], in0=LI[:], in1=DT[:], op=ALU.mult))
                dv(lambda e: e.tensor_scalar(out=tr[:], in0=th[:], scalar1=1.0 / (2.0 * math.pi), scalar2=None, op0=ALU.mult))
                dv(lambda e: e.tensor_copy(out=TI[:], in_=tr[:]))
                dv(lambda e: e.tensor_tensor(out=fT[d][:], in0=tr[:], in1=TI[:], op=ALU.subtract))
                ac(lambda e: e.activation(out=sn[:], in_=fT[d][:], func=AF.Sin, scale=TWO_PI))
                dv(lambda e: e.tensor_scalar(out=TI[:], in0=tr[:], scalar1=0.25, scalar2=None, op0=ALU.add))
                dv(lambda e: e.tensor_tensor(out=tmp[:], in0=tr[:], in1=TI[:], op=ALU.subtract))
                ac(lambda e: e.activation(out=cs[:], in_=tmp[:], func=AF.Sin, scale=TWO_PI, bias=hpi[:, 0:1]))
                ac(lambda e: e.activation(out=rhoT[d][:], in_=rd[:], func=AF.Exp))
                mag = rhoT[d]
                dv(lambda e: e.tensor_tensor(out=cs[:], in0=cs[:], in1=mag[:], op=ALU.mult))
                dv(lambda e: e.tensor_tensor(out=sn[:], in0=sn[:], in1=mag[:], op=ALU.mult))
                dv(lambda e: e.tensor_scalar(out=cs[:], in0=cs[:], scalar1=-1.0, scalar2=None, op0=ALU.add))
                dv(lambda e: e.tensor_tensor(out=tmp[:], in0=lre[:], in1=lre[:], op=ALU.mult))
                dv(lambda e: e.tensor_tensor(out=tr[:], in0=LI[:], in1=LI[:], op=ALU.mult))
                dv(lambda e: e.tensor_tensor(out=tmp[:], in0=tmp[:], in1=tr[:], op=ALU.add))
                dv(lambda e: e.reciprocal(out=tmp[:], in_=tmp[:]))
                dv(lambda e: e.tensor_tensor(out=tr[:], in0=cs[:], in1=lre[:], op=ALU.mult))
                dv(lambda e: e.tensor_tensor(out=th[:], in0=sn[:], in1=LI[:], op=ALU.mult))
                dv(lambda e: e.tensor_tensor(out=tr[:], in0=tr[:], in1=th[:], op=ALU.add))
                dv(lambda e: e.tensor_tensor(out=tr[:], in0=tr[:], in1=tmp[:], op=ALU.mult))
                dv(lambda e: e.tensor_tensor(out=th[:], in0=sn[:], in1=lre[:], op=ALU.mult))
                dv(lambda e: e.tensor_tensor(out=rd[:], in0=cs[:], in1=LI[:], op=ALU.mult))
                dv(lambda e: e.tensor_tensor(out=th[:], in0=th[:], in1=rd[:], op=ALU.subtract))
                dv(lambda e: e.tensor_tensor(out=th[:], in0=th[:], in1=tmp[:], op=ALU.mult))
                dv(lambda e: e.tensor_scalar(out=th[:], in0=th[:], scalar1=sgnX[:, 0:1], scalar2=None, op0=ALU.mult))
                crb = tr[:].unsqueeze(2).to_broadcast([128, G, 16])
                cib = th[:].unsqueeze(2).to_broadcast([128, G, 16])
                dv(lambda e: e.tensor_tensor(out=W1[:], in0=BA[:], in1=crb, op=ALU.mult))
                dv(lambda e: e.tensor_tensor(out=W2[:], in0=BB[:], in1=cib, op=ALU.mult))
                dv(lambda e: e.tensor_tensor(out=TB1[d][:], in0=W1[:], in1=W2[:], op=ALU.subtract))
                dv(lambda e: e.tensor_tensor(out=W1[:], in0=BB[:], in1=crb, op=ALU.mult))
                dv(lambda e: e.tensor_tensor(out=W2[:], in0=BA[:], in1=cib, op=ALU.mult))
                dv(lambda e: e.tensor_tensor(out=W1[:], in0=W1[:], in1=W2[:], op=ALU.add))
                dv(lambda e: e.tensor_scalar(out=TB2[d][:], in0=W1[:], scalar1=sgnX[:, 0:1], scalar2=None, op0=ALU.mult))
                dv(lambda e: e.tensor_scalar(out=TC1[d][:], in0=CAr[:], scalar1=sgnX[:, 0:1], scalar2=None, op0=ALU.mult))
                dv(lambda e: e.tensor_scalar(out=TC2[d][:], in0=CBr[:], scalar1=-1.0, scalar2=None, op0=ALU.mult))
                S.op('dve', lambda e: e.tensor_copy(out=tmp[:, 0:1], in_=tmp[:, 0:1]), reads=[bb], writes=[b_tab])
                S.flush()
        if KSTOP == 'sprep':
            return
        D1m = self.sb(st, 'D1m', (128, 8, 128), BF16)
        D2m = self.sb(st, 'D2m', (128, 8, 128), BF16)
        CAp = self.sb(st, 'CAp', (128, 8, 128), BF16)
        CA2p = self.sb(st, 'CA2p', (128, 8, 128), BF16)
        Dt = self.sb(st, 'Dt', (128, 2, 128), BF16)
        b_D, b_Dt = Buf('Dm'), Buf('Dt')
        cosT = self.sb(st, 'cosT', (128, NM), F32)
        sinT = self.sb(st, 'sinT', (128, NM), F32)
        b_rot = Buf('rot')
        SEGW = 1089
        ki = self.sb(st, 'ki', (128, SEGW), I32)
        rr = self.sb(st, 'rr', (128, SEGW), F32)
        b_ki, b_rr = Buf('ki'), Buf('rr')
        uT = [self.sb(st, 'uTfb', (128, EXT), BF16) for _ in range(nb)]
        Yacc = [self.sb(st, 'Yacc', (128, EXT), F32) for _ in range(nb)]
        b_uT = [Buf('uT%d' % b) for b in range(nb)]
        b_Y = [Buf('Yacc%d' % b) for b in range(nb)]
        ygl = self.sb(st, 'ygl', (128, EXT), BF16)
        b_ygl = Buf('ygl')
        taL = [self.sb(st, 'ta', (128, 512), F32) for _ in range(nb)]
        tbL = [self.sb(st, 'tb', (128, 512), F32) for _ in range(nb)]
        WpL = [self.sb(st, 'Wp', (128, 512), F32) for _ in range(nb)]
        WrL = [[self.sb(st, 'Wr', (128, 512), F32) for _ in range(2)] for _ in range(nb)]
        cWL = [self.sb(st, 'cW', (128, 512), BF16) for _ in range(nb)]
        sWL = [self.sb(st, 'sW', (128, 512), BF16) for _ in range(nb)]
        b_taL, b_tbL, b_WpL, b_cWL, b_sWL = [[Buf('%s%d' % (n, q)) for q in range(nb)] for n in 'ta tb Wp cW sW'.split()]
        b_WrL = [[Buf('Wr%d_0' % q), Buf('Wr%d_1' % q)] for q in range(nb)]
        pX1L = [self.ps(st, 'pX1') for _ in range(nb)]
        pX2L = [self.ps(st, 'pX2') for _ in range(nb)]
        pYL = [self.ps(st, 'pY') for _ in range(nb)]
        b_pX1L = [Buf('pX1_%d' % q) for q in range(nb)]
        b_pX2L = [Buf('pX2_%d' % q) for q in range(nb)]
        b_pYL = [Buf('pY_%d' % q) for q in range(nb)]
        pD = self.ps(st, 'pD', (128, 512), BF16)
        b_pD = Buf('pD')
        blocks = [(0, CTX)] + [(CTX + q * 512, 512) for q in range(SEQ // 512)]
        order = {0: blocks, 1: [blocks[0]] + blocks[:0:-1]}
        wcnt = [0]

        def rot_tables(d, g):
            fcol = fT[d][:, g:g + 1]
            for s0 in range(0, NM, SEGW):
                n = min(SEGW, NM - s0)
                S.op('dve', lambda e, s0=s0, n=n: e.tensor_scalar(out=ki[:, 0:n], in0=iot[:, s0:s0 + n], scalar1=fcol, scalar2=None, op0=ALU.mult),
                     reads=[b_k, b_tab], writes=[b_ki])
                S.op('dve', lambda e, s0=s0, n=n: e.scalar_tensor_tensor(out=rr[:, 0:n], in0=iot[:, s0:s0 + n], scalar=fcol, in1=ki[:, 0:n],
                                                                         op0=ALU.mult, op1=ALU.subtract), reads=[b_k, b_tab, b_ki], writes=[b_rr])
                S.op('act', lambda e, s0=s0, n=n: e.activation(out=sinT[:, s0:s0 + n], in_=rr[:, 0:n], func=AF.Sin, scale=TWO_PI),
                     reads=[b_rr], writes=[b_rot])
                S.op('dve', lambda e, s0=s0, n=n: e.tensor_scalar(out=ki[:, 0:n], in0=iot[:, s0:s0 + n], scalar1=fcol, scalar2=0.25, op0=ALU.mult, op1=ALU.add),
                     reads=[b_k, b_tab], writes=[b_ki])
                S.op('dve', lambda e, s0=s0, n=n: e.scalar_tensor_tensor(out=rr[:, 0:n], in0=iot[:, s0:s0 + n], scalar=fcol, in1=ki[:, 0:n],
                                                                         op0=ALU.mult, op1=ALU.subtract), reads=[b_k, b_tab, b_ki], writes=[b_rr])
                S.op('act', lambda e, s0=s0, n=n: e.activation(out=cosT[:, s0:s0 + n], in_=rr[:, 0:n], func=AF.Sin, scale=TWO_PI, bias=hpi[:, 0:1]),
                     reads=[b_rr, b_k], writes=[b_rot])

        def scan_dir(d, g, g8, b):
            rho = rhoT[d][:, g:g + 1]
            first = True
            ta, tb, Wp, cW, sW, Wr = taL[b], tbL[b], WpL[b], cWL[b], sWL[b], WrL[b]
            b_ta, b_tb, b_Wp, b_cW, b_sW, b_Wr = b_taL[b], b_tbL[b], b_WpL[b], b_cWL[b], b_sWL[b], b_WrL[b]
            pX1, pX2, pY = pX1L[b], pX2L[b], pYL[b]
            b_pX1, b_pX2, b_pY = b_pX1L[b], b_pX2L[b], b_pYL[b]
            wloc = 0
            prev_n = [0]
            for (t0, n) in order[d]:
                if d == 0:
                    m_lo = t0 + 1
                else:
                    base = CTX if t0 < CTX else (EXT + CTX)
                    m_lo = base - (t0 + n - 1)
                wi = wloc % 2
                wloc += 1
                Wr_, bWr_ = Wr[wi], b_Wr[wi]
                Wprev, bWprev = Wr[1 - wi], b_Wr[1 - wi]
                S.op('pe', lambda e, t0=t0, n=n: e.matmul(pX1[:, 0:n], lhsT=D1m[:, g8, :], rhs=uT[b][:, t0:t0 + n], start=True, stop=True),
                     reads=[b_D, b_uT[b]], writes=[b_pX1])
                S.op('pe', lambda e, t0=t0, n=n: e.matmul(pX2[:, 0:n], lhsT=D2m[:, g8, :], rhs=uT[b][:, t0:t0 + n], start=True, stop=True),
                     reads=[b_D, b_uT[b]], writes=[b_pX2])
                if d == 0:
                    c_in = cosT[:, m_lo:m_lo + n]
                    s_in = sinT[:, m_lo:m_lo + n]
                    o_a, o_b = ta[:, 0:n], tb[:, 0:n]
                    w_in_ = Wr_[:, 0:n]
                    c_in2, s_in2 = c_in, s_in
                else:
                    c_in = revap(cosT, m_lo + n - 1, n, NM)
                    s_in = revap(sinT, m_lo + n - 1, n, NM)
                    o_a, o_b = revap(ta, n - 1, n, 512), revap(tb, n - 1, n, 512)
                    w_in_ = revap(Wr_, n - 1, n, 512)
                    c_in2, s_in2 = c_in, s_in
                S.op('dve', lambda e, n=n, c_in=c_in, o_a=o_a: e.tensor_tensor(out=o_a, in0=pX1[:, 0:n], in1=c_in, op=ALU.mult),
                     reads=[b_pX1, b_rot], writes=[b_ta])
                S.op('dve', lambda e, n=n, s_in=s_in, o_b=o_b: e.tensor_tensor(out=o_b, in0=pX2[:, 0:n], in1=s_in, op=ALU.mult),
                     reads=[b_pX2, b_rot], writes=[b_tb])
                S.op('pool', lambda e, n=n: e.tensor_tensor(out=Wp[:, 0:n], in0=ta[:, 0:n], in1=tb[:, 0:n], op=ALU.add),
                     reads=[b_ta, b_tb], writes=[b_Wp])
                if first:
                    S.op('dve', lambda e, n=n, Wr_=Wr_: e.tensor_tensor_scan(out=Wr_[:, 0:n], data0=rho.to_broadcast([128, n]), data1=Wp[:, 0:n],
                                                                             initial=0.0, op0=ALU.mult, op1=ALU.add),
                         reads=[b_Wp, b_tab], writes=[bWr_])
                else:
                    pn = prev_n[0]
                    S.op('dve', lambda e, n=n, Wr_=Wr_, Wprev=Wprev, pn=pn: e.tensor_tensor_scan(
                        out=Wr_[:, 0:n], data0=rho.to_broadcast([128, n]), data1=Wp[:, 0:n], initial=Wprev[:, pn - 1:pn], op0=ALU.mult, op1=ALU.add),
                         reads=[b_Wp, b_tab, bWprev], writes=[bWr_])
                first = False
                prev_n[0] = n
                S.op('dve', lambda e, n=n, w_in_=w_in_, c_in2=c_in2: e.tensor_tensor(out=cW[:, 0:n], in0=w_in_, in1=c_in2, op=ALU.mult),
                     reads=[bWr_, b_rot], writes=[b_cW])
                S.op('pool', lambda e, n=n, w_in_=w_in_, s_in2=s_in2: e.tensor_tensor(out=sW[:, 0:n], in0=w_in_, in1=s_in2, op=ALU.mult),
                     reads=[bWr_, b_rot], writes=[b_sW])
                S.op('pe', lambda e, n=n: e.matmul(pY[:, 0:n], lhsT=CAp[:, g8, :], rhs=cW[:, 0:n], start=True, stop=False),
                     reads=[b_D, b_cW], writes=[b_pY])
                S.op('pe', lambda e, n=n: e.matmul(pY[:, 0:n], lhsT=CA2p[:, g8, :], rhs=sW[:, 0:n], start=False, stop=True),
                     reads=[b_D, b_sW], writes=[b_pY])
                S.op('dve', lambda e, t0=t0, n=n: e.tensor_tensor(out=Yacc[b][:, t0:t0 + n], in0=pY[:, 0:n], in1=Yacc[b][:, t0:t0 + n], op=ALU.add),
                     reads=[b_pY, b_Y[b]], writes=[b_Y[b]])
                yield

        for fb in range(16 if KSTOP != 'ssm1' else 1):
            for b in range(nb):
                S.dma('sp', uT[b][:], uscr[b][:, fb, :], reads=[b_uscr[b]], writes=[b_uT[b]])
                S.op('act', lambda e, b=b, fb=fb: e.activation(out=Yacc[b][:], in_=uT[b][:], func=AF.Copy, scale=dcol[:, fb:fb + 1]),
                     reads=[b_uT[b], b_k], writes=[b_Y[b]])
            for d in range(2):
                for q, TBx in enumerate((TB1[d], TB2[d])):
                    S.op('pe', lambda e, q=q, TBx=TBx, fb=fb: e.transpose(pD[:, q * 128:(q + 1) * 128],
                                                                          TBx[:, fb * 8:(fb + 1) * 8, :].rearrange("p g j -> p (g j)"), self.identb[:]),
                         reads=[b_tab, self.b_const], writes=[b_pD])
                S.op('act', lambda e: e.activation(out=Dt[:], in_=pD[:, 0:256].rearrange("p (q c) -> p q c", q=2), func=AF.Copy),
                     reads=[b_pD], writes=[b_Dt])
                for q, Dm in enumerate((D1m, D2m)):
                    S.op('pool', lambda e, q=q, Dm=Dm: e.tensor_tensor(out=Dm[:], in0=Dt[:, q:q + 1, :].to_broadcast([128, 8, 128]),
                                                                      in1=rmask[:].unsqueeze(2).to_broadcast([128, 8, 128]), op=ALU.mult),
                         reads=[b_Dt, b_k], writes=[b_D])
                for Cp, TCx in ((CAp, TC1[d]), (CA2p, TC2[d])):
                    S.op('pool', lambda e, Cp=Cp: e.memset(Cp[:], 0.0), writes=[b_D])
                    S.op('pool', lambda e, Cp=Cp, TCx=TCx, fb=fb: e.tensor_copy(out=bass.AP(Cp, 0, [[8 * 128, 128], [144, 8], [1, 16]]),
                                                                             in_=TCx[:, fb * 8:(fb + 1) * 8, :]),
                         reads=[b_tab], writes=[b_D])
                for g8 in range(8):
                    g = fb * 8 + g8
                    rot_tables(d, g)
                    gens = [scan_dir(d, g, g8, b) for b in range(nb)]
                    for _ in range(len(blocks)):
                        for gen in gens:
                            next(gen)
            for b in range(nb):
                S.op('act', lambda e, b=b: e.activation(out=ygl[:], in_=Yacc[b][:], func=AF.Gelu), reads=[b_Y[b]], writes=[b_ygl])
                S.dma('act', yscr[b][:, fb, :], ygl[:], reads=[b_ygl], writes=[b_yscr[b]])
        S.flush()

    def build(self):
        S = self.S
        with contextlib.ExitStack() as st:
            self.setup_consts(st)
            S.flush()
            nl = len(self.layers)
            for li, i in enumerate(self.layers):
                last = (li == nl - 1)
                kind = i % 3
                if kind == 0:
                    self.layer_mla(li, i, last)
                elif kind == 1:
                    self.layer_s5(li, i, last)
                else:
                    self.layer_conv(li, i, last)
            S.finish([self.b_y])
            S.flush()
        S.close()
        return self.nc


_PROG_CACHE = {}


def _get_prog(nb, layers):
    key = (nb, tuple(layers))
    if key not in _PROG_CACHE:
        p = Prog(nb, list(layers))
        p.build()
        _PROG_CACHE[key] = p
    return _PROG_CACHE[key]


def kernel(**inputs):
    nb = 16 // N_CORES
    prog = _get_prog(nb, range(DEPTH))
    rope = rope_table()
    shared = {name: np.ascontiguousarray(np.asarray(inputs[name], dtype=np.float32)) for name, _ in WEIGHT_SPECS}
    shared['c_ctx'] = np.ascontiguousarray(np.asarray(inputs['c_ctx'], dtype=np.float32)[None, :])
    shared['rope'] = rope
    x = np.asarray(inputs['x'], dtype=np.float32)
    c = np.asarray(inputs['c'], dtype=np.float32)
    ctx = np.asarray(inputs['ctx'], dtype=np.float32)
    in_maps = []
    for i in range(N_CORES):
        d = dict(shared)
        d['x'] = np.ascontiguousarray(x[i * nb:(i + 1) * nb])
        d['c'] = np.ascontiguousarray(c[i * nb:(i + 1) * nb])
        d['ctx'] = np.ascontiguousarray(ctx[i * nb:(i + 1) * nb])
        in_maps.append(d)
    res = run_bass_kernel_spmd(prog.nc, in_maps, core_ids=list(range(N_CORES)))
    return np.concatenate([np.asarray(r['y'], dtype=np.float32) for r in res.results], axis=0)
```
